# Optimizing a Trainium2 kernel written in Bass

```python
import math
import jax, jax.numpy as jnp
from jax import lax
import numpy as np

D_MODEL = 2048
BATCH = 4
SEQ = 2048
DEPTH = 4

HEAD_DIM = 64
A_HEADS = 16
A_KV_HEADS = 2
A_REP = A_HEADS // A_KV_HEADS
A_WINDOW = 128
B_HEADS = 16
B_PATTERNS = ((128, 1), (512, 4), (2048, 16))
BLK = 128
NUM_BUCKETS = 32
REL_MAX_DISTANCE = 2048
N_ATTN_HEADS = A_HEADS + B_HEADS
A_WIDTH = A_HEADS * HEAD_DIM
A_KV_WIDTH = A_KV_HEADS * HEAD_DIM
B_WIDTH = B_HEADS * HEAD_DIM
EVEN_MIX = A_WIDTH + B_WIDTH
EVEN_IN = 2 * A_WIDTH + 2 * A_KV_WIDTH + 4 * B_WIDTH
C_WIDTH = 2 * D_MODEL
C_GROUPS = 16
C_GROUP_DIM = C_WIDTH // C_GROUPS
C_CHUNK = 128
C_IN = 3 * C_WIDTH
N_EVEN = (DEPTH + 1) // 2
N_ODD = DEPTH // 2
EPS = 1e-6
NEG = -1e30
SCALE = HEAD_DIM ** -0.5

kernel_name = 'hybrid_swa_dilated_sgu_trunk'


def _rms(x, g):
    xf = x.astype(jnp.float32)
    y = xf * lax.rsqrt(jnp.mean(xf * xf, axis=-1, keepdims=True) + EPS)
    return y * g.astype(jnp.float32)


def _t5_bucket(dist):
    n = np.maximum(dist, 0)
    max_exact = NUM_BUCKETS // 2
    large = max_exact + (np.log(np.maximum(n, 1) / max_exact)
                         / np.log(REL_MAX_DISTANCE / max_exact)
                         * (NUM_BUCKETS - max_exact)).astype(np.int32)
    large = np.minimum(large, NUM_BUCKETS - 1)
    return np.where(n < max_exact, n, large).astype(np.int32)


def _block_dist():
    a = np.arange(BLK)[:, None]
    b = np.arange(2 * BLK)[None, :]
    return BLK + a - b


def _rel_bias_block(table, dil):
    bucket = _t5_bucket(_block_dist() * dil)
    return table.astype(jnp.float32)[bucket].transpose(2, 0, 1)


def _band_mask(nb, max_dist):
    dist = _block_dist()
    key_idx = np.arange(nb)[:, None, None] * BLK - BLK + np.arange(2 * BLK)[None, None, :]
    return (dist >= 0)[None] & (dist <= max_dist)[None] & (key_idx >= 0)


def _blocks(x, nb):
    L = x.shape[-2]
    x = jnp.pad(x, [(0, 0)] * (x.ndim - 2) + [(0, nb * BLK - L), (0, 0)])
    return x.reshape(x.shape[:-2] + (nb, BLK, x.shape[-1]))


def _with_prev(xb):
    prev = jnp.pad(xb, [(0, 0)] * (xb.ndim - 3) + [(1, 0), (0, 0), (0, 0)])[..., :-1, :, :]
    return jnp.concatenate([prev, xb], axis=-2)


def _band_parts(q, k, v, bias, max_dist):
    L = q.shape[-2]
    nb = -(-L // BLK)
    qb = _blocks(q, nb)
    kb = _with_prev(_blocks(k, nb))
    vb = _with_prev(_blocks(v, nb))
    s = jnp.einsum('bgrnqd,bgnkd->bgrnqk', qb, kb) + bias[:, :, None]
    s = jnp.where(_band_mask(nb, max_dist), s, NEG)
    m = jnp.max(s, axis=-1)
    p = jnp.exp(s - m[..., None])
    l = jnp.sum(p, axis=-1)
    u = jnp.einsum('bgrnqk,bgnkd->bgrnqd', p, vb)
    m = m.reshape(m.shape[:-2] + (nb * BLK,))[..., :L]
    l = l.reshape(l.shape[:-2] + (nb * BLK,))[..., :L]
    u = u.reshape(u.shape[:-3] + (nb * BLK, HEAD_DIM))[..., :L, :]
    return m, l, u


def _mixer_a(q, k, v, gq, gk, sinks, table):
    Bn, S, _ = q.shape
    q = _rms(q.reshape(Bn, S, A_HEADS, HEAD_DIM), gq) * SCALE
    k = _rms(k.reshape(Bn, S, A_KV_HEADS, HEAD_DIM), gk)
    v = v.reshape(Bn, S, A_KV_HEADS, HEAD_DIM).astype(jnp.float32)
    q = q.reshape(Bn, S, A_KV_HEADS, A_REP, HEAD_DIM).transpose(0, 2, 3, 1, 4)
    k = k.transpose(0, 2, 1, 3)
    v = v.transpose(0, 2, 1, 3)
    bias = _rel_bias_block(table, 1).reshape(A_KV_HEADS, A_REP, BLK, 2 * BLK)
    m, l, u = _band_parts(q, k, v, bias, A_WINDOW - 1)
    snk = sinks.astype(jnp.float32).reshape(1, A_KV_HEADS, A_REP, 1)
    mx = jnp.maximum(m, snk)
    w = jnp.exp(m - mx)
    o = u * (w / (l * w + jnp.exp(snk - mx)))[..., None]
    return o.transpose(0, 3, 1, 2, 4).reshape(Bn, S, A_WIDTH)


def _mixer_b(q, k, v, gq, gk, table):
    Bn, S, _ = q.shape
    q = (_rms(q.reshape(Bn, S, B_HEADS, HEAD_DIM), gq) * SCALE).transpose(0, 2, 1, 3)
    k = _rms(k.reshape(Bn, S, B_HEADS, HEAD_DIM), gk).transpose(0, 2, 1, 3)
    v = v.reshape(Bn, S, B_HEADS, HEAD_DIM).astype(jnp.float32).transpose(0, 2, 1, 3)
    ms, ls, us = [], [], []
    for window, dil in B_PATTERNS:
        L = S // dil
        def strided(t):
            return t.reshape(Bn, B_HEADS, L, dil, HEAD_DIM).transpose(0, 1, 3, 2, 4).reshape(Bn, B_HEADS * dil, L, HEAD_DIM)
        bias = jnp.repeat(_rel_bias_block(table, dil), dil, axis=0)[:, None]
        m, l, u = _band_parts(strided(q)[:, :, None], strided(k), strided(v), bias, window // dil)
        ms.append(m.reshape(Bn, B_HEADS, dil, L).transpose(0, 1, 3, 2).reshape(Bn, B_HEADS, S))
        ls.append(l.reshape(Bn, B_HEADS, dil, L).transpose(0, 1, 3, 2).reshape(Bn, B_HEADS, S))
        us.append(u.reshape(Bn, B_HEADS, dil, L, HEAD_DIM).transpose(0, 1, 3, 2, 4).reshape(Bn, B_HEADS, S, HEAD_DIM))
    m_all = jnp.stack(ms)
    mx = jnp.max(m_all, axis=0)
    w = jnp.exp(m_all - mx)
    num = jnp.sum(w[..., None] * jnp.stack(us), axis=0)
    den = jnp.sum(w * jnp.stack(ls), axis=0)
    o = num / den[..., None]
    return o.transpose(0, 2, 1, 3).reshape(Bn, S, B_WIDTH)


def _even_layer(x, ln_g, w_in, qk_g, sinks, w_out, rel_bias):
    h = _rms(x, ln_g).astype(x.dtype)
    z = h @ w_in
    cuts = np.cumsum([A_WIDTH, A_KV_WIDTH, A_KV_WIDTH, A_WIDTH, B_WIDTH, B_WIDTH, B_WIDTH]).tolist()
    qa, ka, va, ga, qb, kb, vb, gb = jnp.split(z, cuts, axis=-1)
    ya = _mixer_a(qa, ka, va, qk_g[0], qk_g[1], sinks, rel_bias[:, :A_HEADS])
    yb = _mixer_b(qb, kb, vb, qk_g[2], qk_g[3], rel_bias[:, A_HEADS:])
    y = jnp.concatenate([ya * jax.nn.silu(ga.astype(jnp.float32)),
                         yb * jax.nn.silu(gb.astype(jnp.float32))], axis=-1)
    return x + y.astype(x.dtype) @ w_out


def _odd_layer(x, ln_g, w_in, v_g, w_s, b_s, w_out):
    Bn, S, _ = x.shape
    h = _rms(x, ln_g).astype(x.dtype)
    z = h @ w_in
    uv = jax.nn.gelu(z[..., :2 * C_WIDTH].astype(jnp.float32), approximate=False)
    gate = jax.nn.silu(z[..., 2 * C_WIDTH:].astype(jnp.float32))
    u, v = uv[..., :C_WIDTH], uv[..., C_WIDTH:]
    v = _rms(v, v_g).reshape(Bn, S // C_CHUNK, C_CHUNK, C_GROUPS, C_GROUP_DIM)
    ws = w_s.astype(jnp.float32) * np.tril(np.ones((C_CHUNK, C_CHUNK), np.float32))
    s = jnp.einsum('gts,bnsgc->bntgc', ws, v) + b_s.astype(jnp.float32).T[:, :, None]
    y = u * s.reshape(Bn, S, C_WIDTH) * gate
    return x + y.astype(x.dtype) @ w_out


def setup_inputs(seed: int = 0) -> dict:
    key = jax.random.key(seed)
    ks = jax.random.split(key, 14)
    f32 = jnp.float32
    nrm = lambda k, shape, sc: jax.random.normal(k, shape, f32) * sc
    return {
        'x': nrm(ks[0], (BATCH, SEQ, D_MODEL), 1.0),
        'ev_ln_g': 1.0 + nrm(ks[1], (N_EVEN, D_MODEL), 0.01),
        'ev_w_in': nrm(ks[2], (N_EVEN, D_MODEL, EVEN_IN), D_MODEL ** -0.5),
        'ev_qk_g': 1.0 + nrm(ks[3], (N_EVEN, 4, HEAD_DIM), 0.01),
        'ev_sinks': nrm(ks[4], (N_EVEN, A_HEADS), 1.0),
        'ev_w_out': nrm(ks[5], (N_EVEN, EVEN_MIX, D_MODEL), EVEN_MIX ** -0.5),
        'od_ln_g': 1.0 + nrm(ks[6], (N_ODD, D_MODEL), 0.01),
        'od_w_in': nrm(ks[7], (N_ODD, D_MODEL, C_IN), D_MODEL ** -0.5),
        'od_v_g': 1.0 + nrm(ks[8], (N_ODD, C_WIDTH), 0.01),
        'od_w_s': nrm(ks[9], (N_ODD, C_GROUPS, C_CHUNK, C_CHUNK), C_CHUNK ** -0.5),
        'od_b_s': 1.0 + nrm(ks[10], (N_ODD, C_GROUPS, C_CHUNK), 0.01),
        'od_w_out': nrm(ks[11], (N_ODD, C_WIDTH, D_MODEL), C_WIDTH ** -0.5),
        'rel_bias': nrm(ks[12], (NUM_BUCKETS, N_ATTN_HEADS), 0.2),
    }


def reference(x, ev_ln_g, ev_w_in, ev_qk_g, ev_sinks, ev_w_out, od_ln_g, od_w_in,
              od_v_g, od_w_s, od_b_s, od_w_out, rel_bias):
    for i in range(DEPTH):
        j = i // 2
        if i % 2 == 0:
            x = _even_layer(x, ev_ln_g[j], ev_w_in[j], ev_qk_g[j], ev_sinks[j], ev_w_out[j], rel_bias)
        else:
            x = _odd_layer(x, od_ln_g[j], od_w_in[j], od_v_g[j], od_w_s[j], od_b_s[j], od_w_out[j])
    return x
```

```python
import math
import numpy as np
from contextlib import ExitStack
import concourse.bass as bass
import concourse.mybir as mybir
from concourse.bass_utils import run_bass_kernel_spmd

F32 = mybir.dt.float32
BF16 = mybir.dt.bfloat16
ALU = mybir.AluOpType
AF = mybir.ActivationFunctionType
AX = mybir.AxisListType

RAW, WAW, WAR = 1, 2, 4
EPS = 1e-6
NEG = -30000.0


class _Op:
    __slots__ = ("i", "q", "fn", "chan", "ndma", "inc", "deps", "needed", "sem", "val")


class Sched:
    LIMIT = 30000
    QUEUES = ("pe", "act", "dve", "pool", "sp")

    def __init__(self):
        self.ops = []
        self.lastw = {}
        self.rd_q = {}
        self.rd_dma = {}

    def add(self, q, fn, reads=(), writes=(), chan=None, ndma=1, inc=16):
        i = len(self.ops)
        op = _Op()
        op.i, op.q, op.fn, op.chan, op.ndma, op.inc = i, q, fn, chan, ndma, inc
        op.needed = chan is not None
        op.sem = None
        op.val = 0
        deps = {}
        for k in reads:
            w = self.lastw.get(k)
            if w is not None:
                deps[w] = deps.get(w, 0) | RAW
        for k in writes:
            w = self.lastw.get(k)
            if w is not None:
                deps[w] = deps.get(w, 0) | WAW
            for r in self.rd_q.get(k, {}).values():
                deps[r] = deps.get(r, 0) | WAR
            for r in self.rd_dma.get(k, ()):
                deps[r] = deps.get(r, 0) | WAR
        for k in reads:
            if chan is not None:
                self.rd_dma.setdefault(k, []).append(i)
            else:
                self.rd_q.setdefault(k, {})[q] = i
        for k in writes:
            self.lastw[k] = i
            self.rd_q[k] = {}
            self.rd_dma[k] = []
        op.deps = []
        for d, kind in deps.items():
            dop = self.ops[d]
            if dop.chan is None and chan is None and dop.q == q:
                if q == "pe":
                    continue
                if not (kind & RAW):
                    continue
            op.deps.append(d)
            dop.needed = True
        self.ops.append(op)
        return i

    def emit(self, nc, stack):
        state = {}
        sems = {}
        for op in self.ops:
            if not op.needed:
                continue
            key = ("c", op.chan) if op.chan is not None else ("q", op.q)
            inc = op.inc * op.ndma if op.chan is not None else 1
            idx, val = state.get(key, (0, 0))
            if val + inc > self.LIMIT:
                idx, val = idx + 1, 0
            val += inc
            state[key] = (idx, val)
            op.sem = (key, idx)
            op.val = val
            if op.sem not in sems:
                sems[op.sem] = stack.enter_context(nc.semaphore("s%d" % len(sems)))
        self.nsems = len(sems)
        byq = {q: [] for q in self.QUEUES}
        for op in self.ops:
            byq[op.q].append(op)
        ops = self.ops

        def run(q, eng):
            waited = {}
            for op in byq[q]:
                need = {}
                for d in op.deps:
                    dop = ops[d]
                    if waited.get(dop.sem, 0) >= dop.val:
                        continue
                    if need.get(dop.sem, 0) < dop.val:
                        need[dop.sem] = dop.val
                for s, v in need.items():
                    eng.wait_ge(sems[s], v)
                    waited[s] = v
                r = op.fn(eng)
                if op.needed:
                    if op.chan is not None:
                        insts = r if isinstance(r, (list, tuple)) else [r]
                        assert len(insts) == op.ndma, (len(insts), op.ndma)
                        for ins in insts:
                            ins.then_inc(sems[op.sem], op.inc)
                    else:
                        r.then_inc(sems[op.sem], 1)

        block = stack.enter_context(nc.Block())

        @block.tensor
        def _(e):
            run("pe", e)

        @block.scalar
        def _(e):
            run("act", e)

        @block.vector
        def _(e):
            run("dve", e)

        @block.gpsimd
        def _(e):
            run("pool", e)

        @block.sync
        def _(e):
            run("sp", e)


def _t5_bucket(n):
    n = np.maximum(n, 0)
    max_exact = 16
    large = max_exact + (np.log(np.maximum(n, 1) / max_exact) / np.log(2048 / max_exact) * (32 - max_exact)).astype(np.int32)
    large = np.minimum(large, 31)
    return np.where(n < max_exact, n, large).astype(np.int32)


VARIANTS = ((1, 127), (1, 128), (4, 128), (16, 128))


def _consts():
    oh = np.zeros((33, 4, 384), np.float32)
    for v, (dil, maxd) in enumerate(VARIANTS):
        for m in range(384):
            dist = m - 127
            if 0 <= dist <= maxd:
                oh[int(_t5_bucket(np.array(dist * dil))), v, m] = 1.0
            else:
                oh[32, v, m] = 1.0
    cm = np.zeros((128, 4, 128), np.float32)
    cm[:, 0, :] = np.eye(128)
    cm[:, 1, :] = np.eye(128)[::-1]
    cm[:, 2, :] = np.triu(np.ones((128, 128)))
    on = np.zeros((128, 2, 128), np.float32)
    on[:, 0, 0:64] = 1.0
    on[:, 1, 64:128] = 1.0
    return oh, cm, on


PIPE = 3
TOK = 1024
NT = 8
D = 2048
KC = 16


def build_program(layers=(0, 1, 2, 3), stage=99, ncores=8):
    nc = bass.Bass("TRN2", target_bir_lowering=False)
    S = Sched()
    st = ExitStack()

    def din(name, shape, dt=F32):
        return nc.dram_tensor(name, list(shape), dt, kind="ExternalInput").ap()

    def dscr(name, shape, dt=BF16):
        return nc.dram_tensor(name, list(shape), dt, kind="Internal").ap()

    x_in = din("x", [TOK, D])
    ev_ln_g = din("ev_ln_g", [2, D])
    ev_w_in = din("ev_w_in", [2, D, 6400])
    ev_qk_g = din("ev_qk_g", [2, 4, 64])
    ev_sinks = din("ev_sinks", [2, 16])
    ev_w_out = din("ev_w_out", [2, D, D])
    od_ln_g = din("od_ln_g", [2, D])
    has_odd = any(l % 2 == 1 for l in layers)
    od_w_in = din("od_w_in", [2, D, 12288] if has_odd else [2, 128, 128])
    od_v_g = din("od_v_g", [2, 4096])
    od_w_s = din("od_w_s", [2, 16, 128, 128])
    od_b_s = din("od_b_s", [2, 16, 128])
    od_w_out = din("od_w_out", [2, 4096, D] if has_odd else [2, 128, 128])
    rel_bias = din("rel_bias", [32, 32])
    c_oh = din("c_oh", [33, 4, 384])
    c_cm = din("c_cm", [128, 4, 128])
    c_on = din("c_on", [128, 2, 128])
    c_pf = din("c_pf", [128, 1])
    y_out = nc.dram_tensor("y", [TOK, D], F32, kind="ExternalOutput").ap()

    qT_d = dscr("qT_d", [2048, TOK])
    kvK = dscr("kvK", [1024, TOK])
    kvV = dscr("kvV", [1024, TOK])
    kvA = dscr("kvA", [256, TOK])
    gK = dscr("gK", [2048, TOK])
    gV = dscr("gV", [2048, TOK])
    gA = dscr("gA", [512, TOK])
    gT_d = dscr("gT_d", [2048, TOK])
    evd = dscr("evd", [4, 16, 384])
    etd = dscr("etd", [4, 16, 128, 256])
    vsp = dscr("vsp", [TOK, 4096])
    ugd = dscr("ugd", [4096, TOK])

    def sb(name, shape, dt):
        return st.enter_context(nc.sbuf_tensor(name, list(shape), dt))

    with st:
        xres = sb("xres", [128, NT, D], F32)
        big = sb("big", [128, KC, TOK], BF16)
        wsl = [sb("wsl%d" % i, [128, KC, 512], BF16) for i in range(2)]
        AR = [sb("ar%d" % i, [128, 2048], F32) for i in range(7)]
        cmb = sb("cmb", [128, 4, 128], BF16)
        onb = sb("onb", [128, 2, 128], BF16)
        pfl = sb("pfl", [128, 1], F32)
        stat = sb("stat", [128, 256], F32)
        gtab = sb("gtab", [128, 4, 64], F32)
        gqk = sb("gqk", [128, 2, 64], F32)
        esk = sb("esk", [128, 8], F32)
        vgc = sb("vgc", [128, 32], F32)
        PS = [st.enter_context(nc.psum_tensor("ps%d" % i, [128, 512], F32)) for i in range(8)]

        ident = cmb[:, 0, :]
        Jm = cmb[:, 1, :]
        trim = cmb[:, 2, :]

        def av(i, dt, off_bytes, shape):
            n = int(np.prod(shape))
            esz = 4 if dt == F32 else 2
            nbytes = n * esz
            assert off_bytes + nbytes <= 8192 and off_bytes % 4 == 0
            base = AR[i][:, off_bytes // 4:(off_bytes + nbytes) // 4]
            ap = base if dt == F32 else base.bitcast(BF16)
            if len(shape) == 2:
                ap = ap.rearrange("p (a b) -> p a b", a=shape[0])
            elif len(shape) == 3:
                ap = ap.rearrange("p (a b c) -> p a b c", a=shape[0], b=shape[1])
            keys = [("ar", i, s) for s in range(off_bytes // 512, (off_bytes + nbytes + 511) // 512)]
            return ap, keys

        def bc(ap, shape, axis):
            return ap.unsqueeze(axis).broadcast_to(list(shape))

        def psb(i):
            return PS[i][:, :].bitcast(BF16)

        def pk(i):
            return [("ps", i)]

        cnt = {"w": 0, "tb": 0}

        def dma(q, out, in_, reads, writes, chan, nonc=False):
            if nonc:
                S.add(q, lambda e: e.dma_start(out=out, in_=in_, allow_slow_non_contiguous=True), reads=reads, writes=writes, chan=chan)
            else:
                S.add(q, lambda e: e.dma_start(out=out, in_=in_), reads=reads, writes=writes, chan=chan)

        wplan = []
        wst = {"issued": 0, "used": 0}

        def wplan_for(l):
            j = l // 2
            if l % 2 == 0:
                wi, wo = ev_w_in[j], ev_w_out[j]
                sp_ = [(wi, 3328, 512, 0), (wi, 3840, 512, 0), (wi, 1024, 256, 0), (wi, 4352, 512, 0), (wi, 4864, 512, 0)]
                sp_ += [(wi, c, 512, 0) for c in (0, 512, 2304, 2816, 1280, 1792, 5376, 5888)] + [(wo, 512 * s_, 512, 0) for s_ in range(4)]
            else:
                wi, wo = od_w_in[j], od_w_out[j]
                sp_ = [(wi, 4096 + 512 * s_, 512, 0) for s_ in range(8)]
                for i in range(8):
                    sp_ += [(wi, 512 * i, 512, 0), (wi, 8192 + 512 * i, 512, 0)]
                for H in range(2):
                    sp_ += [(wo, 512 * s_, 512, 2048 * H) for s_ in range(4)]
            return sp_

        def load_w(wap, c0, W, rows0=0):
            k = wst["used"]
            assert wplan[k][1:] == (c0, W, rows0), (wplan[k][1:], c0, W, rows0)
            while wst["issued"] < min(k + 2, len(wplan)):
                i = wst["issued"]
                wp, pc0, pW, pr0 = wplan[i]
                src = wp[pr0:pr0 + D, pc0:pc0 + pW].rearrange("(k p) c -> p k c", p=128)
                dma("pool", wsl[i % 2][:, :, 0:pW], src, [], [("w", i % 2)], ("w", i % 2))
                wst["issued"] += 1
            wst["used"] += 1
            return k % 2

        def hkeys(k, tiles):
            return [("big", k, t) for t in tiles]

        def rsqrt_newton(v_ap, r_ap, t_ap, keys_v, keys_r, keys_t):
            S.add("act", lambda e: e.activation(out=r_ap, in_=v_ap, func=AF.Sqrt), reads=keys_v, writes=keys_r)
            S.add("dve", lambda e: e.reciprocal(out=r_ap, in_=r_ap), reads=keys_r, writes=keys_r)
            S.add("dve", lambda e: e.tensor_tensor(out=t_ap, in0=r_ap, in1=r_ap, op=ALU.mult), reads=keys_r, writes=keys_t)
            S.add("dve", lambda e: e.tensor_tensor(out=t_ap, in0=t_ap, in1=v_ap, op=ALU.mult), reads=keys_t + keys_v, writes=keys_t)
            S.add("dve", lambda e: e.tensor_scalar(out=t_ap, in0=t_ap, scalar1=-0.5, scalar2=1.5, op0=ALU.mult, op1=ALU.add), reads=keys_t, writes=keys_t)
            S.add("dve", lambda e: e.tensor_tensor(out=r_ap, in0=r_ap, in1=t_ap, op=ALU.mult), reads=keys_r + keys_t, writes=keys_r)

        dma("pool", cmb[:, :, :], c_cm, [], ["cmb"], "c0")
        dma("pool", onb[:, :, :], c_on, [], ["onb"], "c1")
        dma("sp", pfl[:, :], c_pf, [], ["pfl"], "c2")
        for t in range(NT):
            dma("sp", xres[:, t, :], x_in[t * 128:(t + 1) * 128, :], [], [("x", t)], ("xin", t))

        def phase0(lnrow):
            lng, lng_k = av(0, F32, 0, [2048])
            junk, junk_k = av(1, BF16, 0, [2048])
            hb = [av(1, BF16, 4096, [2048]), av(2, BF16, 0, [2048])]
            ssq, rr, tt = stat[:, 0:8], stat[:, 8:16], stat[:, 16:24]
            dma("sp", lng, bass.AP(lnrow.tensor, lnrow.offset, [[0, 128], [1, D]]), [], lng_k, "lng")
            S.add("dve", lambda e: e.memset(ssq, 0.0), writes=[("st", 0)])
            for t in range(NT):
                S.add("act", lambda e, t=t: e.activation(out=junk, in_=xres[:, t, :], func=AF.Square, accum_out=ssq[:, t:t + 1]),
                      reads=[("x", t)], writes=junk_k + [("st", 0)])
            S.add("dve", lambda e: e.tensor_scalar(out=ssq, in0=ssq, scalar1=1.0 / D, scalar2=EPS, op0=ALU.mult, op1=ALU.add),
                  reads=[("st", 0)], writes=[("st", 0)])
            rsqrt_newton(ssq, rr, tt, [("st", 0)], [("st", 1)], [("st", 2)])
            for t in range(NT):
                h_ap, h_k = hb[t % 2]
                S.add("dve", lambda e, t=t, h_ap=h_ap: e.scalar_tensor_tensor(out=h_ap, in0=xres[:, t, :], scalar=rr[:, t:t + 1], in1=lng,
                                                                              op0=ALU.mult, op1=ALU.mult),
                      reads=[("x", t), ("st", 1)] + lng_k, writes=h_k)
                for half in range(2):
                    bank = 6 + half
                    for i in range(8):
                        k = half * 8 + i
                        S.add("pe", lambda e, bank=bank, i=i, k=k, h_ap=h_ap: e.transpose(psb(bank)[:, i * 128:(i + 1) * 128], h_ap[:, k * 128:(k + 1) * 128], ident),
                              reads=h_k + ["cmb"], writes=pk(bank))
                    S.add("act", lambda e, bank=bank, half=half, t=t: e.activation(
                        out=big[:, half * 8:half * 8 + 8, t * 128:(t + 1) * 128],
                        in_=psb(bank).rearrange("p (a b) -> p a b", a=8), func=AF.Copy),
                        reads=pk(bank), writes=[("big", k, t) for k in range(half * 8, half * 8 + 8)])

        def mm_tok(bank, wb, W, t, wcol0=0):
            for k in range(KC):
                S.add("pe", lambda e, k=k: e.matmul(PS[bank][:, 0:W], lhsT=big[:, k, t * 128:(t + 1) * 128], rhs=wsl[wb][:, k, wcol0:wcol0 + W],
                                                    start=(k == 0), stop=(k == KC - 1)),
                      reads=[("big", k, t), ("w", wb)], writes=pk(bank))

        def mm_feat(bank, wb, wcol0, span):
            for k in range(KC):
                S.add("pe", lambda e, k=k: e.matmul(PS[bank][:, :], lhsT=wsl[wb][:, k, wcol0:wcol0 + 128], rhs=big[:, k, span * 512:(span + 1) * 512],
                                                    start=(k == 0), stop=(k == KC - 1)),
                      reads=hkeys(k, range(span * 4, span * 4 + 4)) + [("w", wb)], writes=pk(bank))

        def out_proj(wap, rows0):
            for sl in range(4):
                wb = load_w(wap, sl * 512, 512, rows0)
                for t in range(NT):
                    bank = cnt["tb"] % 4
                    cnt["tb"] += 1
                    mm_tok(bank, wb, 512, t)
                    S.add("dve", lambda e, bank=bank, t=t, sl=sl: e.tensor_tensor(out=xres[:, t, sl * 512:(sl + 1) * 512], in0=PS[bank][:, :],
                                                                                 in1=xres[:, t, sl * 512:(sl + 1) * 512], op=ALU.add),
                          reads=pk(bank) + [("x", t)], writes=[("x", t)])

        def build_etables():
            text, text_k = av(3, F32, 0, [32])
            oht, oht_k = av(4, F32, 0, [4, 384])
            evs, evs_k = av(3, BF16, 512, [4, 384])
            S.add("dve", lambda e: e.memset(text[32:33, :], NEG), writes=[("txr",)])
            dma("sp", text[0:32, :], rel_bias, [], text_k, "tb0")
            dma("sp", oht[0:33, :, :], c_oh, [], oht_k, "tb1")
            for v in range(4):
                hs = 0 if v == 0 else 16
                S.add("pe", lambda e, v=v, hs=hs: e.matmul(PS[0][0:16, 0:384], lhsT=text[0:33, hs:hs + 16], rhs=oht[0:33, v, :], start=True, stop=True),
                      reads=text_k + oht_k + [("txr",)], writes=pk(0))
                S.add("act", lambda e, v=v: e.activation(out=evs[0:16, v, :], in_=PS[0][0:16, 0:384], func=AF.Exp), reads=pk(0), writes=evs_k)
            dma("sp", evd.rearrange("v h m -> h v m"), evs[0:16, :, :], evs_k, ["evd"], "tb2")
            hk, hk_k = av(5, BF16, 0, [16, 256])
            ets, ets_k = av(4, BF16, 0, [16, 256])
            for v in range(4):
                dma("sp", hk, bass.AP(evd.tensor, v * 16 * 384, [[1, 128], [384, 16], [1, 256]]), ["evd"], hk_k, "tb3")
                for hh in range(8):
                    bank = hh % 2
                    S.add("pe", lambda e, hh=hh, bank=bank: e.matmul(PS[bank][:, :], lhsT=Jm, rhs=hk[:, 2 * hh:2 * hh + 2, :], start=True, stop=True),
                          reads=hk_k + ["cmb"], writes=pk(bank))
                    S.add("act", lambda e, hh=hh, bank=bank: e.activation(out=ets[:, 2 * hh:2 * hh + 2, :], in_=PS[bank][:, :].rearrange("p (a b) -> p a b", a=2), func=AF.Copy),
                          reads=pk(bank), writes=ets_k)
                dma("sp", etd[v].rearrange("h k c -> k h c"), ets, ets_k, [("etd", v)], "tb4")

        def even_layer(j):
            phase0(ev_ln_g[j])
            w_in = ev_w_in[j]
            dma("sp", gtab[:, :, :], bass.AP(ev_qk_g.tensor, j * 256, [[0, 128], [1, 256]]).rearrange("p (a b) -> p a b", a=4), [], ["gtab"], "gt")
            for m in range(2):
                S.add("dve", lambda e, m=m: e.scalar_tensor_tensor(out=gqk[:, m, :], in0=gtab[:, 2 * m, :], scalar=0.125, in1=gtab[:, 2 * m + 1, :],
                                                                   op0=ALU.mult, op1=ALU.mult), reads=["gtab"], writes=[("gqk", m)])
            for hf in range(2):
                dma("sp", esk[hf * 64:(hf + 1) * 64, :], bass.AP(ev_sinks.tensor, j * 16 + hf, [[0, 64], [2, 8]]), [], [("esk", hf)], ("esk", hf), nonc=True)
            S.add("act", lambda e: e.activation(out=esk[:, :], in_=esk[:, :], func=AF.Exp), reads=[("esk", 0), ("esk", 1)], writes=[("esk", 0), ("esk", 1)])

            f_sq, f_sq_k = av(0, F32, 0, [512])
            f_q, f_q_k = av(0, F32, 2048, [512])
            qnb = [av(0, BF16, 4096 + 1024 * i, [512]) for i in range(4)]
            qst, qst_k = av(2, BF16, 0, [4, 1024])
            vst, vst_k = av(3, BF16, 0, [8, 512])
            gst = [av(4, BF16, 2048 * i, [1024]) for i in range(4)]
            vast, vast_k = av(5, BF16, 0, [8, 128])
            it = {"n": 0}

            def qk_tile(bank, t, nh, col0, gm, dst, dst_k, dcol):
                W = nh * 64
                ss, rr, tt = stat[:, 32:32 + nh], stat[:, 48:48 + nh], stat[:, 64:64 + nh]
                src = PS[bank][:, col0:col0 + W]
                S.add("act", lambda e: e.activation(out=f_sq[:, 0:W], in_=src, func=AF.Square), reads=pk(bank), writes=f_sq_k)
                S.add("dve", lambda e: e.reduce_sum(out=ss, in_=f_sq[:, 0:W].rearrange("p (h d) -> p h d", d=64), axis=AX.X), reads=f_sq_k, writes=[("st", 4)])
                S.add("dve", lambda e: e.tensor_scalar(out=ss, in0=ss, scalar1=1.0 / 64, scalar2=EPS, op0=ALU.mult, op1=ALU.add), reads=[("st", 4)], writes=[("st", 4)])
                rsqrt_newton(ss, rr, tt, [("st", 4)], [("st", 5)], [("st", 6)])
                qb_ap, qb_k = qnb[it["n"] % 4]
                it["n"] += 1
                src3 = src.rearrange("p (h d) -> p h d", d=64)
                if gm is not None:
                    S.add("dve", lambda e: e.tensor_tensor(out=f_q[:, 0:W].rearrange("p (h d) -> p h d", d=64), in0=src3, in1=bc(rr, [128, nh, 64], 2), op=ALU.mult),
                          reads=pk(bank) + [("st", 5)], writes=f_q_k)
                    S.add("dve", lambda e: e.tensor_tensor(out=qb_ap[:, 0:W].rearrange("p (h d) -> p h d", d=64), in0=f_q[:, 0:W].rearrange("p (h d) -> p h d", d=64),
                                                            in1=bc(gqk[:, gm, :], [128, nh, 64], 1), op=ALU.mult),
                          reads=f_q_k + [("gqk", gm)], writes=qb_k)
                else:
                    S.add("dve", lambda e: e.tensor_tensor(out=qb_ap[:, 0:W].rearrange("p (h d) -> p h d", d=64), in0=src3, in1=bc(rr, [128, nh, 64], 2), op=ALU.mult),
                          reads=pk(bank) + [("st", 5)], writes=qb_k)
                npair = nh // 2
                tb = 6 + (it["n"] % 2)

                def post():
                    for i in range(npair):
                        S.add("pe", lambda e, i=i: e.transpose(psb(tb)[:, i * 128:(i + 1) * 128], qb_ap[:, i * 128:(i + 1) * 128], ident), reads=qb_k + ["cmb"], writes=pk(tb))
                    S.add("act", lambda e: e.activation(out=dst[:, dcol:dcol + npair, t * 128:(t + 1) * 128],
                                                        in_=psb(tb)[:, 0:npair * 128].rearrange("p (a b) -> p a b", a=npair), func=AF.Copy),
                          reads=pk(tb), writes=dst_k)
                return post

            def nb():
                b = cnt["tb"] % 4
                cnt["tb"] += 1
                return b

            def qk_slabs(specs):
                for (c0, gm, ddst, drow) in specs:
                    wb = load_w(w_in, c0, 512)
                    pend = []
                    for t in range(NT):
                        bank = nb()
                        mm_tok(bank, wb, 512, t)
                        if len(pend) >= 2:
                            pend.pop(0)()
                        pend.append(qk_tile(bank, t, 8, 0, gm, qst, qst_k, 0))
                    for p_ in pend:
                        p_()
                    dma("sp", ddst[drow:drow + 512, :].rearrange("(i p) n -> p i n", p=128), qst, qst_k, [("qk_d", id(ddst), drow)], "qst")

            qk_slabs(((3328, None, kvK, 0), (3840, None, kvK, 512)))
            wb = load_w(w_in, 1024, 256)
            pend = []
            for t in range(NT):
                bank = nb()
                mm_tok(bank, wb, 256, t)
                if len(pend) >= 2:
                    pend.pop(0)()
                pend.append(qk_tile(bank, t, 2, 0, None, qst, qst_k, 0))
                S.add("act", lambda e, bank=bank, t=t: e.activation(out=vast[:, t, :], in_=PS[bank][:, 128:256], func=AF.Copy), reads=pk(bank), writes=vast_k)
            for p_ in pend:
                p_()
            dma("sp", kvA[0:128, :], qst[:, 0, :], qst_k, [("kvl", "ka")], "qst")
            dma("sp", bass.AP(kvA.tensor, 128 * TOK, [[128, 128], [128 * 128, 8], [1, 128]]), vast, vast_k, [("kvl", "va")], "vast")
            for s2 in range(2):
                wb = load_w(w_in, 4352 + 512 * s2, 512)
                for t in range(NT):
                    bank = nb()
                    mm_tok(bank, wb, 512, t)
                    S.add("act", lambda e, bank=bank, t=t: e.activation(out=vst[:, t, :], in_=PS[bank][:, :], func=AF.Copy), reads=pk(bank), writes=vst_k)
                dma("sp", kvV[:, 512 * s2:512 * s2 + 512].rearrange("(t p) c -> p t c", p=128), vst, vst_k, [("kvl", "vb", s2)], "vst")
            rg = [[2 * i, 2 * i + 1] for i in range(ncores // 2)]
            for (src_, dst_, rk, wk, ch) in ((kvK, gK, [("qk_d", id(kvK), 0), ("qk_d", id(kvK), 512)], "gK", "cc0"),
                                            (kvV, gV, [("kvl", "vb", 0), ("kvl", "vb", 1)], "gV", "cc1"),
                                            (kvA, gA, [("kvl", "ka"), ("kvl", "va")], "gA", "cc2")):
                S.add("pool", lambda e, src_=src_, dst_=dst_: e.collective_compute("AllGather", ALU.bypass, replica_groups=rg, ins=[src_], outs=[dst_]),
                      reads=rk, writes=[wk], chan=ch, inc=1)

            qk_slabs(((0, 0, qT_d, 0), (512, 0, qT_d, 512), (2304, 1, qT_d, 1024), (2816, 1, qT_d, 1536)))
            gi = 0
            for (c0, drow) in ((1280, 0), (1792, 512), (5376, 1024), (5888, 1536)):
                wb = load_w(w_in, c0, 512)
                for ch in range(4):
                    g_ap, g_k = gst[gi % 4]
                    gi += 1
                    for span in range(2):
                        bank = nb()
                        mm_feat(bank, wb, ch * 128, span)
                        S.add("act", lambda e, bank=bank, span=span, g_ap=g_ap: e.activation(out=g_ap[:, span * 512:(span + 1) * 512], in_=PS[bank][:, :], func=AF.Silu),
                              reads=pk(bank), writes=g_k)
                    dma("sp", gT_d[drow + ch * 128:drow + (ch + 1) * 128, :], g_ap, g_k, [("gT_d", drow + ch * 128)], ("gst", gi % 4))

            if stage <= 1:
                return
            if stage <= 2:
                return
            vz, vz_k = av(5, BF16, 0, [4, 2, 128])
            S.add("dve", lambda e: e.memset(vz, 0.0), writes=vz_k + [("vz",)])
            NUM = (2, 3)
            DEN = (4, 5)
            SB = ((0, 1), (6, 7))
            LD = {}

            def pair_loads(pi, emit=True):
                    def dma_(*a_, **k_):
                        if emit:
                            dma(*a_, **k_)

                    def sadd(*a_, **k_):
                        if emit:
                            S.add(*a_, **k_)
                    m = pi // 8
                    jj = pi % 8
                    ab = pi % 2
                    arq = 1 + ab
                    qp, qp_k = av(arq, BF16, 0, [1024])
                    kp, kp_k = av(arq, BF16, 2048, [1024])
                    kpp, kpp_k = av(arq, BF16, 4096, [1024])
                    gp, gp_k = av(arq, BF16, 6144, [1024])
                    nv = 1 if m == 0 else 3
                    etp, etp_k = av(3, BF16, 4096 * 0, [3, 2, 256]) if ab == 0 else av(4, BF16, 0, [3, 2, 256])
                    frow = m * 1024 + jj * 128
                    dma_("sp", qp, qT_d[frow:frow + 128, :], [("qk_d", id(qT_d), (frow // 512) * 512)], qp_k, ("qp", ab))
                    if m == 1:
                        dma_("sp", kp, kvK[jj * 128:(jj + 1) * 128, :], [("qk_d", id(kvK), (jj // 4) * 512)], kp_k, ("kp", ab))
                        dma_("sp", kpp, gK[jj * 128:(jj + 1) * 128, :], ["gK"], kpp_k, ("kpp", ab))
                    else:
                        g = jj // 4
                        r0 = 64 * g
                        sadd("sp", lambda e, kp=kp, r0=r0: [e.dma_start(out=kp[0:64, :], in_=kvA[r0:r0 + 64, :]), e.dma_start(out=kp[64:128, :], in_=kvA[r0:r0 + 64, :])],
                              reads=[("kvl", "ka")], writes=kp_k, chan=("kp", ab), ndma=2)
                        sadd("sp", lambda e, kpp=kpp, r0=r0: [e.dma_start(out=kpp[0:64, :], in_=gA[r0:r0 + 64, :]), e.dma_start(out=kpp[64:128, :], in_=gA[r0:r0 + 64, :])],
                              reads=["gA"], writes=kpp_k, chan=("kpp", ab), ndma=2)
                    dma_("sp", gp, gT_d[frow:frow + 128, :], [("gT_d", frow)], gp_k, ("gp", ab))
                    vlist = (0,) if m == 0 else (1, 2, 3)
                    for vi, v in enumerate(vlist):
                        dma_("sp", etp[:, vi, :, :], etd[v, 2 * jj:2 * jj + 2].rearrange("h k c -> k h c"), [("etd", v)], etp_k, ("etp", ab, vi))
                    LD[pi] = (qp, qp_k, kp, kp_k, kpp, kpp_k, gp, gp_k, etp, etp_k)

            PR = {}

            def zero_acc():
                for b_ in NUM + DEN:
                    S.add("act", lambda e, b_=b_: e.memzero(PS[b_][:, :]), writes=pk(b_))

            def make_pair(pi):
                    m = pi // 8
                    jj = pi % 8
                    pair_loads(pi, emit=False)
                    qp, qp_k, kp, kp_k, kpp, kpp_k, gp, gp_k, etp, etp_k = LD[pi]

                    jobs = []
                    if m == 0:
                        pats = ((0, 0),)
                    else:
                        pats = ((1, 0), (2, 1), (3, 2))
                    for (v, vi) in pats:
                        dil = VARIANTS[v][0]
                        if dil == 1:
                            for kb in range(8):
                                nq = 256 if kb < 7 else 128
                                pieces = []
                                for qi in range(nq // 128):
                                    qt = kb + qi
                                    pieces.append((qt // 4, (qt % 4) * 128, 1, 128, qi * 128))
                                jobs.append(dict(par=False, k=(kb * 128, 1, 128), q=(kb * 128, 1, nq), vi=vi, e0=0, pieces=pieces, vtok=(kb * 128, 1)))
                            jobs.append(dict(par=True, k=(896, 1, 128), q=(0, 1, 128), vi=vi, e0=128, pieces=[(0, 0, 1, 128, 0)], vtok=(896, 1)))
                        elif dil == 4:
                            for c in range(4):
                                for n in range(2):
                                    nq = 256 if n == 0 else 128
                                    pieces = [(n, c, 4, 128, 0)]
                                    if n == 0:
                                        pieces.append((1, c, 4, 128, 128))
                                    jobs.append(dict(par=False, k=(512 * n + c, 4, 128), q=(512 * n + c, 4, nq), vi=vi, e0=0, pieces=pieces, vtok=(512 * n + c, 4)))
                                jobs.append(dict(par=True, k=(512 + c, 4, 128), q=(c, 4, 128), vi=vi, e0=128, pieces=[(0, c, 4, 128, 0)], vtok=(512 + c, 4)))
                        else:
                            for c in range(16):
                                pieces = [(0, c, 16, 32, 0), (1, c, 16, 32, 32)]
                                jobs.append(dict(par=False, k=(c, 16, 64), q=(c, 16, 64), vi=vi, e0=0, pieces=pieces, vtok=(c, 16)))
                                jobs.append(dict(par=True, k=(c, 16, 64), q=(c, 16, 64), vi=vi, e0=64, pieces=pieces, vtok=(c, 16)))

                    if stage <= 3:
                        jobs = []

                    def job_bufs(ji):
                        r = ji % 4
                        sbk = ji % 2
                        vbuf, vbuf_k = av(5, BF16, 512 * r, [2, 128])
                        pe_ap, pe_k = av(6, BF16, 1024 * r, [2, 256])
                        pt_ap, pt_k = av(5, BF16, 4096 + 1024 * r, [2, 256])
                        return r, sbk, vbuf, vbuf_k, pe_ap, pe_k, pt_ap, pt_k

                    def front(ji, jb):
                        r, sbk, vbuf, vbuf_k, pe_ap, pe_k, pt_ap, pt_k = job_bufs(ji)
                        k0, kst, nk = jb["k"]
                        q0, qst_, nq = jb["q"]
                        ksrc, ksrc_k = (kpp, kpp_k) if jb["par"] else (kp, kp_k)
                        vt0, vts = jb["vtok"]
                        vv4 = av(5, BF16, 512 * r, [4, 64])[0]
                        if m == 1:
                            vsrc_t = gV if jb["par"] else kvV
                            vin1 = bass.AP(vsrc_t.tensor, vt0 * TOK + jj * 128, [[vts * TOK, nk], [64, 2], [1, 64]])
                            vrd = ["gV"] if jb["par"] else [("kvl", "vb", jj // 4)]
                            vout1 = vv4[0:nk, 0:4:3, :]
                            S.add("sp", lambda e: e.dma_start(out=vout1, in_=vin1), reads=vrd + [("vz",)], writes=vbuf_k, chan=("vb", r))
                        else:
                            vsrc_t = gA if jb["par"] else kvA
                            vin = [bass.AP(vsrc_t.tensor, 128 * TOK + vt0 * 128 + 64 * (jj // 4), [[vts * 128, nk], [1, 64]]) for hh_ in range(2)]
                            vrd = ["gA"] if jb["par"] else [("kvl", "va")]
                            vout = [vv4[0:nk, 0, :], vv4[0:nk, 3, :]]
                            S.add("sp", lambda e: [e.dma_start(out=vout[0], in_=vin[0]), e.dma_start(out=vout[1], in_=vin[1])],
                                  reads=vrd + [("vz",)], writes=vbuf_k, chan=("vba", r), ndma=2)
                        kap = ksrc[:, k0:k0 + kst * (nk - 1) + 1:kst]
                        qap = qp[:, q0:q0 + qst_ * (nq - 1) + 1:qst_]
                        for h in range(2):
                            bnk = SB[sbk][h]
                            S.add("pe", lambda e, h=h, bnk=bnk: e.matmul(
                                PS[bnk][0:nk, 0:nq], lhsT=kap[h * 64:(h + 1) * 64, :], rhs=qap[h * 64:(h + 1) * 64, :], start=True, stop=True),
                                reads=ksrc_k + qp_k, writes=pk(bnk))
                            S.add("act", lambda e, h=h, bnk=bnk: e.activation(out=pe_ap[0:nk, h, 0:nq], in_=PS[bnk][0:nk, 0:nq], func=AF.Exp),
                                  reads=pk(bnk), writes=pe_k)
                        e0 = jb["e0"]
                        vi = jb["vi"]
                        etp_l = etp
                        eng = "pool" if (ji % 3 == 2) else "dve"
                        if jb["par"]:
                            S.add("dve", lambda e: e.scalar_tensor_tensor(
                                out=pt_ap[0:nk, :, 0:nq], in0=pe_ap[0:nk, :, 0:nq], scalar=pfl[0:nk, 0:1], in1=etp_l[0:nk, vi, :, e0:e0 + nq], op0=ALU.mult, op1=ALU.mult),
                                reads=pe_k + etp_k + ["pfl"], writes=pt_k)
                        else:
                            S.add(eng, lambda e: e.tensor_tensor(
                                out=pt_ap[0:nk, :, 0:nq], in0=pe_ap[0:nk, :, 0:nq], in1=etp_l[0:nk, vi, :, e0:e0 + nq], op=ALU.mult),
                                reads=pe_k + etp_k, writes=pt_k)

                    def back(ji, jb):
                        r, sbk, vbuf, vbuf_k, pe_ap, pe_k, pt_ap, pt_k = job_bufs(ji)
                        nk = jb["k"][2]
                        for h in range(2):
                            for (span, c0, cs, cn, qo) in jb["pieces"]:
                                ocols = slice(c0, c0 + cs * (cn - 1) + 1, cs)
                                S.add("pe", lambda e, h=h, span=span, ocols=ocols, qo=qo, cn=cn: e.matmul(
                                    PS[NUM[span]][:, ocols], lhsT=vbuf[0:nk, h, :], rhs=pt_ap[0:nk, h, qo:qo + cn], start=False, stop=False, skip_group_check=True),
                                    reads=vbuf_k + pt_k, writes=pk(NUM[span]))
                                S.add("pe", lambda e, h=h, span=span, ocols=ocols, qo=qo, cn=cn: e.matmul(
                                    PS[DEN[span]][:, ocols], lhsT=onb[0:nk, h, :], rhs=pt_ap[0:nk, h, qo:qo + cn], start=False, stop=False, skip_group_check=True),
                                    reads=["onb"] + pt_k, writes=pk(DEN[span]))

                    def fin():
                        for span in range(2):
                            rden, rden_k = av(0, F32, 6144 * span, [512])
                            tmpf, tmpf_k = av(0, F32, 2048 + 2048 * span, [512])
                            if m == 0:
                                S.add("act", lambda e, span=span, jj=jj, rden=rden: e.activation(out=rden, in_=PS[DEN[span]][:, :], func=AF.Identity, bias=esk[:, jj:jj + 1]),
                                      reads=pk(DEN[span]) + [("esk", 0), ("esk", 1)], writes=rden_k)
                                S.add("dve", lambda e, rden=rden: e.reciprocal(out=rden, in_=rden), reads=rden_k, writes=rden_k)
                            else:
                                S.add("dve", lambda e, span=span, rden=rden: e.reciprocal(out=rden, in_=PS[DEN[span]][:, :]), reads=pk(DEN[span]), writes=rden_k)
                            S.add("dve", lambda e, span=span, rden=rden, tmpf=tmpf: e.tensor_tensor(out=tmpf, in0=PS[NUM[span]][:, :], in1=rden, op=ALU.mult), reads=pk(NUM[span]) + rden_k, writes=tmpf_k)
                            S.add("pool", lambda e, span=span, pi=pi, gp=gp, tmpf=tmpf: e.tensor_tensor(out=big[:, pi, span * 512:(span + 1) * 512], in0=tmpf, in1=gp[:, span * 512:(span + 1) * 512], op=ALU.mult),
                                  reads=tmpf_k + gp_k, writes=[("big", pi, t) for t in range(span * 4, span * 4 + 4)])

                    PR[pi] = (jobs, front, back, fin)

            for pi in range(16):
                make_pair(pi)
            seq = [(pi, ji, jb) for pi in range(16) for ji, jb in enumerate(PR[pi][0])]
            pair_loads(0)
            pair_loads(1)
            zero_acc()
            for step in range(len(seq) + PIPE):
                if step < len(seq):
                    pi, ji, jb = seq[step]
                    PR[pi][1](step, jb)
                if step >= PIPE:
                    pi, ji, jb = seq[step - PIPE]
                    PR[pi][2](step - PIPE, jb)
                    if ji == len(PR[pi][0]) - 1:
                        PR[pi][3]()
                        if pi + 2 < 16:
                            pair_loads(pi + 2)
                        if pi + 1 < 16:
                            zero_acc()
            out_proj(ev_w_out[j], 0)

        def odd_layer(j):
            phase0(od_ln_g[j])
            w_in = od_w_in[j]
            dma("sp", vgc[:, :], od_v_g[j].rearrange("(c p) -> p c", p=128), [], ["vgc"], "vgc", nonc=True)

            def nb():
                b = cnt["tb"] % 4
                cnt["tb"] += 1
                return b
            vst, vst_k = av(1, BF16, 0, [8, 512])
            junk, junk_k = av(0, BF16, 0, [512])
            ssv = stat[:, 128:192]
            S.add("dve", lambda e: e.memset(ssv, 0.0), writes=[("st", 8)])
            for s in range(8):
                wb = load_w(w_in, 4096 + 512 * s, 512)
                for t in range(NT):
                    bank = nb()
                    mm_tok(bank, wb, 512, t)
                    S.add("act", lambda e, bank=bank, t=t: e.activation(out=vst[:, t, :], in_=PS[bank][:, :], func=AF.Gelu), reads=pk(bank), writes=vst_k)
                    S.add("act", lambda e, t=t, s=s: e.activation(out=junk, in_=vst[:, t, :], func=AF.Square, accum_out=ssv[:, t * 8 + s:t * 8 + s + 1]),
                          reads=vst_k, writes=junk_k + [("st", 8)])
                dma("sp", vsp[:, 512 * s:512 * s + 512].rearrange("(t p) c -> p t c", p=128), vst, vst_k, [("vsp", s)], "vst")
            sv, rv, tv = stat[:, 192:200], stat[:, 200:208], stat[:, 208:216]
            S.add("dve", lambda e: e.reduce_sum(out=sv, in_=ssv.rearrange("p (t s) -> p t s", s=8), axis=AX.X), reads=[("st", 8)], writes=[("st", 9)])
            S.add("dve", lambda e: e.tensor_scalar(out=sv, in0=sv, scalar1=1.0 / 4096, scalar2=EPS, op0=ALU.mult, op1=ALU.add), reads=[("st", 9)], writes=[("st", 9)])
            rsqrt_newton(sv, rv, tv, [("st", 9)], [("st", 10)], [("st", 11)])
            ust, ust_k = av(2, BF16, 0, [4, 1024])
            ugst, ugst_k = av(3, BF16, 0, [4, 1024])
            gtm = [av(0, F32, 2048 + 2048 * i, [512]) for i in range(2)]
            for i in range(8):
                wb = load_w(w_in, 512 * i, 512)
                for ch in range(4):
                    for span in range(2):
                        bank = nb()
                        mm_feat(bank, wb, ch * 128, span)
                        S.add("act", lambda e, bank=bank, ch=ch, span=span: e.activation(out=ust[:, ch, span * 512:(span + 1) * 512], in_=PS[bank][:, :], func=AF.Gelu),
                              reads=pk(bank), writes=ust_k)
                wb = load_w(w_in, 8192 + 512 * i, 512)
                for ch in range(4):
                    for span in range(2):
                        bank = nb()
                        mm_feat(bank, wb, ch * 128, span)
                        g_ap, g_k = gtm[(ch * 2 + span) % 2]
                        S.add("act", lambda e, bank=bank, g_ap=g_ap: e.activation(out=g_ap, in_=PS[bank][:, :], func=AF.Silu), reads=pk(bank), writes=g_k)
                        S.add("dve", lambda e, ch=ch, span=span, g_ap=g_ap: e.tensor_tensor(out=ugst[:, ch, span * 512:(span + 1) * 512], in0=ust[:, ch, span * 512:(span + 1) * 512],
                                                                                                 in1=g_ap, op=ALU.mult), reads=ust_k + g_k, writes=ugst_k)
                dma("sp", ugd[512 * i:512 * i + 512, :].rearrange("(c p) n -> p c n", p=128), ugst, ugst_k, [("ugd", i)], "ugst")
            wsn, wsn_k = av(4, BF16, 0, [16, 128])
            wsT, wsT_k = av(5, BF16, 0, [16, 128])
            dma("pool", wsn, od_w_s[j].rearrange("g t s -> t g s"), [], wsn_k, "wsn")
            for half in range(2):
                bank = 6 + half
                for i in range(8):
                    g = half * 8 + i
                    S.add("pe", lambda e, bank=bank, i=i, g=g: e.transpose(psb(bank)[:, i * 128:(i + 1) * 128], wsn[:, g, :], ident), reads=wsn_k + ["cmb"], writes=pk(bank))
                S.add("dve", lambda e, bank=bank, half=half: e.tensor_tensor(out=wsT[:, half * 8:half * 8 + 8, :], in0=psb(bank).rearrange("p (a b) -> p a b", a=8),
                                                                           in1=bc(trim, [128, 8, 128], 1), op=ALU.mult), reads=pk(bank) + ["cmb"], writes=wsT_k)
            b_bc, b_bc_k = av(4, F32, 0, [16, 128])
            dma("sp", b_bc, bass.AP(od_b_s.tensor, j * 2048, [[0, 128], [1, 2048]]).rearrange("p (a b) -> p a b", a=16), [], b_bc_k, "bbc")
            def load_vg(g):
                if g < 16:
                    vg_, vg_k_ = av(1, BF16, 4096 * (g % 2), [8, 256])
                    dma("sp", vg_, vsp[:, 256 * g:256 * g + 256].rearrange("(t p) c -> p t c", p=128), [("vsp", g // 2)], vg_k_, ("vg", g % 2))

            def load_ug(c):
                if c < 32:
                    ug_, ug_k_ = av(3, BF16, 2048 * (c % 2), [1024])
                    dma("sp", ug_, ugd[128 * c:128 * c + 128, :], [("ugd", c // 4)], ug_k_, ("ug", c % 2))

            load_vg(0)
            load_ug(0)
            for H in range(2):
                for g in range(8 * H, 8 * H + 8):
                    gb_ = g % 2
                    vg, vg_k = av(1, BF16, 4096 * gb_, [8, 256])
                    wsr, wsr_k = av(2, BF16, 2048 * gb_, [8, 128])
                    for n in range(NT):
                        S.add("act", lambda e, n=n, g=g, wsr=wsr: e.activation(out=wsr[:, n, :], in_=wsT[:, g, :], func=AF.Copy, scale=rv[:, n:n + 1]),
                              reads=wsT_k + [("st", 10)], writes=wsr_k)
                    load_vg(g + 1)
                    for cc in range(2):
                        c = 2 * g + cc
                        ugb = c % 2
                        ug, ug_k = av(3, BF16, 2048 * ugb, [1024])
                        load_ug(c + 1)
                        for span in range(2):
                            bank = nb()
                            for i in range(4):
                                n = span * 4 + i
                                S.add("pe", lambda e, bank=bank, i=i, n=n, cc=cc, vg=vg, wsr=wsr: e.matmul(
                                    PS[bank][:, i * 128:(i + 1) * 128], lhsT=vg[:, n, cc * 128:(cc + 1) * 128], rhs=wsr[:, n, :], start=True, stop=True),
                                    reads=vg_k + wsr_k, writes=pk(bank))
                            t1, t1_k = gtm[span]
                            S.add("dve", lambda e, bank=bank, c=c, g=g, t1=t1: e.scalar_tensor_tensor(
                                out=t1.rearrange("p (a b) -> p a b", a=4), in0=PS[bank][:, :].rearrange("p (a b) -> p a b", a=4), scalar=vgc[:, c:c + 1],
                                in1=bc(b_bc[:, g, :], [128, 4, 128], 1), op0=ALU.mult, op1=ALU.add),
                                reads=pk(bank) + ["vgc"] + b_bc_k, writes=t1_k)
                            S.add("dve", lambda e, c=c, span=span, H=H, t1=t1, ug=ug: e.tensor_tensor(
                                out=big[:, c - 16 * H, span * 512:(span + 1) * 512], in0=t1, in1=ug[:, span * 512:(span + 1) * 512], op=ALU.mult),
                                reads=t1_k + ug_k, writes=[("big", c - 16 * H, t) for t in range(span * 4, span * 4 + 4)])
                out_proj(od_w_out[j], 2048 * H)

        if any(l % 2 == 0 for l in layers):
            build_etables()
        for l in layers:
            wplan.extend(wplan_for(l))
        for l in layers:
            if stage <= 0:
                break
            if l % 2 == 0:
                even_layer(l // 2)
            else:
                odd_layer(l // 2)
        for t in range(NT):
            dma("sp", y_out[t * 128:(t + 1) * 128, :], xres[:, t, :], [("x", t)], [("y", t)], ("yout", t))
        S.add("sp", lambda e: e.nop(), reads=[("y", t) for t in range(NT)])
        S.emit(nc, st)
    return nc


_CACHE = {}


def kernel(x, ev_ln_g, ev_w_in, ev_qk_g, ev_sinks, ev_w_out, od_ln_g, od_w_in, od_v_g, od_w_s, od_b_s, od_w_out, rel_bias,
           _layers=(0, 1, 2, 3)):
    if _layers not in _CACHE:
        _CACHE[_layers] = build_program(_layers)
    nc = _CACHE[_layers]
    oh, cm, on = _consts()
    f = lambda a: np.ascontiguousarray(np.asarray(a, dtype=np.float32))
    x = f(x)
    shared = dict(ev_ln_g=f(ev_ln_g), ev_w_in=f(ev_w_in), ev_qk_g=f(ev_qk_g), ev_sinks=f(ev_sinks), ev_w_out=f(ev_w_out),
                  od_ln_g=f(od_ln_g), od_w_in=f(od_w_in), od_v_g=f(od_v_g), od_w_s=f(od_w_s), od_b_s=f(od_b_s), od_w_out=f(od_w_out),
                  rel_bias=f(rel_bias), c_oh=oh, c_cm=cm, c_on=on)
    in_maps = []
    for c in range(8):
        b, h = c // 2, c % 2
        m = dict(shared)
        m["x"] = np.ascontiguousarray(x[b, h * TOK:(h + 1) * TOK, :])
        m["c_pf"] = np.full((128, 1), float(h), np.float32)
        in_maps.append(m)
    res = run_bass_kernel_spmd(nc, in_maps, core_ids=list(range(8)))
    out = np.empty((4, 2048, D), np.float32)
    for c in range(8):
        b, h = c // 2, c % 2
        out[b, h * TOK:(h + 1) * TOK, :] = np.asarray(res.results[c]["y"], dtype=np.float32)
    return out
```

```python
import math
import numpy as np
from contextlib import ExitStack
import concourse.bass as bass
import concourse.mybir as mybir
from concourse.bass_utils import run_bass_kernel_spmd

F32 = mybir.dt.float32
BF16 = mybir.dt.bfloat16
ALU = mybir.AluOpType
AF = mybir.ActivationFunctionType
AX = mybir.AxisListType

RAW, WAW, WAR = 1, 2, 4
EPS = 1e-6
NEG = -30000.0


class _Op:
    __slots__ = ("i", "q", "fn", "chan", "ndma", "inc", "deps", "needed", "sem", "val")


class Sched:
    LIMIT = 30000
    QUEUES = ("pe", "act", "dve", "pool", "sp")

    def __init__(self):
        self.ops = []
        self.lastw = {}
        self.rd_q = {}
        self.rd_dma = {}

    def add(self, q, fn, reads=(), writes=(), chan=None, ndma=1, inc=16):
        i = len(self.ops)
        op = _Op()
        op.i, op.q, op.fn, op.chan, op.ndma, op.inc = i, q, fn, chan, ndma, inc
        op.needed = chan is not None
        op.sem = None
        op.val = 0
        deps = {}
        for k in reads:
            w = self.lastw.get(k)
            if w is not None:
                deps[w] = deps.get(w, 0) | RAW
        for k in writes:
            w = self.lastw.get(k)
            if w is not None:
                deps[w] = deps.get(w, 0) | WAW
            for r in self.rd_q.get(k, {}).values():
                deps[r] = deps.get(r, 0) | WAR
            for r in self.rd_dma.get(k, ()):
                deps[r] = deps.get(r, 0) | WAR
        for k in reads:
            if chan is not None:
                self.rd_dma.setdefault(k, []).append(i)
            else:
                self.rd_q.setdefault(k, {})[q] = i
        for k in writes:
            self.lastw[k] = i
            self.rd_q[k] = {}
            self.rd_dma[k] = []
        op.deps = []
        for d, kind in deps.items():
            dop = self.ops[d]
            if dop.chan is None and chan is None and dop.q == q:
                if q == "pe":
                    continue
                if not (kind & RAW):
                    continue
            op.deps.append(d)
            dop.needed = True
        self.ops.append(op)
        return i

    def emit(self, nc, stack):
        state = {}
        sems = {}
        for op in self.ops:
            if not op.needed:
                continue
            key = ("c", op.chan) if op.chan is not None else ("q", op.q)
            inc = op.inc * op.ndma if op.chan is not None else 1
            idx, val = state.get(key, (0, 0))
            if val + inc > self.LIMIT:
                idx, val = idx + 1, 0
            val += inc
            state[key] = (idx, val)
            op.sem = (key, idx)
            op.val = val
            if op.sem not in sems:
                sems[op.sem] = stack.enter_context(nc.semaphore("s%d" % len(sems)))
        self.nsems = len(sems)
        byq = {q: [] for q in self.QUEUES}
        for op in self.ops:
            byq[op.q].append(op)
        ops = self.ops

        def run(q, eng):
            waited = {}
            for op in byq[q]:
                need = {}
                for d in op.deps:
                    dop = ops[d]
                    if waited.get(dop.sem, 0) >= dop.val:
                        continue
                    if need.get(dop.sem, 0) < dop.val:
                        need[dop.sem] = dop.val
                for s, v in need.items():
                    eng.wait_ge(sems[s], v)
                    waited[s] = v
                r = op.fn(eng)
                if op.needed:
                    if op.chan is not None:
                        insts = r if isinstance(r, (list, tuple)) else [r]
                        assert len(insts) == op.ndma, (len(insts), op.ndma)
                        for ins in insts:
                            ins.then_inc(sems[op.sem], op.inc)
                    else:
                        r.then_inc(sems[op.sem], 1)

        block = stack.enter_context(nc.Block())

        @block.tensor
        def _(e):
            run("pe", e)

        @block.scalar
        def _(e):
            run("act", e)

        @block.vector
        def _(e):
            run("dve", e)

        @block.gpsimd
        def _(e):
            run("pool", e)

        @block.sync
        def _(e):
            run("sp", e)


def _t5_bucket(n):
    n = np.maximum(n, 0)
    max_exact = 16
    large = max_exact + (np.log(np.maximum(n, 1) / max_exact) / np.log(2048 / max_exact) * (32 - max_exact)).astype(np.int32)
    large = np.minimum(large, 31)
    return np.where(n < max_exact, n, large).astype(np.int32)


VARIANTS = ((1, 127), (1, 128), (4, 128), (16, 128))


def _consts():
    oh = np.zeros((33, 4, 384), np.float32)
    for v, (dil, maxd) in enumerate(VARIANTS):
        for m in range(384):
            dist = m - 127
            if 0 <= dist <= maxd:
                oh[int(_t5_bucket(np.array(dist * dil))), v, m] = 1.0
            else:
                oh[32, v, m] = 1.0
    cm = np.zeros((128, 4, 128), np.float32)
    cm[:, 0, :] = np.eye(128)
    cm[:, 1, :] = np.eye(128)[::-1]
    cm[:, 2, :] = np.triu(np.ones((128, 128)))
    on = np.zeros((128, 2, 128), np.float32)
    on[:, 0, 0:64] = 1.0
    on[:, 1, 64:128] = 1.0
    return oh, cm, on


PIPE = 3
TOK = 1024
NT = 8
D = 2048
KC = 16


def build_program(layers=(0, 1, 2, 3), stage=99, ncores=8):
    nc = bass.Bass("TRN2", target_bir_lowering=False)
    S = Sched()
    st = ExitStack()

    def din(name, shape, dt=F32):
        return nc.dram_tensor(name, list(shape), dt, kind="ExternalInput").ap()

    def dscr(name, shape, dt=BF16):
        return nc.dram_tensor(name, list(shape), dt, kind="Internal").ap()

    x_in = din("x", [TOK, D])
    ev_ln_g = din("ev_ln_g", [2, D])
    ev_w_in = din("ev_w_in", [2, D, 6400])
    ev_qk_g = din("ev_qk_g", [2, 4, 64])
    ev_sinks = din("ev_sinks", [2, 16])
    ev_w_out = din("ev_w_out", [2, D, D])
    od_ln_g = din("od_ln_g", [2, D])
    has_odd = any(l % 2 == 1 for l in layers)
    od_w_in = din("od_w_in", [2, D, 12288] if has_odd else [2, 128, 128])
    od_v_g = din("od_v_g", [2, 4096])
    od_w_s = din("od_w_s", [2, 16, 128, 128])
    od_b_s = din("od_b_s", [2, 16, 128])
    od_w_out = din("od_w_out", [2, 4096, D] if has_odd else [2, 128, 128])
    rel_bias = din("rel_bias", [32, 32])
    c_oh = din("c_oh", [33, 4, 384])
    c_cm = din("c_cm", [128, 4, 128])
    c_on = din("c_on", [128, 2, 128])
    c_pf = din("c_pf", [128, 1])
    y_out = nc.dram_tensor("y", [TOK, D], F32, kind="ExternalOutput").ap()

    qT_d = dscr("qT_d", [2048, TOK])
    kvK = dscr("kvK", [1024, TOK])
    kvV = dscr("kvV", [1024, TOK])
    kvA = dscr("kvA", [256, TOK])
    gK = dscr("gK", [2048, TOK])
    gV = dscr("gV", [2048, TOK])
    gA = dscr("gA", [512, TOK])
    gT_d = dscr("gT_d", [2048, TOK])
    evd = dscr("evd", [4, 16, 384])
    etd = dscr("etd", [4, 16, 128, 256])
    vsp = dscr("vsp", [TOK, 4096])
    ugd = dscr("ugd", [4096, TOK])

    def sb(name, shape, dt):
        return st.enter_context(nc.sbuf_tensor(name, list(shape), dt))

    with st:
        xres = sb("xres", [128, NT, D], F32)
        big = sb("big", [128, KC, TOK], BF16)
        wsl = [sb("wsl%d" % i, [128, KC, 512], BF16) for i in range(2)]
        AR = [sb("ar%d" % i, [128, 2048], F32) for i in range(7)]
        cmb = sb("cmb", [128, 4, 128], BF16)
        onb = sb("onb", [128, 2, 128], BF16)
        pfl = sb("pfl", [128, 1], F32)
        stat = sb("stat", [128, 256], F32)
        gtab = sb("gtab", [128, 4, 64], F32)
        gqk = sb("gqk", [128, 2, 64], F32)
        esk = sb("esk", [128, 8], F32)
        vgc = sb("vgc", [128, 32], F32)
        vAb = sb("vAb", [128, 2, 9, 256], BF16)
        PS = [st.enter_context(nc.psum_tensor("ps%d" % i, [128, 512], F32)) for i in range(8)]

        ident = cmb[:, 0, :]
        Jm = cmb[:, 1, :]
        trim = cmb[:, 2, :]

        def av(i, dt, off_bytes, shape):
            n = int(np.prod(shape))
            esz = 4 if dt == F32 else 2
            nbytes = n * esz
            assert off_bytes + nbytes <= 8192 and off_bytes % 4 == 0
            base = AR[i][:, off_bytes // 4:(off_bytes + nbytes) // 4]
            ap = base if dt == F32 else base.bitcast(BF16)
            if len(shape) == 2:
                ap = ap.rearrange("p (a b) -> p a b", a=shape[0])
            elif len(shape) == 3:
                ap = ap.rearrange("p (a b c) -> p a b c", a=shape[0], b=shape[1])
            keys = [("ar", i, s) for s in range(off_bytes // 512, (off_bytes + nbytes + 511) // 512)]
            return ap, keys

        def bc(ap, shape, axis):
            return ap.unsqueeze(axis).broadcast_to(list(shape))

        def psb(i):
            return PS[i][:, :].bitcast(BF16)

        def pk(i):
            return [("ps", i)]

        cnt = {"w": 0, "tb": 0}

        def dma(q, out, in_, reads, writes, chan, nonc=False):
            if nonc:
                S.add(q, lambda e: e.dma_start(out=out, in_=in_, allow_slow_non_contiguous=True), reads=reads, writes=writes, chan=chan)
            else:
                S.add(q, lambda e: e.dma_start(out=out, in_=in_), reads=reads, writes=writes, chan=chan)

        wplan = []
        wst = {"issued": 0, "used": 0}

        def wplan_for(l):
            j = l // 2
            if l % 2 == 0:
                wi, wo = ev_w_in[j], ev_w_out[j]
                sp_ = [(wi, 3328, 512, 0), (wi, 3840, 512, 0), (wi, 1024, 256, 0), (wi, 4352, 512, 0), (wi, 4864, 512, 0)]
                sp_ += [(wi, c, 512, 0) for c in (0, 512, 2304, 2816, 1280, 1792, 5376, 5888)] + [(wo, 512 * s_, 512, 0) for s_ in range(4)]
            else:
                wi, wo = od_w_in[j], od_w_out[j]
                sp_ = [(wi, 4096 + 512 * s_, 512, 0) for s_ in range(8)]
                for i in range(8):
                    sp_ += [(wi, 512 * i, 512, 0), (wi, 8192 + 512 * i, 512, 0)]
                for H in range(2):
                    sp_ += [(wo, 512 * s_, 512, 2048 * H) for s_ in range(4)]
            return sp_

        def load_w(wap, c0, W, rows0=0):
            k = wst["used"]
            assert wplan[k][1:] == (c0, W, rows0), (wplan[k][1:], c0, W, rows0)
            while wst["issued"] < min(k + 2, len(wplan)):
                i = wst["issued"]
                wp, pc0, pW, pr0 = wplan[i]
                src = wp[pr0:pr0 + D, pc0:pc0 + pW].rearrange("(k p) c -> p k c", p=128)
                dma("pool", wsl[i % 2][:, :, 0:pW], src, [], [("w", i % 2)], ("w", i % 2))
                wst["issued"] += 1
            wst["used"] += 1
            return k % 2

        def hkeys(k, tiles):
            return [("big", k, t) for t in tiles]

        def rsqrt_newton(v_ap, r_ap, t_ap, keys_v, keys_r, keys_t):
            S.add("act", lambda e: e.activation(out=r_ap, in_=v_ap, func=AF.Sqrt), reads=keys_v, writes=keys_r)
            S.add("dve", lambda e: e.reciprocal(out=r_ap, in_=r_ap), reads=keys_r, writes=keys_r)
            S.add("dve", lambda e: e.tensor_tensor(out=t_ap, in0=r_ap, in1=r_ap, op=ALU.mult), reads=keys_r, writes=keys_t)
            S.add("dve", lambda e: e.tensor_tensor(out=t_ap, in0=t_ap, in1=v_ap, op=ALU.mult), reads=keys_t + keys_v, writes=keys_t)
            S.add("dve", lambda e: e.tensor_scalar(out=t_ap, in0=t_ap, scalar1=-0.5, scalar2=1.5, op0=ALU.mult, op1=ALU.add), reads=keys_t, writes=keys_t)
            S.add("dve", lambda e: e.tensor_tensor(out=r_ap, in0=r_ap, in1=t_ap, op=ALU.mult), reads=keys_r + keys_t, writes=keys_r)

        dma("pool", cmb[:, :, :], c_cm, [], ["cmb"], "c0")
        dma("pool", onb[:, :, :], c_on, [], ["onb"], "c1")
        dma("sp", pfl[:, :], c_pf, [], ["pfl"], "c2")
        for t in range(NT):
            dma("sp", xres[:, t, :], x_in[t * 128:(t + 1) * 128, :], [], [("x", t)], ("xin", t))

        def phase0(lnrow):
            lng, lng_k = av(0, F32, 0, [2048])
            junk, junk_k = av(1, BF16, 0, [2048])
            hb = [av(1, BF16, 4096, [2048]), av(2, BF16, 0, [2048])]
            ssq, rr, tt = stat[:, 0:8], stat[:, 8:16], stat[:, 16:24]
            dma("sp", lng, bass.AP(lnrow.tensor, lnrow.offset, [[0, 128], [1, D]]), [], lng_k, "lng")
            S.add("dve", lambda e: e.memset(ssq, 0.0), writes=[("st", 0)])
            for t in range(NT):
                S.add("act", lambda e, t=t: e.activation(out=junk, in_=xres[:, t, :], func=AF.Square, accum_out=ssq[:, t:t + 1]),
                      reads=[("x", t)], writes=junk_k + [("st", 0)])
            S.add("dve", lambda e: e.tensor_scalar(out=ssq, in0=ssq, scalar1=1.0 / D, scalar2=EPS, op0=ALU.mult, op1=ALU.add),
                  reads=[("st", 0)], writes=[("st", 0)])
            rsqrt_newton(ssq, rr, tt, [("st", 0)], [("st", 1)], [("st", 2)])
            for t in range(NT):
                h_ap, h_k = hb[t % 2]
                S.add("dve", lambda e, t=t, h_ap=h_ap: e.scalar_tensor_tensor(out=h_ap, in0=xres[:, t, :], scalar=rr[:, t:t + 1], in1=lng,
                                                                              op0=ALU.mult, op1=ALU.mult),
                      reads=[("x", t), ("st", 1)] + lng_k, writes=h_k)
                for half in range(2):
                    bank = 6 + half
                    for i in range(8):
                        k = half * 8 + i
                        S.add("pe", lambda e, bank=bank, i=i, k=k, h_ap=h_ap: e.transpose(psb(bank)[:, i * 128:(i + 1) * 128], h_ap[:, k * 128:(k + 1) * 128], ident),
                              reads=h_k + ["cmb"], writes=pk(bank))
                    S.add("act", lambda e, bank=bank, half=half, t=t: e.activation(
                        out=big[:, half * 8:half * 8 + 8, t * 128:(t + 1) * 128],
                        in_=psb(bank).rearrange("p (a b) -> p a b", a=8), func=AF.Copy),
                        reads=pk(bank), writes=[("big", k, t) for k in range(half * 8, half * 8 + 8)])

        def mm_tok(bank, wb, W, t, wcol0=0):
            for k in range(KC):
                S.add("pe", lambda e, k=k: e.matmul(PS[bank][:, 0:W], lhsT=big[:, k, t * 128:(t + 1) * 128], rhs=wsl[wb][:, k, wcol0:wcol0 + W],
                                                    start=(k == 0), stop=(k == KC - 1)),
                      reads=[("big", k, t), ("w", wb)], writes=pk(bank))

        def mm_feat(bank, wb, wcol0, span):
            for k in range(KC):
                S.add("pe", lambda e, k=k: e.matmul(PS[bank][:, :], lhsT=wsl[wb][:, k, wcol0:wcol0 + 128], rhs=big[:, k, span * 512:(span + 1) * 512],
                                                    start=(k == 0), stop=(k == KC - 1)),
                      reads=hkeys(k, range(span * 4, span * 4 + 4)) + [("w", wb)], writes=pk(bank))

        def out_proj(wap, rows0):
            for sl in range(4):
                wb = load_w(wap, sl * 512, 512, rows0)
                for t in range(NT):
                    bank = cnt["tb"] % 4
                    cnt["tb"] += 1
                    mm_tok(bank, wb, 512, t)
                    S.add("dve", lambda e, bank=bank, t=t, sl=sl: e.tensor_tensor(out=xres[:, t, sl * 512:(sl + 1) * 512], in0=PS[bank][:, :],
                                                                                 in1=xres[:, t, sl * 512:(sl + 1) * 512], op=ALU.add),
                          reads=pk(bank) + [("x", t)], writes=[("x", t)])

        def build_etables():
            text, text_k = av(3, F32, 0, [32])
            oht, oht_k = av(4, F32, 0, [4, 384])
            evs, evs_k = av(3, BF16, 512, [4, 384])
            S.add("dve", lambda e: e.memset(text[32:33, :], NEG), writes=[("txr",)])
            dma("sp", text[0:32, :], rel_bias, [], text_k, "tb0")
            dma("sp", oht[0:33, :, :], c_oh, [], oht_k, "tb1")
            for v in range(4):
                hs = 0 if v == 0 else 16
                S.add("pe", lambda e, v=v, hs=hs: e.matmul(PS[0][0:16, 0:384], lhsT=text[0:33, hs:hs + 16], rhs=oht[0:33, v, :], start=True, stop=True),
                      reads=text_k + oht_k + [("txr",)], writes=pk(0))
                S.add("act", lambda e, v=v: e.activation(out=evs[0:16, v, :], in_=PS[0][0:16, 0:384], func=AF.Exp), reads=pk(0), writes=evs_k)
            dma("sp", evd.rearrange("v h m -> h v m"), evs[0:16, :, :], evs_k, ["evd"], "tb2")
            hk, hk_k = av(5, BF16, 0, [16, 256])
            ets, ets_k = av(4, BF16, 0, [16, 256])
            for v in range(4):
                dma("sp", hk, bass.AP(evd.tensor, v * 16 * 384, [[1, 128], [384, 16], [1, 256]]), ["evd"], hk_k, "tb3")
                for hh in range(8):
                    bank = hh % 2
                    S.add("pe", lambda e, hh=hh, bank=bank: e.matmul(PS[bank][:, :], lhsT=Jm, rhs=hk[:, 2 * hh:2 * hh + 2, :], start=True, stop=True),
                          reads=hk_k + ["cmb"], writes=pk(bank))
                    S.add("act", lambda e, hh=hh, bank=bank: e.activation(out=ets[:, 2 * hh:2 * hh + 2, :], in_=PS[bank][:, :].rearrange("p (a b) -> p a b", a=2), func=AF.Copy),
                          reads=pk(bank), writes=ets_k)
                dma("sp", etd[v].rearrange("h k c -> k h c"), ets, ets_k, [("etd", v)], "tb4")

        def even_layer(j):
            phase0(ev_ln_g[j])
            w_in = ev_w_in[j]
            dma("sp", gtab[:, :, :], bass.AP(ev_qk_g.tensor, j * 256, [[0, 128], [1, 256]]).rearrange("p (a b) -> p a b", a=4), [], ["gtab"], "gt")
            for m in range(2):
                S.add("dve", lambda e, m=m: e.scalar_tensor_tensor(out=gqk[:, m, :], in0=gtab[:, 2 * m, :], scalar=0.125, in1=gtab[:, 2 * m + 1, :],
                                                                   op0=ALU.mult, op1=ALU.mult), reads=["gtab"], writes=[("gqk", m)])
            for hf in range(2):
                dma("sp", esk[hf * 64:(hf + 1) * 64, :], bass.AP(ev_sinks.tensor, j * 16 + hf, [[0, 64], [2, 8]]), [], [("esk", hf)], ("esk", hf), nonc=True)
            S.add("act", lambda e: e.activation(out=esk[:, :], in_=esk[:, :], func=AF.Exp), reads=[("esk", 0), ("esk", 1)], writes=[("esk", 0), ("esk", 1)])

            f_sq, f_sq_k = av(0, F32, 0, [512])
            f_q, f_q_k = av(0, F32, 2048, [512])
            qnb = [av(0, BF16, 4096 + 1024 * i, [512]) for i in range(4)]
            qst, qst_k = av(2, BF16, 0, [4, 1024])
            vst, vst_k = av(3, BF16, 0, [8, 512])
            gst = [av(4, BF16, 2048 * i, [1024]) for i in range(4)]
            vast, vast_k = av(5, BF16, 0, [8, 128])
            it = {"n": 0}

            def qk_tile(bank, t, nh, col0, gm, dst, dst_k, dcol):
                W = nh * 64
                ss, rr, tt = stat[:, 32:32 + nh], stat[:, 48:48 + nh], stat[:, 64:64 + nh]
                src = PS[bank][:, col0:col0 + W]
                S.add("act", lambda e: e.activation(out=f_sq[:, 0:W], in_=src, func=AF.Square), reads=pk(bank), writes=f_sq_k)
                S.add("dve", lambda e: e.reduce_sum(out=ss, in_=f_sq[:, 0:W].rearrange("p (h d) -> p h d", d=64), axis=AX.X), reads=f_sq_k, writes=[("st", 4)])
                S.add("dve", lambda e: e.tensor_scalar(out=ss, in0=ss, scalar1=1.0 / 64, scalar2=EPS, op0=ALU.mult, op1=ALU.add), reads=[("st", 4)], writes=[("st", 4)])
                rsqrt_newton(ss, rr, tt, [("st", 4)], [("st", 5)], [("st", 6)])
                qb_ap, qb_k = qnb[it["n"] % 4]
                it["n"] += 1
                src3 = src.rearrange("p (h d) -> p h d", d=64)
                if gm is not None:
                    S.add("dve", lambda e: e.tensor_tensor(out=f_q[:, 0:W].rearrange("p (h d) -> p h d", d=64), in0=src3, in1=bc(rr, [128, nh, 64], 2), op=ALU.mult),
                          reads=pk(bank) + [("st", 5)], writes=f_q_k)
                    S.add("dve", lambda e: e.tensor_tensor(out=qb_ap[:, 0:W].rearrange("p (h d) -> p h d", d=64), in0=f_q[:, 0:W].rearrange("p (h d) -> p h d", d=64),
                                                            in1=bc(gqk[:, gm, :], [128, nh, 64], 1), op=ALU.mult),
                          reads=f_q_k + [("gqk", gm)], writes=qb_k)
                else:
                    S.add("dve", lambda e: e.tensor_tensor(out=qb_ap[:, 0:W].rearrange("p (h d) -> p h d", d=64), in0=src3, in1=bc(rr, [128, nh, 64], 2), op=ALU.mult),
                          reads=pk(bank) + [("st", 5)], writes=qb_k)
                npair = nh // 2
                tb = 6 + (it["n"] % 2)

                def post():
                    for i in range(npair):
                        S.add("pe", lambda e, i=i: e.transpose(psb(tb)[:, i * 128:(i + 1) * 128], qb_ap[:, i * 128:(i + 1) * 128], ident), reads=qb_k + ["cmb"], writes=pk(tb))
                    S.add("act", lambda e: e.activation(out=dst[:, dcol:dcol + npair, t * 128:(t + 1) * 128],
                                                        in_=psb(tb)[:, 0:npair * 128].rearrange("p (a b) -> p a b", a=npair), func=AF.Copy),
                          reads=pk(tb), writes=dst_k)
                return post

            def nb():
                b = cnt["tb"] % 4
                cnt["tb"] += 1
                return b

            def qk_slabs(specs):
                for (c0, gm, ddst, drow) in specs:
                    wb = load_w(w_in, c0, 512)
                    pend = []
                    for t in range(NT):
                        bank = nb()
                        mm_tok(bank, wb, 512, t)
                        if len(pend) >= 2:
                            pend.pop(0)()
                        pend.append(qk_tile(bank, t, 8, 0, gm, qst, qst_k, 0))
                    for p_ in pend:
                        p_()
                    dma("sp", ddst[drow:drow + 512, :].rearrange("(i p) n -> p i n", p=128), qst, qst_k, [("qk_d", id(ddst), drow)], "qst")

            qk_slabs(((3328, None, kvK, 0), (3840, None, kvK, 512)))
            wb = load_w(w_in, 1024, 256)
            pend = []
            for t in range(NT):
                bank = nb()
                mm_tok(bank, wb, 256, t)
                if len(pend) >= 2:
                    pend.pop(0)()
                pend.append(qk_tile(bank, t, 2, 0, None, qst, qst_k, 0))
                S.add("act", lambda e, bank=bank, t=t: e.activation(out=vast[:, t, :], in_=PS[bank][:, 128:256], func=AF.Copy), reads=pk(bank), writes=vast_k)
            for p_ in pend:
                p_()
            dma("sp", kvA[0:128, :], qst[:, 0, :], qst_k, [("kvl", "ka")], "qst")
            dma("sp", bass.AP(kvA.tensor, 128 * TOK, [[128, 128], [128 * 128, 8], [1, 128]]), vast, vast_k, [("kvl", "va")], "vast")
            for s2 in range(2):
                wb = load_w(w_in, 4352 + 512 * s2, 512)
                for t in range(NT):
                    bank = nb()
                    mm_tok(bank, wb, 512, t)
                    S.add("act", lambda e, bank=bank, t=t: e.activation(out=vst[:, t, :], in_=PS[bank][:, :], func=AF.Copy), reads=pk(bank), writes=vst_k)
                dma("sp", kvV[:, 512 * s2:512 * s2 + 512].rearrange("(t p) c -> p t c", p=128), vst, vst_k, [("kvl", "vb", s2)], "vst")
            rg = [[2 * i, 2 * i + 1] for i in range(ncores // 2)]
            for (src_, dst_, rk, wk, ch) in ((kvK, gK, [("qk_d", id(kvK), 0), ("qk_d", id(kvK), 512)], "gK", "cc0"),
                                            (kvV, gV, [("kvl", "vb", 0), ("kvl", "vb", 1)], "gV", "cc1"),
                                            (kvA, gA, [("kvl", "ka"), ("kvl", "va")], "gA", "cc2")):
                S.add("pool", lambda e, src_=src_, dst_=dst_: e.collective_compute("AllGather", ALU.bypass, replica_groups=rg, ins=[src_], outs=[dst_]),
                      reads=rk, writes=[wk], chan=ch, inc=1)

            qk_slabs(((0, 0, qT_d, 0), (512, 0, qT_d, 512), (2304, 1, qT_d, 1024), (2816, 1, qT_d, 1536)))
            gi = 0
            for (c0, drow) in ((1280, 0), (1792, 512), (5376, 1024), (5888, 1536)):
                wb = load_w(w_in, c0, 512)
                for ch in range(4):
                    g_ap, g_k = gst[gi % 4]
                    gi += 1
                    for span in range(2):
                        bank = nb()
                        mm_feat(bank, wb, ch * 128, span)
                        S.add("act", lambda e, bank=bank, span=span, g_ap=g_ap: e.activation(out=g_ap[:, span * 512:(span + 1) * 512], in_=PS[bank][:, :], func=AF.Silu),
                              reads=pk(bank), writes=g_k)
                    dma("sp", gT_d[drow + ch * 128:drow + (ch + 1) * 128, :], g_ap, g_k, [("gT_d", drow + ch * 128)], ("gst", gi % 4))

            if stage <= 1:
                return
            if stage <= 2:
                return
            vz, vz_k = av(5, BF16, 0, [4, 2, 128])
            S.add("dve", lambda e: e.memset(vz, 0.0), writes=vz_k + [("vz",)])
            NUM = (2, 3)
            DEN = (4, 5)
            SB = ((0, 1), (6, 7))
            LD = {}

            def pair_loads(pi, emit=True):
                    def dma_(*a_, **k_):
                        if emit:
                            dma(*a_, **k_)

                    def sadd(*a_, **k_):
                        if emit:
                            S.add(*a_, **k_)
                    m = pi // 8
                    jj = pi % 8
                    ab = pi % 2
                    arq = 1 + ab
                    qp, qp_k = av(arq, BF16, 0, [1024])
                    kp, kp_k = av(arq, BF16, 2048, [1024])
                    kpp, kpp_k = av(arq, BF16, 4096, [1024])
                    gp, gp_k = av(arq, BF16, 6144, [1024])
                    nv = 1 if m == 0 else 3
                    etp, etp_k = av(3, BF16, 4096 * 0, [3, 2, 256]) if ab == 0 else av(4, BF16, 0, [3, 2, 256])
                    frow = m * 1024 + jj * 128
                    dma_("sp", qp, qT_d[frow:frow + 128, :], [("qk_d", id(qT_d), (frow // 512) * 512)], qp_k, ("qp", ab))
                    if m == 1:
                        dma_("sp", kp, kvK[jj * 128:(jj + 1) * 128, :], [("qk_d", id(kvK), (jj // 4) * 512)], kp_k, ("kp", ab))
                        dma_("sp", kpp, gK[jj * 128:(jj + 1) * 128, :], ["gK"], kpp_k, ("kpp", ab))
                    else:
                        g = jj // 4
                        r0 = 64 * g
                        sadd("sp", lambda e, kp=kp, r0=r0: [e.dma_start(out=kp[0:64, :], in_=kvA[r0:r0 + 64, :]), e.dma_start(out=kp[64:128, :], in_=kvA[r0:r0 + 64, :])],
                              reads=[("kvl", "ka")], writes=kp_k, chan=("kp", ab), ndma=2)
                        sadd("sp", lambda e, kpp=kpp, r0=r0: [e.dma_start(out=kpp[0:64, :], in_=gA[r0:r0 + 64, :]), e.dma_start(out=kpp[64:128, :], in_=gA[r0:r0 + 64, :])],
                              reads=["gA"], writes=kpp_k, chan=("kpp", ab), ndma=2)
                    dma_("sp", gp, gT_d[frow:frow + 128, :], [("gT_d", frow)], gp_k, ("gp", ab))
                    if m == 0:
                        g = jj // 4
                        va4 = vAb[:, ab, :, :].rearrange("p t (s d) -> p t s d", s=4)
                        vsrc_o = [bass.AP(kvA.tensor, 128 * TOK + 64 * g, [[128, 128], [128 * 128, 8], [1, 64]]) for _ in range(2)]
                        vsrc_p = [bass.AP(gA.tensor, 128 * TOK + 896 * 128 + 64 * g, [[128, 128], [1, 64]]) for _ in range(2)]
                        sadd("sp", lambda e, va4=va4, vsrc_o=vsrc_o, vsrc_p=vsrc_p: [
                            e.dma_start(out=va4[:, 0:8, 0, :], in_=vsrc_o[0]), e.dma_start(out=va4[:, 0:8, 3, :], in_=vsrc_o[1]),
                            e.dma_start(out=va4[:, 8, 0, :], in_=vsrc_p[0]), e.dma_start(out=va4[:, 8, 3, :], in_=vsrc_p[1])],
                             reads=[("kvl", "va"), "gA"], writes=[("vA", ab)], chan=("vA", ab), ndma=4)
                    vlist = (0,) if m == 0 else (1, 2, 3)
                    for vi, v in enumerate(vlist):
                        dma_("sp", etp[:, vi, :, :], etd[v, 2 * jj:2 * jj + 2].rearrange("h k c -> k h c"), [("etd", v)], etp_k, ("etp", ab, vi))
                    LD[pi] = (qp, qp_k, kp, kp_k, kpp, kpp_k, gp, gp_k, etp, etp_k)

            PR = {}

            def zero_acc():
                for b_ in NUM + DEN:
                    S.add("act", lambda e, b_=b_: e.memzero(PS[b_][:, :]), writes=pk(b_))

            def make_pair(pi):
                    m = pi // 8
                    jj = pi % 8
                    pair_loads(pi, emit=False)
                    qp, qp_k, kp, kp_k, kpp, kpp_k, gp, gp_k, etp, etp_k = LD[pi]

                    jobs = []
                    if m == 0:
                        pats = ((0, 0),)
                    else:
                        pats = ((1, 0), (2, 1), (3, 2))
                    for (v, vi) in pats:
                        dil = VARIANTS[v][0]
                        if dil == 1:
                            for kb in range(8):
                                nq = 256 if kb < 7 else 128
                                pieces = []
                                for qi in range(nq // 128):
                                    qt = kb + qi
                                    pieces.append((qt // 4, (qt % 4) * 128, 1, 128, qi * 128))
                                jobs.append(dict(par=False, k=(kb * 128, 1, 128), q=(kb * 128, 1, nq), vi=vi, e0=0, pieces=pieces, vtok=(kb * 128, 1)))
                            jobs.append(dict(par=True, k=(896, 1, 128), q=(0, 1, 128), vi=vi, e0=128, pieces=[(0, 0, 1, 128, 0)], vtok=(896, 1)))
                        elif dil == 4:
                            for c in range(4):
                                for n in range(2):
                                    nq = 256 if n == 0 else 128
                                    pieces = [(n, c, 4, 128, 0)]
                                    if n == 0:
                                        pieces.append((1, c, 4, 128, 128))
                                    jobs.append(dict(par=False, k=(512 * n + c, 4, 128), q=(512 * n + c, 4, nq), vi=vi, e0=0, pieces=pieces, vtok=(512 * n + c, 4)))
                                jobs.append(dict(par=True, k=(512 + c, 4, 128), q=(c, 4, 128), vi=vi, e0=128, pieces=[(0, c, 4, 128, 0)], vtok=(512 + c, 4)))
                        else:
                            for c in range(16):
                                pieces = [(0, c, 16, 32, 0), (1, c, 16, 32, 32)]
                                jobs.append(dict(par=False, k=(c, 16, 64), q=(c, 16, 64), vi=vi, e0=0, pieces=pieces, vtok=(c, 16)))
                                jobs.append(dict(par=True, k=(c, 16, 64), q=(c, 16, 64), vi=vi, e0=64, pieces=pieces, vtok=(c, 16)))

                    if stage <= 3:
                        jobs = []

                    def job_bufs(ji):
                        r = ji % 4
                        sbk = ji % 2
                        vbuf, vbuf_k = av(5, BF16, 512 * r, [2, 128])
                        pe_ap, pe_k = av(6, BF16, 1024 * r, [2, 256])
                        pt_ap, pt_k = av(5, BF16, 4096 + 1024 * r, [2, 256])
                        return r, sbk, vbuf, vbuf_k, pe_ap, pe_k, pt_ap, pt_k

                    def front(ji, jb):
                        r, sbk, vbuf, vbuf_k, pe_ap, pe_k, pt_ap, pt_k = job_bufs(ji)
                        k0, kst, nk = jb["k"]
                        q0, qst_, nq = jb["q"]
                        ksrc, ksrc_k = (kpp, kpp_k) if jb["par"] else (kp, kp_k)
                        vt0, vts = jb["vtok"]
                        vv4 = av(5, BF16, 512 * r, [4, 64])[0]
                        if m == 1:
                            vsrc_t = gV if jb["par"] else kvV
                            vin1 = bass.AP(vsrc_t.tensor, vt0 * TOK + jj * 128, [[vts * TOK, nk], [64, 2], [1, 64]])
                            vrd = ["gV"] if jb["par"] else [("kvl", "vb", jj // 4)]
                            vout1 = vv4[0:nk, 0:4:3, :]
                            S.add("sp" if ji % 2 == 0 else "pool", lambda e: e.dma_start(out=vout1, in_=vin1), reads=vrd + [("vz",)], writes=vbuf_k, chan=("vb", r))
                        kap = ksrc[:, k0:k0 + kst * (nk - 1) + 1:kst]
                        qap = qp[:, q0:q0 + qst_ * (nq - 1) + 1:qst_]
                        for h in range(2):
                            bnk = SB[sbk][h]
                            S.add("pe", lambda e, h=h, bnk=bnk: e.matmul(
                                PS[bnk][0:nk, 0:nq], lhsT=kap[h * 64:(h + 1) * 64, :], rhs=qap[h * 64:(h + 1) * 64, :], start=True, stop=True),
                                reads=ksrc_k + qp_k, writes=pk(bnk))
                            S.add("act", lambda e, h=h, bnk=bnk: e.activation(out=pe_ap[0:nk, h, 0:nq], in_=PS[bnk][0:nk, 0:nq], func=AF.Exp),
                                  reads=pk(bnk), writes=pe_k)
                        e0 = jb["e0"]
                        vi = jb["vi"]
                        etp_l = etp
                        eng = "pool" if (ji % 3 == 2) else "dve"
                        if jb["par"]:
                            S.add("dve", lambda e: e.scalar_tensor_tensor(
                                out=pt_ap[0:nk, :, 0:nq], in0=pe_ap[0:nk, :, 0:nq], scalar=pfl[0:nk, 0:1], in1=etp_l[0:nk, vi, :, e0:e0 + nq], op0=ALU.mult, op1=ALU.mult),
                                reads=pe_k + etp_k + ["pfl"], writes=pt_k)
                        else:
                            S.add(eng, lambda e: e.tensor_tensor(
                                out=pt_ap[0:nk, :, 0:nq], in0=pe_ap[0:nk, :, 0:nq], in1=etp_l[0:nk, vi, :, e0:e0 + nq], op=ALU.mult),
                                reads=pe_k + etp_k, writes=pt_k)

                    def back(ji, jb):
                        r, sbk, vbuf, vbuf_k, pe_ap, pe_k, pt_ap, pt_k = job_bufs(ji)
                        nk = jb["k"][2]
                        if m == 0:
                            kbi = 8 if jb["par"] else jb["vtok"][0] // 128
                            vl = [vAb[0:nk, pi % 2, kbi, h_ * 128:(h_ + 1) * 128] for h_ in range(2)]
                            vl_k = [("vA", pi % 2)]
                        else:
                            vl = [vbuf[0:nk, h_, :] for h_ in range(2)]
                            vl_k = vbuf_k
                        for h in range(2):
                            for (span, c0, cs, cn, qo) in jb["pieces"]:
                                ocols = slice(c0, c0 + cs * (cn - 1) + 1, cs)
                                S.add("pe", lambda e, h=h, span=span, ocols=ocols, qo=qo, cn=cn: e.matmul(
                                    PS[NUM[span]][:, ocols], lhsT=vl[h], rhs=pt_ap[0:nk, h, qo:qo + cn], start=False, stop=False, skip_group_check=True),
                                    reads=vl_k + pt_k, writes=pk(NUM[span]))
                                S.add("pe", lambda e, h=h, span=span, ocols=ocols, qo=qo, cn=cn: e.matmul(
                                    PS[DEN[span]][:, ocols], lhsT=onb[0:nk, h, :], rhs=pt_ap[0:nk, h, qo:qo + cn], start=False, stop=False, skip_group_check=True),
                                    reads=["onb"] + pt_k, writes=pk(DEN[span]))

                    def fin():
                        for span in range(2):
                            rden, rden_k = av(0, F32, 6144 * span, [512])
                            tmpf, tmpf_k = av(0, F32, 2048 + 2048 * span, [512])
                            if m == 0:
                                S.add("act", lambda e, span=span, jj=jj, rden=rden: e.activation(out=rden, in_=PS[DEN[span]][:, :], func=AF.Identity, bias=esk[:, jj:jj + 1]),
                                      reads=pk(DEN[span]) + [("esk", 0), ("esk", 1)], writes=rden_k)
                                S.add("dve", lambda e, rden=rden: e.reciprocal(out=rden, in_=rden), reads=rden_k, writes=rden_k)
                            else:
                                S.add("dve", lambda e, span=span, rden=rden: e.reciprocal(out=rden, in_=PS[DEN[span]][:, :]), reads=pk(DEN[span]), writes=rden_k)
                            S.add("dve", lambda e, span=span, rden=rden, tmpf=tmpf: e.tensor_tensor(out=tmpf, in0=PS[NUM[span]][:, :], in1=rden, op=ALU.mult), reads=pk(NUM[span]) + rden_k, writes=tmpf_k)
                            S.add("pool", lambda e, span=span, pi=pi, gp=gp, tmpf=tmpf: e.tensor_tensor(out=big[:, pi, span * 512:(span + 1) * 512], in0=tmpf, in1=gp[:, span * 512:(span + 1) * 512], op=ALU.mult),
                                  reads=tmpf_k + gp_k, writes=[("big", pi, t) for t in range(span * 4, span * 4 + 4)])

                    PR[pi] = (jobs, front, back, fin)

            for pi in range(16):
                make_pair(pi)
            seq = [(pi, ji, jb) for pi in range(16) for ji, jb in enumerate(PR[pi][0])]
            pair_loads(0)
            pair_loads(1)
            zero_acc()
            for step in range(len(seq) + PIPE):
                if step < len(seq):
                    pi, ji, jb = seq[step]
                    PR[pi][1](step, jb)
                if step >= PIPE:
                    pi, ji, jb = seq[step - PIPE]
                    PR[pi][2](step - PIPE, jb)
                    if ji == len(PR[pi][0]) - 1:
                        PR[pi][3]()
                        if pi + 2 < 16:
                            pair_loads(pi + 2)
                        if pi + 1 < 16:
                            zero_acc()
            out_proj(ev_w_out[j], 0)

        def odd_layer(j):
            phase0(od_ln_g[j])
            w_in = od_w_in[j]
            dma("sp", vgc[:, :], od_v_g[j].rearrange("(c p) -> p c", p=128), [], ["vgc"], "vgc", nonc=True)

            def nb():
                b = cnt["tb"] % 4
                cnt["tb"] += 1
                return b
            vst, vst_k = av(1, BF16, 0, [8, 512])
            junk, junk_k = av(0, BF16, 0, [512])
            ssv = stat[:, 128:192]
            S.add("dve", lambda e: e.memset(ssv, 0.0), writes=[("st", 8)])
            for s in range(8):
                wb = load_w(w_in, 4096 + 512 * s, 512)
                for t in range(NT):
                    bank = nb()
                    mm_tok(bank, wb, 512, t)
                    S.add("act", lambda e, bank=bank, t=t: e.activation(out=vst[:, t, :], in_=PS[bank][:, :], func=AF.Gelu), reads=pk(bank), writes=vst_k)
                    S.add("act", lambda e, t=t, s=s: e.activation(out=junk, in_=vst[:, t, :], func=AF.Square, accum_out=ssv[:, t * 8 + s:t * 8 + s + 1]),
                          reads=vst_k, writes=junk_k + [("st", 8)])
                dma("sp", vsp[:, 512 * s:512 * s + 512].rearrange("(t p) c -> p t c", p=128), vst, vst_k, [("vsp", s)], "vst")
            sv, rv, tv = stat[:, 192:200], stat[:, 200:208], stat[:, 208:216]
            S.add("dve", lambda e: e.reduce_sum(out=sv, in_=ssv.rearrange("p (t s) -> p t s", s=8), axis=AX.X), reads=[("st", 8)], writes=[("st", 9)])
            S.add("dve", lambda e: e.tensor_scalar(out=sv, in0=sv, scalar1=1.0 / 4096, scalar2=EPS, op0=ALU.mult, op1=ALU.add), reads=[("st", 9)], writes=[("st", 9)])
            rsqrt_newton(sv, rv, tv, [("st", 9)], [("st", 10)], [("st", 11)])
            ust, ust_k = av(2, BF16, 0, [4, 1024])
            ugst, ugst_k = av(3, BF16, 0, [4, 1024])
            gtm = [av(0, F32, 2048 + 2048 * i, [512]) for i in range(2)]
            for i in range(8):
                wb = load_w(w_in, 512 * i, 512)
                for ch in range(4):
                    for span in range(2):
                        bank = nb()
                        mm_feat(bank, wb, ch * 128, span)
                        S.add("act", lambda e, bank=bank, ch=ch, span=span: e.activation(out=ust[:, ch, span * 512:(span + 1) * 512], in_=PS[bank][:, :], func=AF.Gelu),
                              reads=pk(bank), writes=ust_k)
                wb = load_w(w_in, 8192 + 512 * i, 512)
                for ch in range(4):
                    for span in range(2):
                        bank = nb()
                        mm_feat(bank, wb, ch * 128, span)
                        g_ap, g_k = gtm[(ch * 2 + span) % 2]
                        S.add("act", lambda e, bank=bank, g_ap=g_ap: e.activation(out=g_ap, in_=PS[bank][:, :], func=AF.Silu), reads=pk(bank), writes=g_k)
                        S.add("dve", lambda e, ch=ch, span=span, g_ap=g_ap: e.tensor_tensor(out=ugst[:, ch, span * 512:(span + 1) * 512], in0=ust[:, ch, span * 512:(span + 1) * 512],
                                                                                                 in1=g_ap, op=ALU.mult), reads=ust_k + g_k, writes=ugst_k)
                dma("sp", ugd[512 * i:512 * i + 512, :].rearrange("(c p) n -> p c n", p=128), ugst, ugst_k, [("ugd", i)], "ugst")
            wsn, wsn_k = av(4, BF16, 0, [16, 128])
            wsT, wsT_k = av(5, BF16, 0, [16, 128])
            dma("pool", wsn, od_w_s[j].rearrange("g t s -> t g s"), [], wsn_k, "wsn")
            for half in range(2):
                bank = 6 + half
                for i in range(8):
                    g = half * 8 + i
                    S.add("pe", lambda e, bank=bank, i=i, g=g: e.transpose(psb(bank)[:, i * 128:(i + 1) * 128], wsn[:, g, :], ident), reads=wsn_k + ["cmb"], writes=pk(bank))
                S.add("dve", lambda e, bank=bank, half=half: e.tensor_tensor(out=wsT[:, half * 8:half * 8 + 8, :], in0=psb(bank).rearrange("p (a b) -> p a b", a=8),
                                                                           in1=bc(trim, [128, 8, 128], 1), op=ALU.mult), reads=pk(bank) + ["cmb"], writes=wsT_k)
            b_bc, b_bc_k = av(4, F32, 0, [16, 128])
            dma("sp", b_bc, bass.AP(od_b_s.tensor, j * 2048, [[0, 128], [1, 2048]]).rearrange("p (a b) -> p a b", a=16), [], b_bc_k, "bbc")
            def load_vg(g):
                if g < 16:
                    vg_, vg_k_ = av(1, BF16, 4096 * (g % 2), [8, 256])
                    dma("sp", vg_, vsp[:, 256 * g:256 * g + 256].rearrange("(t p) c -> p t c", p=128), [("vsp", g // 2)], vg_k_, ("vg", g % 2))

            def load_ug(c):
                if c < 32:
                    ug_, ug_k_ = av(3, BF16, 2048 * (c % 2), [1024])
                    dma("sp", ug_, ugd[128 * c:128 * c + 128, :], [("ugd", c // 4)], ug_k_, ("ug", c % 2))

            load_vg(0)
            load_ug(0)
            for H in range(2):
                for g in range(8 * H, 8 * H + 8):
                    gb_ = g % 2
                    vg, vg_k = av(1, BF16, 4096 * gb_, [8, 256])
                    wsr, wsr_k = av(2, BF16, 2048 * gb_, [8, 128])
                    for n in range(NT):
                        S.add("act", lambda e, n=n, g=g, wsr=wsr: e.activation(out=wsr[:, n, :], in_=wsT[:, g, :], func=AF.Copy, scale=rv[:, n:n + 1]),
                              reads=wsT_k + [("st", 10)], writes=wsr_k)
                    load_vg(g + 1)
                    for cc in range(2):
                        c = 2 * g + cc
                        ugb = c % 2
                        ug, ug_k = av(3, BF16, 2048 * ugb, [1024])
                        load_ug(c + 1)
                        for span in range(2):
                            bank = nb()
                            for i in range(4):
                                n = span * 4 + i
                                S.add("pe", lambda e, bank=bank, i=i, n=n, cc=cc, vg=vg, wsr=wsr: e.matmul(
                                    PS[bank][:, i * 128:(i + 1) * 128], lhsT=vg[:, n, cc * 128:(cc + 1) * 128], rhs=wsr[:, n, :], start=True, stop=True),
                                    reads=vg_k + wsr_k, writes=pk(bank))
                            t1, t1_k = gtm[span]
                            S.add("dve", lambda e, bank=bank, c=c, g=g, t1=t1: e.scalar_tensor_tensor(
                                out=t1.rearrange("p (a b) -> p a b", a=4), in0=PS[bank][:, :].rearrange("p (a b) -> p a b", a=4), scalar=vgc[:, c:c + 1],
                                in1=bc(b_bc[:, g, :], [128, 4, 128], 1), op0=ALU.mult, op1=ALU.add),
                                reads=pk(bank) + ["vgc"] + b_bc_k, writes=t1_k)
                            S.add("dve", lambda e, c=c, span=span, H=H, t1=t1, ug=ug: e.tensor_tensor(
                                out=big[:, c - 16 * H, span * 512:(span + 1) * 512], in0=t1, in1=ug[:, span * 512:(span + 1) * 512], op=ALU.mult),
                                reads=t1_k + ug_k, writes=[("big", c - 16 * H, t) for t in range(span * 4, span * 4 + 4)])
                out_proj(od_w_out[j], 2048 * H)

        if any(l % 2 == 0 for l in layers):
            S.add("dve", lambda e: e.memset(vAb[:, :, :, :], 0.0), writes=[("vA", 0), ("vA", 1)])
            build_etables()
        for l in layers:
            wplan.extend(wplan_for(l))
        for l in layers:
            if stage <= 0:
                break
            if l % 2 == 0:
                even_layer(l // 2)
            else:
                odd_layer(l // 2)
        for t in range(NT):
            dma("sp", y_out[t * 128:(t + 1) * 128, :], xres[:, t, :], [("x", t)], [("y", t)], ("yout", t))
        S.add("sp", lambda e: e.nop(), reads=[("y", t) for t in range(NT)])
        S.emit(nc, st)
    return nc


_CACHE = {}


def kernel(x, ev_ln_g, ev_w_in, ev_qk_g, ev_sinks, ev_w_out, od_ln_g, od_w_in, od_v_g, od_w_s, od_b_s, od_w_out, rel_bias,
           _layers=(0, 1, 2, 3)):
    if _layers not in _CACHE:
        _CACHE[_layers] = build_program(_layers)
    nc = _CACHE[_layers]
    oh, cm, on = _consts()
    f = lambda a: np.ascontiguousarray(np.asarray(a, dtype=np.float32))
    x = f(x)
    shared = dict(ev_ln_g=f(ev_ln_g), ev_w_in=f(ev_w_in), ev_qk_g=f(ev_qk_g), ev_sinks=f(ev_sinks), ev_w_out=f(ev_w_out),
                  od_ln_g=f(od_ln_g), od_w_in=f(od_w_in), od_v_g=f(od_v_g), od_w_s=f(od_w_s), od_b_s=f(od_b_s), od_w_out=f(od_w_out),
                  rel_bias=f(rel_bias), c_oh=oh, c_cm=cm, c_on=on)
    in_maps = []
    for c in range(8):
        b, h = c // 2, c % 2
        m = dict(shared)
        m["x"] = np.ascontiguousarray(x[b, h * TOK:(h + 1) * TOK, :])
        m["c_pf"] = np.full((128, 1), float(h), np.float32)
        in_maps.append(m)
    res = run_bass_kernel_spmd(nc, in_maps, core_ids=list(range(8)))
    out = np.empty((4, 2048, D), np.float32)
    for c in range(8):
        b, h = c // 2, c % 2
        out[b, h * TOK:(h + 1) * TOK, :] = np.asarray(res.results[c]["y"], dtype=np.float32)
    return out
```

```python
import math
import numpy as np
from contextlib import ExitStack
import concourse.bass as bass
import concourse.mybir as mybir
from concourse.bass_utils import run_bass_kernel_spmd

F32 = mybir.dt.float32
BF16 = mybir.dt.bfloat16
ALU = mybir.AluOpType
AF = mybir.ActivationFunctionType
AX = mybir.AxisListType

RAW, WAW, WAR = 1, 2, 4
EPS = 1e-6
NEG = -30000.0


class _Op:
    __slots__ = ("i", "q", "fn", "chan", "ndma", "inc", "deps", "needed", "sem", "val")


class Sched:
    LIMIT = 30000
    QUEUES = ("pe", "act", "dve", "pool", "sp")

    def __init__(self):
        self.ops = []
        self.lastw = {}
        self.rd_q = {}
        self.rd_dma = {}

    def add(self, q, fn, reads=(), writes=(), chan=None, ndma=1, inc=16):
        i = len(self.ops)
        op = _Op()
        op.i, op.q, op.fn, op.chan, op.ndma, op.inc = i, q, fn, chan, ndma, inc
        op.needed = chan is not None
        op.sem = None
        op.val = 0
        deps = {}
        for k in reads:
            w = self.lastw.get(k)
            if w is not None:
                deps[w] = deps.get(w, 0) | RAW
        for k in writes:
            w = self.lastw.get(k)
            if w is not None:
                deps[w] = deps.get(w, 0) | WAW
            for r in self.rd_q.get(k, {}).values():
                deps[r] = deps.get(r, 0) | WAR
            for r in self.rd_dma.get(k, ()):
                deps[r] = deps.get(r, 0) | WAR
        for k in reads:
            if chan is not None:
                self.rd_dma.setdefault(k, []).append(i)
            else:
                self.rd_q.setdefault(k, {})[q] = i
        for k in writes:
            self.lastw[k] = i
            self.rd_q[k] = {}
            self.rd_dma[k] = []
        op.deps = []
        for d, kind in deps.items():
            dop = self.ops[d]
            if dop.chan is None and chan is None and dop.q == q:
                if q == "pe":
                    continue
                if not (kind & RAW):
                    continue
            op.deps.append(d)
            dop.needed = True
        self.ops.append(op)
        return i

    def emit(self, nc, stack):
        state = {}
        sems = {}
        for op in self.ops:
            if not op.needed:
                continue
            key = ("c", op.chan) if op.chan is not None else ("q", op.q)
            inc = op.inc * op.ndma if op.chan is not None else 1
            idx, val = state.get(key, (0, 0))
            if val + inc > self.LIMIT:
                idx, val = idx + 1, 0
            val += inc
            state[key] = (idx, val)
            op.sem = (key, idx)
            op.val = val
            if op.sem not in sems:
                sems[op.sem] = stack.enter_context(nc.semaphore("s%d" % len(sems)))
        self.nsems = len(sems)
        byq = {q: [] for q in self.QUEUES}
        for op in self.ops:
            byq[op.q].append(op)
        ops = self.ops

        def run(q, eng):
            waited = {}
            for op in byq[q]:
                need = {}
                for d in op.deps:
                    dop = ops[d]
                    if waited.get(dop.sem, 0) >= dop.val:
                        continue
                    if need.get(dop.sem, 0) < dop.val:
                        need[dop.sem] = dop.val
                for s, v in need.items():
                    eng.wait_ge(sems[s], v)
                    waited[s] = v
                r = op.fn(eng)
                if op.needed:
                    if op.chan is not None:
                        insts = r if isinstance(r, (list, tuple)) else [r]
                        assert len(insts) == op.ndma, (len(insts), op.ndma)
                        for ins in insts:
                            ins.then_inc(sems[op.sem], op.inc)
                    else:
                        r.then_inc(sems[op.sem], 1)

        block = stack.enter_context(nc.Block())

        @block.tensor
        def _(e):
            run("pe", e)

        @block.scalar
        def _(e):
            run("act", e)

        @block.vector
        def _(e):
            run("dve", e)

        @block.gpsimd
        def _(e):
            run("pool", e)

        @block.sync
        def _(e):
            run("sp", e)


def _t5_bucket(n):
    n = np.maximum(n, 0)
    max_exact = 16
    large = max_exact + (np.log(np.maximum(n, 1) / max_exact) / np.log(2048 / max_exact) * (32 - max_exact)).astype(np.int32)
    large = np.minimum(large, 31)
    return np.where(n < max_exact, n, large).astype(np.int32)


VARIANTS = ((1, 127), (1, 128), (4, 128), (16, 128))


def _consts():
    oh = np.zeros((33, 4, 384), np.float32)
    for v, (dil, maxd) in enumerate(VARIANTS):
        for m in range(384):
            dist = m - 127
            if 0 <= dist <= maxd:
                oh[int(_t5_bucket(np.array(dist * dil))), v, m] = 1.0
            else:
                oh[32, v, m] = 1.0
    cm = np.zeros((128, 4, 128), np.float32)
    cm[:, 0, :] = np.eye(128)
    cm[:, 1, :] = np.eye(128)[::-1]
    cm[:, 2, :] = np.triu(np.ones((128, 128)))
    on = np.zeros((128, 2, 128), np.float32)
    on[:, 0, 0:64] = 1.0
    on[:, 1, 64:128] = 1.0
    return oh, cm, on


PIPE = 3
TOK = 1024
NT = 8
D = 2048
KC = 16


def build_program(layers=(0, 1, 2, 3), stage=99, ncores=8):
    nc = bass.Bass("TRN2", target_bir_lowering=False)
    S = Sched()
    st = ExitStack()

    def din(name, shape, dt=F32):
        return nc.dram_tensor(name, list(shape), dt, kind="ExternalInput").ap()

    def dscr(name, shape, dt=BF16):
        return nc.dram_tensor(name, list(shape), dt, kind="Internal").ap()

    x_in = din("x", [TOK, D])
    ev_ln_g = din("ev_ln_g", [2, D])
    ev_w_in = din("ev_w_in", [2, D, 6400])
    ev_qk_g = din("ev_qk_g", [2, 4, 64])
    ev_sinks = din("ev_sinks", [2, 16])
    ev_w_out = din("ev_w_out", [2, D, D])
    od_ln_g = din("od_ln_g", [2, D])
    has_odd = any(l % 2 == 1 for l in layers)
    od_w_in = din("od_w_in", [2, D, 12288] if has_odd else [2, 128, 128])
    od_v_g = din("od_v_g", [2, 4096])
    od_w_s = din("od_w_s", [2, 16, 128, 128])
    od_b_s = din("od_b_s", [2, 16, 128])
    od_w_out = din("od_w_out", [2, 4096, D] if has_odd else [2, 128, 128])
    rel_bias = din("rel_bias", [32, 32])
    c_oh = din("c_oh", [33, 4, 384])
    c_cm = din("c_cm", [128, 4, 128])
    c_on = din("c_on", [128, 2, 128])
    c_pf = din("c_pf", [128, 1])
    y_out = nc.dram_tensor("y", [TOK, D], F32, kind="ExternalOutput").ap()

    qT_d = dscr("qT_d", [2048, TOK])
    kvK = dscr("kvK", [1024, TOK])
    kvV = dscr("kvV", [1024, TOK])
    kvA = dscr("kvA", [256, TOK])
    gK = dscr("gK", [2048, TOK])
    gV = dscr("gV", [2048, TOK])
    gA = dscr("gA", [512, TOK])
    gT_d = dscr("gT_d", [2048, TOK])
    evd = dscr("evd", [4, 16, 384])
    etd = dscr("etd", [4, 16, 128, 256])
    vsp = dscr("vsp", [TOK, 4096])
    ugd = dscr("ugd", [4096, TOK])

    def sb(name, shape, dt):
        return st.enter_context(nc.sbuf_tensor(name, list(shape), dt))

    with st:
        xres = sb("xres", [128, NT, D], F32)
        big = sb("big", [128, KC, TOK], BF16)
        wsl = [sb("wsl%d" % i, [128, KC, 512], BF16) for i in range(2)]
        AR = [sb("ar%d" % i, [128, 2048], F32) for i in range(7)]
        cmb = sb("cmb", [128, 4, 128], BF16)
        onb = sb("onb", [128, 2, 128], BF16)
        pfl = sb("pfl", [128, 1], F32)
        stat = sb("stat", [128, 256], F32)
        gtab = sb("gtab", [128, 4, 64], F32)
        gqk = sb("gqk", [128, 2, 64], F32)
        esk = sb("esk", [128, 8], F32)
        vgc = sb("vgc", [128, 32], F32)
        vAb = sb("vAb", [128, 2, 9, 256], BF16)
        PS = [st.enter_context(nc.psum_tensor("ps%d" % i, [128, 512], F32)) for i in range(8)]

        ident = cmb[:, 0, :]
        Jm = cmb[:, 1, :]
        trim = cmb[:, 2, :]

        def av(i, dt, off_bytes, shape):
            n = int(np.prod(shape))
            esz = 4 if dt == F32 else 2
            nbytes = n * esz
            assert off_bytes + nbytes <= 8192 and off_bytes % 4 == 0
            base = AR[i][:, off_bytes // 4:(off_bytes + nbytes) // 4]
            ap = base if dt == F32 else base.bitcast(BF16)
            if len(shape) == 2:
                ap = ap.rearrange("p (a b) -> p a b", a=shape[0])
            elif len(shape) == 3:
                ap = ap.rearrange("p (a b c) -> p a b c", a=shape[0], b=shape[1])
            keys = [("ar", i, s) for s in range(off_bytes // 512, (off_bytes + nbytes + 511) // 512)]
            return ap, keys

        def bc(ap, shape, axis):
            return ap.unsqueeze(axis).broadcast_to(list(shape))

        def psb(i):
            return PS[i][:, :].bitcast(BF16)

        def pk(i):
            return [("ps", i)]

        cnt = {"w": 0, "tb": 0}

        def dma(q, out, in_, reads, writes, chan, nonc=False):
            if nonc:
                S.add(q, lambda e: e.dma_start(out=out, in_=in_, allow_slow_non_contiguous=True), reads=reads, writes=writes, chan=chan)
            else:
                S.add(q, lambda e: e.dma_start(out=out, in_=in_), reads=reads, writes=writes, chan=chan)

        wplan = []
        wst = {"issued": 0, "used": 0}

        def wplan_for(l):
            j = l // 2
            if l % 2 == 0:
                wi, wo = ev_w_in[j], ev_w_out[j]
                sp_ = [(wi, 3328, 512, 0), (wi, 3840, 512, 0), (wi, 1024, 256, 0), (wi, 4352, 512, 0), (wi, 4864, 512, 0)]
                sp_ += [(wi, c, 512, 0) for c in (0, 512, 2304, 2816, 1280, 1792, 5376, 5888)] + [(wo, 512 * s_, 512, 0) for s_ in range(4)]
            else:
                wi, wo = od_w_in[j], od_w_out[j]
                sp_ = [(wi, 4096 + 512 * s_, 512, 0) for s_ in range(8)]
                for i in range(8):
                    sp_ += [(wi, 512 * i, 512, 0), (wi, 8192 + 512 * i, 512, 0)]
                for H in range(2):
                    sp_ += [(wo, 512 * s_, 512, 2048 * H) for s_ in range(4)]
            return sp_

        def load_w(wap, c0, W, rows0=0):
            k = wst["used"]
            assert wplan[k][1:] == (c0, W, rows0), (wplan[k][1:], c0, W, rows0)
            while wst["issued"] < min(k + 2, len(wplan)):
                i = wst["issued"]
                wp, pc0, pW, pr0 = wplan[i]
                src = wp[pr0:pr0 + D, pc0:pc0 + pW].rearrange("(k p) c -> p k c", p=128)
                dma("pool", wsl[i % 2][:, :, 0:pW], src, [], [("w", i % 2)], ("w", i % 2))
                wst["issued"] += 1
            wst["used"] += 1
            return k % 2

        def hkeys(k, tiles):
            return [("big", k, t) for t in tiles]

        def rsqrt_newton(v_ap, r_ap, t_ap, keys_v, keys_r, keys_t):
            S.add("act", lambda e: e.activation(out=r_ap, in_=v_ap, func=AF.Sqrt), reads=keys_v, writes=keys_r)
            S.add("dve", lambda e: e.reciprocal(out=r_ap, in_=r_ap), reads=keys_r, writes=keys_r)
            S.add("dve", lambda e: e.tensor_tensor(out=t_ap, in0=r_ap, in1=r_ap, op=ALU.mult), reads=keys_r, writes=keys_t)
            S.add("dve", lambda e: e.tensor_tensor(out=t_ap, in0=t_ap, in1=v_ap, op=ALU.mult), reads=keys_t + keys_v, writes=keys_t)
            S.add("dve", lambda e: e.tensor_scalar(out=t_ap, in0=t_ap, scalar1=-0.5, scalar2=1.5, op0=ALU.mult, op1=ALU.add), reads=keys_t, writes=keys_t)
            S.add("dve", lambda e: e.tensor_tensor(out=r_ap, in0=r_ap, in1=t_ap, op=ALU.mult), reads=keys_r + keys_t, writes=keys_r)

        dma("pool", cmb[:, :, :], c_cm, [], ["cmb"], "c0")
        dma("pool", onb[:, :, :], c_on, [], ["onb"], "c1")
        dma("sp", pfl[:, :], c_pf, [], ["pfl"], "c2")
        for t in range(NT):
            dma("sp", xres[:, t, :], x_in[t * 128:(t + 1) * 128, :], [], [("x", t)], ("xin", t))

        def phase0(lnrow):
            lng, lng_k = av(0, F32, 0, [2048])
            junk, junk_k = av(1, BF16, 0, [2048])
            hb = [av(1, BF16, 4096, [2048]), av(2, BF16, 0, [2048])]
            ssq, rr, tt = stat[:, 0:8], stat[:, 8:16], stat[:, 16:24]
            dma("sp", lng, bass.AP(lnrow.tensor, lnrow.offset, [[0, 128], [1, D]]), [], lng_k, "lng")
            S.add("dve", lambda e: e.memset(ssq, 0.0), writes=[("st", 0)])
            for t in range(NT):
                S.add("act", lambda e, t=t: e.activation(out=junk, in_=xres[:, t, :], func=AF.Square, accum_out=ssq[:, t:t + 1]),
                      reads=[("x", t)], writes=junk_k + [("st", 0)])
            S.add("dve", lambda e: e.tensor_scalar(out=ssq, in0=ssq, scalar1=1.0 / D, scalar2=EPS, op0=ALU.mult, op1=ALU.add),
                  reads=[("st", 0)], writes=[("st", 0)])
            rsqrt_newton(ssq, rr, tt, [("st", 0)], [("st", 1)], [("st", 2)])
            for t in range(NT):
                h_ap, h_k = hb[t % 2]
                S.add("dve", lambda e, t=t, h_ap=h_ap: e.scalar_tensor_tensor(out=h_ap, in0=xres[:, t, :], scalar=rr[:, t:t + 1], in1=lng,
                                                                              op0=ALU.mult, op1=ALU.mult),
                      reads=[("x", t), ("st", 1)] + lng_k, writes=h_k)
                for half in range(2):
                    bank = 6 + half
                    for i in range(8):
                        k = half * 8 + i
                        S.add("pe", lambda e, bank=bank, i=i, k=k, h_ap=h_ap: e.transpose(psb(bank)[:, i * 128:(i + 1) * 128], h_ap[:, k * 128:(k + 1) * 128], ident),
                              reads=h_k + ["cmb"], writes=pk(bank))
                    S.add("act", lambda e, bank=bank, half=half, t=t: e.activation(
                        out=big[:, half * 8:half * 8 + 8, t * 128:(t + 1) * 128],
                        in_=psb(bank).rearrange("p (a b) -> p a b", a=8), func=AF.Copy),
                        reads=pk(bank), writes=[("big", k, t) for k in range(half * 8, half * 8 + 8)])

        def mm_tok(bank, wb, W, t, wcol0=0):
            for k in range(KC):
                S.add("pe", lambda e, k=k: e.matmul(PS[bank][:, 0:W], lhsT=big[:, k, t * 128:(t + 1) * 128], rhs=wsl[wb][:, k, wcol0:wcol0 + W],
                                                    start=(k == 0), stop=(k == KC - 1)),
                      reads=[("big", k, t), ("w", wb)], writes=pk(bank))

        def mm_feat(bank, wb, wcol0, span):
            for k in range(KC):
                S.add("pe", lambda e, k=k: e.matmul(PS[bank][:, :], lhsT=wsl[wb][:, k, wcol0:wcol0 + 128], rhs=big[:, k, span * 512:(span + 1) * 512],
                                                    start=(k == 0), stop=(k == KC - 1)),
                      reads=hkeys(k, range(span * 4, span * 4 + 4)) + [("w", wb)], writes=pk(bank))

        def out_proj(wap, rows0):
            for sl in range(4):
                wb = load_w(wap, sl * 512, 512, rows0)
                for t in range(NT):
                    bank = cnt["tb"] % 4
                    cnt["tb"] += 1
                    mm_tok(bank, wb, 512, t)
                    S.add("dve", lambda e, bank=bank, t=t, sl=sl: e.tensor_tensor(out=xres[:, t, sl * 512:(sl + 1) * 512], in0=PS[bank][:, :],
                                                                                 in1=xres[:, t, sl * 512:(sl + 1) * 512], op=ALU.add),
                          reads=pk(bank) + [("x", t)], writes=[("x", t)])

        def build_etables():
            text, text_k = av(3, F32, 0, [32])
            oht, oht_k = av(4, F32, 0, [4, 384])
            evs, evs_k = av(3, BF16, 512, [4, 384])
            S.add("dve", lambda e: e.memset(text[32:33, :], NEG), writes=[("txr",)])
            dma("sp", text[0:32, :], rel_bias, [], text_k, "tb0")
            dma("sp", oht[0:33, :, :], c_oh, [], oht_k, "tb1")
            for v in range(4):
                hs = 0 if v == 0 else 16
                S.add("pe", lambda e, v=v, hs=hs: e.matmul(PS[0][0:16, 0:384], lhsT=text[0:33, hs:hs + 16], rhs=oht[0:33, v, :], start=True, stop=True),
                      reads=text_k + oht_k + [("txr",)], writes=pk(0))
                S.add("act", lambda e, v=v: e.activation(out=evs[0:16, v, :], in_=PS[0][0:16, 0:384], func=AF.Exp), reads=pk(0), writes=evs_k)
            dma("sp", evd.rearrange("v h m -> h v m"), evs[0:16, :, :], evs_k, ["evd"], "tb2")
            hk, hk_k = av(5, BF16, 0, [16, 256])
            ets, ets_k = av(4, BF16, 0, [16, 256])
            for v in range(4):
                dma("sp", hk, bass.AP(evd.tensor, v * 16 * 384, [[1, 128], [384, 16], [1, 256]]), ["evd"], hk_k, "tb3")
                for hh in range(8):
                    bank = hh % 2
                    S.add("pe", lambda e, hh=hh, bank=bank: e.matmul(PS[bank][:, :], lhsT=Jm, rhs=hk[:, 2 * hh:2 * hh + 2, :], start=True, stop=True),
                          reads=hk_k + ["cmb"], writes=pk(bank))
                    S.add("act", lambda e, hh=hh, bank=bank: e.activation(out=ets[:, 2 * hh:2 * hh + 2, :], in_=PS[bank][:, :].rearrange("p (a b) -> p a b", a=2), func=AF.Copy),
                          reads=pk(bank), writes=ets_k)
                dma("sp", etd[v].rearrange("h k c -> k h c"), ets, ets_k, [("etd", v)], "tb4")

        def even_layer(j):
            phase0(ev_ln_g[j])
            w_in = ev_w_in[j]
            dma("sp", gtab[:, :, :], bass.AP(ev_qk_g.tensor, j * 256, [[0, 128], [1, 256]]).rearrange("p (a b) -> p a b", a=4), [], ["gtab"], "gt")
            for m in range(2):
                S.add("dve", lambda e, m=m: e.scalar_tensor_tensor(out=gqk[:, m, :], in0=gtab[:, 2 * m, :], scalar=0.125, in1=gtab[:, 2 * m + 1, :],
                                                                   op0=ALU.mult, op1=ALU.mult), reads=["gtab"], writes=[("gqk", m)])
            for hf in range(2):
                dma("sp", esk[hf * 64:(hf + 1) * 64, :], bass.AP(ev_sinks.tensor, j * 16 + hf, [[0, 64], [2, 8]]), [], [("esk", hf)], ("esk", hf), nonc=True)
            S.add("act", lambda e: e.activation(out=esk[:, :], in_=esk[:, :], func=AF.Exp), reads=[("esk", 0), ("esk", 1)], writes=[("esk", 0), ("esk", 1)])

            f_sq, f_sq_k = av(0, F32, 0, [512])
            f_q, f_q_k = av(0, F32, 2048, [512])
            qnb = [av(0, BF16, 4096 + 1024 * i, [512]) for i in range(4)]
            qst, qst_k = av(2, BF16, 0, [4, 1024])
            vst, vst_k = av(3, BF16, 0, [8, 512])
            gst = [av(4, BF16, 2048 * i, [1024]) for i in range(4)]
            vast, vast_k = av(5, BF16, 0, [8, 128])
            it = {"n": 0}

            def qk_tile(bank, t, nh, col0, gm, dst, dst_k, dcol):
                W = nh * 64
                ss, rr, tt = stat[:, 32:32 + nh], stat[:, 48:48 + nh], stat[:, 64:64 + nh]
                src = PS[bank][:, col0:col0 + W]
                S.add("act", lambda e: e.activation(out=f_sq[:, 0:W], in_=src, func=AF.Square), reads=pk(bank), writes=f_sq_k)
                S.add("dve", lambda e: e.reduce_sum(out=ss, in_=f_sq[:, 0:W].rearrange("p (h d) -> p h d", d=64), axis=AX.X), reads=f_sq_k, writes=[("st", 4)])
                S.add("dve", lambda e: e.tensor_scalar(out=ss, in0=ss, scalar1=1.0 / 64, scalar2=EPS, op0=ALU.mult, op1=ALU.add), reads=[("st", 4)], writes=[("st", 4)])
                rsqrt_newton(ss, rr, tt, [("st", 4)], [("st", 5)], [("st", 6)])
                qb_ap, qb_k = qnb[it["n"] % 4]
                it["n"] += 1
                src3 = src.rearrange("p (h d) -> p h d", d=64)
                if gm is not None:
                    S.add("dve", lambda e: e.tensor_tensor(out=f_q[:, 0:W].rearrange("p (h d) -> p h d", d=64), in0=src3, in1=bc(rr, [128, nh, 64], 2), op=ALU.mult),
                          reads=pk(bank) + [("st", 5)], writes=f_q_k)
                    S.add("dve", lambda e: e.tensor_tensor(out=qb_ap[:, 0:W].rearrange("p (h d) -> p h d", d=64), in0=f_q[:, 0:W].rearrange("p (h d) -> p h d", d=64),
                                                            in1=bc(gqk[:, gm, :], [128, nh, 64], 1), op=ALU.mult),
                          reads=f_q_k + [("gqk", gm)], writes=qb_k)
                else:
                    S.add("dve", lambda e: e.tensor_tensor(out=qb_ap[:, 0:W].rearrange("p (h d) -> p h d", d=64), in0=src3, in1=bc(rr, [128, nh, 64], 2), op=ALU.mult),
                          reads=pk(bank) + [("st", 5)], writes=qb_k)
                npair = nh // 2
                tb = 6 + (it["n"] % 2)

                def post():
                    for i in range(npair):
                        S.add("pe", lambda e, i=i: e.transpose(psb(tb)[:, i * 128:(i + 1) * 128], qb_ap[:, i * 128:(i + 1) * 128], ident), reads=qb_k + ["cmb"], writes=pk(tb))
                    S.add("act", lambda e: e.activation(out=dst[:, dcol:dcol + npair, t * 128:(t + 1) * 128],
                                                        in_=psb(tb)[:, 0:npair * 128].rearrange("p (a b) -> p a b", a=npair), func=AF.Copy),
                          reads=pk(tb), writes=dst_k)
                return post

            def nb():
                b = cnt["tb"] % 4
                cnt["tb"] += 1
                return b

            def qk_slabs(specs):
                for (c0, gm, ddst, drow) in specs:
                    wb = load_w(w_in, c0, 512)
                    pend = []
                    for t in range(NT):
                        bank = nb()
                        mm_tok(bank, wb, 512, t)
                        if len(pend) >= 2:
                            pend.pop(0)()
                        pend.append(qk_tile(bank, t, 8, 0, gm, qst, qst_k, 0))
                    for p_ in pend:
                        p_()
                    dma("sp", ddst[drow:drow + 512, :].rearrange("(i p) n -> p i n", p=128), qst, qst_k, [("qk_d", id(ddst), drow)], "qst")

            qk_slabs(((3328, None, kvK, 0), (3840, None, kvK, 512)))
            wb = load_w(w_in, 1024, 256)
            pend = []
            for t in range(NT):
                bank = nb()
                mm_tok(bank, wb, 256, t)
                if len(pend) >= 2:
                    pend.pop(0)()
                pend.append(qk_tile(bank, t, 2, 0, None, qst, qst_k, 0))
                S.add("act", lambda e, bank=bank, t=t: e.activation(out=vast[:, t, :], in_=PS[bank][:, 128:256], func=AF.Copy), reads=pk(bank), writes=vast_k)
            for p_ in pend:
                p_()
            dma("sp", kvA[0:128, :], qst[:, 0, :], qst_k, [("kvl", "ka")], "qst")
            dma("sp", bass.AP(kvA.tensor, 128 * TOK, [[128, 128], [128 * 128, 8], [1, 128]]), vast, vast_k, [("kvl", "va")], "vast")
            for s2 in range(2):
                wb = load_w(w_in, 4352 + 512 * s2, 512)
                for t in range(NT):
                    bank = nb()
                    mm_tok(bank, wb, 512, t)
                    S.add("act", lambda e, bank=bank, t=t: e.activation(out=vst[:, t, :], in_=PS[bank][:, :], func=AF.Copy), reads=pk(bank), writes=vst_k)
                dma("sp", kvV[:, 512 * s2:512 * s2 + 512].rearrange("(t p) c -> p t c", p=128), vst, vst_k, [("kvl", "vb", s2)], "vst")
            rg = [[2 * i, 2 * i + 1] for i in range(ncores // 2)]
            for (src_, dst_, rk, wk, ch) in ((kvK, gK, [("qk_d", id(kvK), 0), ("qk_d", id(kvK), 512)], "gK", "cc0"),
                                            (kvV, gV, [("kvl", "vb", 0), ("kvl", "vb", 1)], "gV", "cc1"),
                                            (kvA, gA, [("kvl", "ka"), ("kvl", "va")], "gA", "cc2")):
                S.add("pool", lambda e, src_=src_, dst_=dst_: e.collective_compute("AllGather", ALU.bypass, replica_groups=rg, ins=[src_], outs=[dst_]),
                      reads=rk, writes=[wk], chan=ch, inc=1)

            qk_slabs(((0, 0, qT_d, 0), (512, 0, qT_d, 512), (2304, 1, qT_d, 1024), (2816, 1, qT_d, 1536)))
            gi = 0
            for (c0, drow) in ((1280, 0), (1792, 512), (5376, 1024), (5888, 1536)):
                wb = load_w(w_in, c0, 512)
                for ch in range(4):
                    g_ap, g_k = gst[gi % 4]
                    gi += 1
                    for span in range(2):
                        bank = nb()
                        mm_feat(bank, wb, ch * 128, span)
                        S.add("act", lambda e, bank=bank, span=span, g_ap=g_ap: e.activation(out=g_ap[:, span * 512:(span + 1) * 512], in_=PS[bank][:, :], func=AF.Silu),
                              reads=pk(bank), writes=g_k)
                    dma("sp", gT_d[drow + ch * 128:drow + (ch + 1) * 128, :], g_ap, g_k, [("gT_d", drow + ch * 128)], ("gst", gi % 4))

            if stage <= 1:
                return
            if stage <= 2:
                return
            vz, vz_k = av(5, BF16, 0, [4, 2, 128])
            S.add("dve", lambda e: e.memset(vz, 0.0), writes=vz_k + [("vz",)])
            NUM = (2, 3)
            DEN = (4, 5)
            SB = ((0, 1), (6, 7))
            LD = {}

            def pair_loads(pi, emit=True):
                    def dma_(*a_, **k_):
                        if emit:
                            dma(*a_, **k_)

                    def sadd(*a_, **k_):
                        if emit:
                            S.add(*a_, **k_)
                    m = pi // 8
                    jj = pi % 8
                    ab = pi % 2
                    arq = 1 + ab
                    qp, qp_k = av(arq, BF16, 0, [1024])
                    kp, kp_k = av(arq, BF16, 2048, [1024])
                    kpp, kpp_k = av(arq, BF16, 4096, [1024])
                    gp, gp_k = av(arq, BF16, 6144, [1024])
                    nv = 1 if m == 0 else 3
                    etp, etp_k = av(3, BF16, 4096 * 0, [3, 2, 256]) if ab == 0 else av(4, BF16, 0, [3, 2, 256])
                    frow = m * 1024 + jj * 128
                    dma_("sp", qp, qT_d[frow:frow + 128, :], [("qk_d", id(qT_d), (frow // 512) * 512)], qp_k, ("qp", ab))
                    if m == 1:
                        dma_("sp", kp, kvK[jj * 128:(jj + 1) * 128, :], [("qk_d", id(kvK), (jj // 4) * 512)], kp_k, ("kp", ab))
                        dma_("sp", kpp, gK[jj * 128:(jj + 1) * 128, :], ["gK"], kpp_k, ("kpp", ab))
                    else:
                        g = jj // 4
                        r0 = 64 * g
                        sadd("sp", lambda e, kp=kp, r0=r0: [e.dma_start(out=kp[0:64, :], in_=kvA[r0:r0 + 64, :]), e.dma_start(out=kp[64:128, :], in_=kvA[r0:r0 + 64, :])],
                              reads=[("kvl", "ka")], writes=kp_k, chan=("kp", ab), ndma=2)
                        sadd("sp", lambda e, kpp=kpp, r0=r0: [e.dma_start(out=kpp[0:64, :], in_=gA[r0:r0 + 64, :]), e.dma_start(out=kpp[64:128, :], in_=gA[r0:r0 + 64, :])],
                              reads=["gA"], writes=kpp_k, chan=("kpp", ab), ndma=2)
                    dma_("sp", gp, gT_d[frow:frow + 128, :], [("gT_d", frow)], gp_k, ("gp", ab))
                    if m == 0:
                        g = jj // 4
                        va4 = vAb[:, ab, :, :].rearrange("p t (s d) -> p t s d", s=4)
                        vsrc_o = [bass.AP(kvA.tensor, 128 * TOK + 64 * g, [[128, 128], [128 * 128, 8], [1, 64]]) for _ in range(2)]
                        vsrc_p = [bass.AP(gA.tensor, 128 * TOK + 896 * 128 + 64 * g, [[128, 128], [1, 64]]) for _ in range(2)]
                        sadd("sp", lambda e, va4=va4, vsrc_o=vsrc_o, vsrc_p=vsrc_p: [
                            e.dma_start(out=va4[:, 0:8, 0, :], in_=vsrc_o[0]), e.dma_start(out=va4[:, 0:8, 3, :], in_=vsrc_o[1]),
                            e.dma_start(out=va4[:, 8, 0, :], in_=vsrc_p[0]), e.dma_start(out=va4[:, 8, 3, :], in_=vsrc_p[1])],
                             reads=[("kvl", "va"), "gA"], writes=[("vA", ab)], chan=("vA", ab), ndma=4)
                    vlist = (0,) if m == 0 else (1, 2, 3)
                    for vi, v in enumerate(vlist):
                        dma_("sp", etp[:, vi, :, :], etd[v, 2 * jj:2 * jj + 2].rearrange("h k c -> k h c"), [("etd", v)], etp_k, ("etp", ab, vi))
                    LD[pi] = (qp, qp_k, kp, kp_k, kpp, kpp_k, gp, gp_k, etp, etp_k)

            PR = {}

            def zero_acc():
                for b_ in NUM + DEN:
                    S.add("act", lambda e, b_=b_: e.memzero(PS[b_][:, :]), writes=pk(b_))

            def make_pair(pi):
                    m = pi // 8
                    jj = pi % 8
                    pair_loads(pi, emit=False)
                    qp, qp_k, kp, kp_k, kpp, kpp_k, gp, gp_k, etp, etp_k = LD[pi]

                    jobs = []
                    if m == 0:
                        pats = ((0, 0),)
                    else:
                        pats = ((1, 0), (2, 1), (3, 2))
                    for (v, vi) in pats:
                        dil = VARIANTS[v][0]
                        if dil == 1:
                            for kb in range(8):
                                nq = 256 if kb < 7 else 128
                                pieces = []
                                for qi in range(nq // 128):
                                    qt = kb + qi
                                    pieces.append((qt // 4, (qt % 4) * 128, 1, 128, qi * 128))
                                jobs.append(dict(par=False, k=(kb * 128, 1, 128), q=(kb * 128, 1, nq), vi=vi, e0=0, pieces=pieces, vtok=(kb * 128, 1)))
                            jobs.append(dict(par=True, k=(896, 1, 128), q=(0, 1, 128), vi=vi, e0=128, pieces=[(0, 0, 1, 128, 0)], vtok=(896, 1)))
                        elif dil == 4:
                            for c in range(4):
                                for n in range(2):
                                    nq = 256 if n == 0 else 128
                                    pieces = [(n, c, 4, 128, 0)]
                                    if n == 0:
                                        pieces.append((1, c, 4, 128, 128))
                                    jobs.append(dict(par=False, k=(512 * n + c, 4, 128), q=(512 * n + c, 4, nq), vi=vi, e0=0, pieces=pieces, vtok=(512 * n + c, 4)))
                                jobs.append(dict(par=True, k=(512 + c, 4, 128), q=(c, 4, 128), vi=vi, e0=128, pieces=[(0, c, 4, 128, 0)], vtok=(512 + c, 4)))
                        else:
                            for c in range(16):
                                pieces = [(0, c, 16, 32, 0), (1, c, 16, 32, 32)]
                                jobs.append(dict(par=False, k=(c, 16, 64), q=(c, 16, 64), vi=vi, e0=0, pieces=pieces, vtok=(c, 16)))
                                jobs.append(dict(par=True, k=(c, 16, 64), q=(c, 16, 64), vi=vi, e0=64, pieces=pieces, vtok=(c, 16)))

                    if stage <= 3:
                        jobs = []

                    def job_bufs(ji):
                        r = ji % 4
                        sbk = ji % 2
                        vbuf, vbuf_k = av(5, BF16, 512 * r, [2, 128])
                        pe_ap, pe_k = av(6, BF16, 1024 * r, [2, 256])
                        pt_ap, pt_k = av(5, BF16, 4096 + 1024 * r, [2, 256])
                        return r, sbk, vbuf, vbuf_k, pe_ap, pe_k, pt_ap, pt_k

                    def front(ji, jb):
                        r, sbk, vbuf, vbuf_k, pe_ap, pe_k, pt_ap, pt_k = job_bufs(ji)
                        k0, kst, nk = jb["k"]
                        q0, qst_, nq = jb["q"]
                        ksrc, ksrc_k = (kpp, kpp_k) if jb["par"] else (kp, kp_k)
                        vt0, vts = jb["vtok"]
                        vv4 = av(5, BF16, 512 * r, [4, 64])[0]
                        if m == 1:
                            vsrc_t = gV if jb["par"] else kvV
                            vin1 = bass.AP(vsrc_t.tensor, vt0 * TOK + jj * 128, [[vts * TOK, nk], [64, 2], [1, 64]])
                            vrd = ["gV"] if jb["par"] else [("kvl", "vb", jj // 4)]
                            vout1 = vv4[0:nk, 0:4:3, :]
                            S.add("sp", lambda e: e.dma_start(out=vout1, in_=vin1), reads=vrd + [("vz",)], writes=vbuf_k, chan=("vb", r))
                        kap = ksrc[:, k0:k0 + kst * (nk - 1) + 1:kst]
                        qap = qp[:, q0:q0 + qst_ * (nq - 1) + 1:qst_]
                        for h in range(2):
                            bnk = SB[sbk][h]
                            S.add("pe", lambda e, h=h, bnk=bnk: e.matmul(
                                PS[bnk][0:nk, 0:nq], lhsT=kap[h * 64:(h + 1) * 64, :], rhs=qap[h * 64:(h + 1) * 64, :], start=True, stop=True),
                                reads=ksrc_k + qp_k, writes=pk(bnk))
                            S.add("act", lambda e, h=h, bnk=bnk: e.activation(out=pe_ap[0:nk, h, 0:nq], in_=PS[bnk][0:nk, 0:nq], func=AF.Exp),
                                  reads=pk(bnk), writes=pe_k)
                        e0 = jb["e0"]
                        vi = jb["vi"]
                        etp_l = etp
                        eng = "pool" if (ji % 3 == 2) else "dve"
                        if jb["par"]:
                            S.add("dve", lambda e: e.scalar_tensor_tensor(
                                out=pt_ap[0:nk, :, 0:nq], in0=pe_ap[0:nk, :, 0:nq], scalar=pfl[0:nk, 0:1], in1=etp_l[0:nk, vi, :, e0:e0 + nq], op0=ALU.mult, op1=ALU.mult),
                                reads=pe_k + etp_k + ["pfl"], writes=pt_k)
                        else:
                            S.add(eng, lambda e: e.tensor_tensor(
                                out=pt_ap[0:nk, :, 0:nq], in0=pe_ap[0:nk, :, 0:nq], in1=etp_l[0:nk, vi, :, e0:e0 + nq], op=ALU.mult),
                                reads=pe_k + etp_k, writes=pt_k)

                    def back(ji, jb):
                        r, sbk, vbuf, vbuf_k, pe_ap, pe_k, pt_ap, pt_k = job_bufs(ji)
                        nk = jb["k"][2]
                        if m == 0:
                            kbi = 8 if jb["par"] else jb["vtok"][0] // 128
                            vl = [vAb[0:nk, pi % 2, kbi, h_ * 128:(h_ + 1) * 128] for h_ in range(2)]
                            vl_k = [("vA", pi % 2)]
                        else:
                            vl = [vbuf[0:nk, h_, :] for h_ in range(2)]
                            vl_k = vbuf_k
                        for h in range(2):
                            for (span, c0, cs, cn, qo) in jb["pieces"]:
                                ocols = slice(c0, c0 + cs * (cn - 1) + 1, cs)
                                S.add("pe", lambda e, h=h, span=span, ocols=ocols, qo=qo, cn=cn: e.matmul(
                                    PS[NUM[span]][:, ocols], lhsT=vl[h], rhs=pt_ap[0:nk, h, qo:qo + cn], start=False, stop=False, skip_group_check=True),
                                    reads=vl_k + pt_k, writes=pk(NUM[span]))
                                S.add("pe", lambda e, h=h, span=span, ocols=ocols, qo=qo, cn=cn: e.matmul(
                                    PS[DEN[span]][:, ocols], lhsT=onb[0:nk, h, :], rhs=pt_ap[0:nk, h, qo:qo + cn], start=False, stop=False, skip_group_check=True),
                                    reads=["onb"] + pt_k, writes=pk(DEN[span]))

                    def fin():
                        for span in range(2):
                            rden, rden_k = av(0, F32, 6144 * span, [512])
                            tmpf, tmpf_k = av(0, F32, 2048 + 2048 * span, [512])
                            if m == 0:
                                S.add("act", lambda e, span=span, jj=jj, rden=rden: e.activation(out=rden, in_=PS[DEN[span]][:, :], func=AF.Identity, bias=esk[:, jj:jj + 1]),
                                      reads=pk(DEN[span]) + [("esk", 0), ("esk", 1)], writes=rden_k)
                                S.add("dve", lambda e, rden=rden: e.reciprocal(out=rden, in_=rden), reads=rden_k, writes=rden_k)
                            else:
                                S.add("dve", lambda e, span=span, rden=rden: e.reciprocal(out=rden, in_=PS[DEN[span]][:, :]), reads=pk(DEN[span]), writes=rden_k)
                            S.add("dve", lambda e, span=span, rden=rden, tmpf=tmpf: e.tensor_tensor(out=tmpf, in0=PS[NUM[span]][:, :], in1=rden, op=ALU.mult), reads=pk(NUM[span]) + rden_k, writes=tmpf_k)
                            S.add("pool", lambda e, span=span, pi=pi, gp=gp, tmpf=tmpf: e.tensor_tensor(out=big[:, pi, span * 512:(span + 1) * 512], in0=tmpf, in1=gp[:, span * 512:(span + 1) * 512], op=ALU.mult),
                                  reads=tmpf_k + gp_k, writes=[("big", pi, t) for t in range(span * 4, span * 4 + 4)])

                    PR[pi] = (jobs, front, back, fin)

            for pi in range(16):
                make_pair(pi)
            seq = [(pi, ji, jb) for pi in range(16) for ji, jb in enumerate(PR[pi][0])]
            pair_loads(0)
            pair_loads(1)
            zero_acc()
            for step in range(len(seq) + PIPE):
                if step < len(seq):
                    pi, ji, jb = seq[step]
                    PR[pi][1](step, jb)
                if step >= PIPE:
                    pi, ji, jb = seq[step - PIPE]
                    PR[pi][2](step - PIPE, jb)
                    if ji == len(PR[pi][0]) - 1:
                        PR[pi][3]()
                        if pi + 2 < 16:
                            pair_loads(pi + 2)
                        if pi + 1 < 16:
                            zero_acc()
            out_proj(ev_w_out[j], 0)

        def odd_layer(j):
            phase0(od_ln_g[j])
            w_in = od_w_in[j]
            dma("sp", vgc[:, :], od_v_g[j].rearrange("(c p) -> p c", p=128), [], ["vgc"], "vgc", nonc=True)

            def nb():
                b = cnt["tb"] % 4
                cnt["tb"] += 1
                return b
            vst, vst_k = av(1, BF16, 0, [8, 512])
            junk, junk_k = av(0, BF16, 0, [512])
            ssv = stat[:, 128:192]
            S.add("dve", lambda e: e.memset(ssv, 0.0), writes=[("st", 8)])
            for s in range(8):
                wb = load_w(w_in, 4096 + 512 * s, 512)
                for t in range(NT):
                    bank = nb()
                    mm_tok(bank, wb, 512, t)
                    S.add("act", lambda e, bank=bank, t=t: e.activation(out=vst[:, t, :], in_=PS[bank][:, :], func=AF.Gelu), reads=pk(bank), writes=vst_k)
                    S.add("act", lambda e, t=t, s=s: e.activation(out=junk, in_=vst[:, t, :], func=AF.Square, accum_out=ssv[:, t * 8 + s:t * 8 + s + 1]),
                          reads=vst_k, writes=junk_k + [("st", 8)])
                dma("sp", vsp[:, 512 * s:512 * s + 512].rearrange("(t p) c -> p t c", p=128), vst, vst_k, [("vsp", s)], "vst")
            sv, rv, tv = stat[:, 192:200], stat[:, 200:208], stat[:, 208:216]
            S.add("dve", lambda e: e.reduce_sum(out=sv, in_=ssv.rearrange("p (t s) -> p t s", s=8), axis=AX.X), reads=[("st", 8)], writes=[("st", 9)])
            S.add("dve", lambda e: e.tensor_scalar(out=sv, in0=sv, scalar1=1.0 / 4096, scalar2=EPS, op0=ALU.mult, op1=ALU.add), reads=[("st", 9)], writes=[("st", 9)])
            rsqrt_newton(sv, rv, tv, [("st", 9)], [("st", 10)], [("st", 11)])
            ust, ust_k = av(2, BF16, 0, [4, 1024])
            ugst, ugst_k = av(3, BF16, 0, [4, 1024])
            gtm = [av(0, F32, 2048 + 2048 * i, [512]) for i in range(2)]
            for i in range(8):
                wb = load_w(w_in, 512 * i, 512)
                for ch in range(4):
                    for span in range(2):
                        bank = nb()
                        mm_feat(bank, wb, ch * 128, span)
                        S.add("act", lambda e, bank=bank, ch=ch, span=span: e.activation(out=ust[:, ch, span * 512:(span + 1) * 512], in_=PS[bank][:, :], func=AF.Gelu),
                              reads=pk(bank), writes=ust_k)
                wb = load_w(w_in, 8192 + 512 * i, 512)
                for ch in range(4):
                    for span in range(2):
                        bank = nb()
                        mm_feat(bank, wb, ch * 128, span)
                        g_ap, g_k = gtm[(ch * 2 + span) % 2]
                        S.add("act", lambda e, bank=bank, g_ap=g_ap: e.activation(out=g_ap, in_=PS[bank][:, :], func=AF.Silu), reads=pk(bank), writes=g_k)
                        S.add("dve", lambda e, ch=ch, span=span, g_ap=g_ap: e.tensor_tensor(out=ugst[:, ch, span * 512:(span + 1) * 512], in0=ust[:, ch, span * 512:(span + 1) * 512],
                                                                                                 in1=g_ap, op=ALU.mult), reads=ust_k + g_k, writes=ugst_k)
                dma("sp", ugd[512 * i:512 * i + 512, :].rearrange("(c p) n -> p c n", p=128), ugst, ugst_k, [("ugd", i)], "ugst")
            wsn, wsn_k = av(4, BF16, 0, [16, 128])
            wsT, wsT_k = av(5, BF16, 0, [16, 128])
            dma("pool", wsn, od_w_s[j].rearrange("g t s -> t g s"), [], wsn_k, "wsn")
            for half in range(2):
                bank = 6 + half
                for i in range(8):
                    g = half * 8 + i
                    S.add("pe", lambda e, bank=bank, i=i, g=g: e.transpose(psb(bank)[:, i * 128:(i + 1) * 128], wsn[:, g, :], ident), reads=wsn_k + ["cmb"], writes=pk(bank))
                S.add("dve", lambda e, bank=bank, half=half: e.tensor_tensor(out=wsT[:, half * 8:half * 8 + 8, :], in0=psb(bank).rearrange("p (a b) -> p a b", a=8),
                                                                           in1=bc(trim, [128, 8, 128], 1), op=ALU.mult), reads=pk(bank) + ["cmb"], writes=wsT_k)
            b_bc, b_bc_k = av(4, F32, 0, [16, 128])
            dma("sp", b_bc, bass.AP(od_b_s.tensor, j * 2048, [[0, 128], [1, 2048]]).rearrange("p (a b) -> p a b", a=16), [], b_bc_k, "bbc")
            def load_vg(g):
                if g < 16:
                    vg_, vg_k_ = av(1, BF16, 4096 * (g % 2), [8, 256])
                    dma("sp", vg_, vsp[:, 256 * g:256 * g + 256].rearrange("(t p) c -> p t c", p=128), [("vsp", g // 2)], vg_k_, ("vg", g % 2))

            def load_ug(c):
                if c < 32:
                    ug_, ug_k_ = av(3, BF16, 2048 * (c % 2), [1024])
                    dma("sp", ug_, ugd[128 * c:128 * c + 128, :], [("ugd", c // 4)], ug_k_, ("ug", c % 2))

            load_vg(0)
            load_ug(0)
            for H in range(2):
                for g in range(8 * H, 8 * H + 8):
                    gb_ = g % 2
                    vg, vg_k = av(1, BF16, 4096 * gb_, [8, 256])
                    wsr, wsr_k = av(2, BF16, 2048 * gb_, [8, 128])
                    for n in range(NT):
                        S.add("act", lambda e, n=n, g=g, wsr=wsr: e.activation(out=wsr[:, n, :], in_=wsT[:, g, :], func=AF.Copy, scale=rv[:, n:n + 1]),
                              reads=wsT_k + [("st", 10)], writes=wsr_k)
                    load_vg(g + 1)
                    for cc in range(2):
                        c = 2 * g + cc
                        ugb = c % 2
                        ug, ug_k = av(3, BF16, 2048 * ugb, [1024])
                        load_ug(c + 1)
                        for span in range(2):
                            bank = nb()
                            for i in range(4):
                                n = span * 4 + i
                                S.add("pe", lambda e, bank=bank, i=i, n=n, cc=cc, vg=vg, wsr=wsr: e.matmul(
                                    PS[bank][:, i * 128:(i + 1) * 128], lhsT=vg[:, n, cc * 128:(cc + 1) * 128], rhs=wsr[:, n, :], start=True, stop=True),
                                    reads=vg_k + wsr_k, writes=pk(bank))
                            t1, t1_k = gtm[span]
                            S.add("dve", lambda e, bank=bank, c=c, g=g, t1=t1: e.scalar_tensor_tensor(
                                out=t1.rearrange("p (a b) -> p a b", a=4), in0=PS[bank][:, :].rearrange("p (a b) -> p a b", a=4), scalar=vgc[:, c:c + 1],
                                in1=bc(b_bc[:, g, :], [128, 4, 128], 1), op0=ALU.mult, op1=ALU.add),
                                reads=pk(bank) + ["vgc"] + b_bc_k, writes=t1_k)
                            S.add("dve", lambda e, c=c, span=span, H=H, t1=t1, ug=ug: e.tensor_tensor(
                                out=big[:, c - 16 * H, span * 512:(span + 1) * 512], in0=t1, in1=ug[:, span * 512:(span + 1) * 512], op=ALU.mult),
                                reads=t1_k + ug_k, writes=[("big", c - 16 * H, t) for t in range(span * 4, span * 4 + 4)])
                out_proj(od_w_out[j], 2048 * H)

        if any(l % 2 == 0 for l in layers):
            S.add("dve", lambda e: e.memset(vAb[:, :, :, :], 0.0), writes=[("vA", 0), ("vA", 1)])
            build_etables()
        for l in layers:
            wplan.extend(wplan_for(l))
        for l in layers:
            if stage <= 0:
                break
            if l % 2 == 0:
                even_layer(l // 2)
            else:
                odd_layer(l // 2)
        for t in range(NT):
            dma("sp", y_out[t * 128:(t + 1) * 128, :], xres[:, t, :], [("x", t)], [("y", t)], ("yout", t))
        S.add("sp", lambda e: e.nop(), reads=[("y", t) for t in range(NT)])
        S.emit(nc, st)
    return nc


_CACHE = {}


def kernel(x, ev_ln_g, ev_w_in, ev_qk_g, ev_sinks, ev_w_out, od_ln_g, od_w_in, od_v_g, od_w_s, od_b_s, od_w_out, rel_bias,
           _layers=(0, 1, 2, 3)):
    if _layers not in _CACHE:
        _CACHE[_layers] = build_program(_layers)
    nc = _CACHE[_layers]
    oh, cm, on = _consts()
    f = lambda a: np.ascontiguousarray(np.asarray(a, dtype=np.float32))
    x = f(x)
    shared = dict(ev_ln_g=f(ev_ln_g), ev_w_in=f(ev_w_in), ev_qk_g=f(ev_qk_g), ev_sinks=f(ev_sinks), ev_w_out=f(ev_w_out),
                  od_ln_g=f(od_ln_g), od_w_in=f(od_w_in), od_v_g=f(od_v_g), od_w_s=f(od_w_s), od_b_s=f(od_b_s), od_w_out=f(od_w_out),
                  rel_bias=f(rel_bias), c_oh=oh, c_cm=cm, c_on=on)
    in_maps = []
    for c in range(8):
        b, h = c // 2, c % 2
        m = dict(shared)
        m["x"] = np.ascontiguousarray(x[b, h * TOK:(h + 1) * TOK, :])
        m["c_pf"] = np.full((128, 1), float(h), np.float32)
        in_maps.append(m)
    res = run_bass_kernel_spmd(nc, in_maps, core_ids=list(range(8)))
    out = np.empty((4, 2048, D), np.float32)
    for c in range(8):
        b, h = c // 2, c % 2
        out[b, h * TOK:(h + 1) * TOK, :] = np.asarray(res.results[c]["y"], dtype=np.float32)
    return out
```

```python
import math
import numpy as np
from contextlib import ExitStack
import concourse.bass as bass
import concourse.mybir as mybir
from concourse.bass_utils import run_bass_kernel_spmd

F32 = mybir.dt.float32
BF16 = mybir.dt.bfloat16
ALU = mybir.AluOpType
AF = mybir.ActivationFunctionType
AX = mybir.AxisListType

RAW, WAW, WAR = 1, 2, 4
EPS = 1e-6
NEG = -30000.0


class _Op:
    __slots__ = ("i", "q", "fn", "chan", "ndma", "inc", "deps", "needed", "sem", "val")


class Sched:
    LIMIT = 30000
    QUEUES = ("pe", "act", "dve", "pool", "sp")

    def __init__(self):
        self.ops = []
        self.lastw = {}
        self.rd_q = {}
        self.rd_dma = {}

    def add(self, q, fn, reads=(), writes=(), chan=None, ndma=1, inc=16):
        i = len(self.ops)
        op = _Op()
        op.i, op.q, op.fn, op.chan, op.ndma, op.inc = i, q, fn, chan, ndma, inc
        op.needed = chan is not None
        op.sem = None
        op.val = 0
        deps = {}
        for k in reads:
            w = self.lastw.get(k)
            if w is not None:
                deps[w] = deps.get(w, 0) | RAW
        for k in writes:
            w = self.lastw.get(k)
            if w is not None:
                deps[w] = deps.get(w, 0) | WAW
            for r in self.rd_q.get(k, {}).values():
                deps[r] = deps.get(r, 0) | WAR
            for r in self.rd_dma.get(k, ()):
                deps[r] = deps.get(r, 0) | WAR
        for k in reads:
            if chan is not None:
                self.rd_dma.setdefault(k, []).append(i)
            else:
                self.rd_q.setdefault(k, {})[q] = i
        for k in writes:
            self.lastw[k] = i
            self.rd_q[k] = {}
            self.rd_dma[k] = []
        op.deps = []
        for d, kind in deps.items():
            dop = self.ops[d]
            if dop.chan is None and chan is None and dop.q == q:
                if q == "pe":
                    continue
                if not (kind & RAW):
                    continue
            op.deps.append(d)
            dop.needed = True
        self.ops.append(op)
        return i

    def emit(self, nc, stack):
        state = {}
        sems = {}
        for op in self.ops:
            if not op.needed:
                continue
            key = ("c", op.chan) if op.chan is not None else ("q", op.q)
            inc = op.inc * op.ndma if op.chan is not None else 1
            idx, val = state.get(key, (0, 0))
            if val + inc > self.LIMIT:
                idx, val = idx + 1, 0
            val += inc
            state[key] = (idx, val)
            op.sem = (key, idx)
            op.val = val
            if op.sem not in sems:
                sems[op.sem] = stack.enter_context(nc.semaphore("s%d" % len(sems)))
        self.nsems = len(sems)
        byq = {q: [] for q in self.QUEUES}
        for op in self.ops:
            byq[op.q].append(op)
        ops = self.ops

        def run(q, eng):
            waited = {}
            for op in byq[q]:
                need = {}
                for d in op.deps:
                    dop = ops[d]
                    if waited.get(dop.sem, 0) >= dop.val:
                        continue
                    if need.get(dop.sem, 0) < dop.val:
                        need[dop.sem] = dop.val
                for s, v in need.items():
                    eng.wait_ge(sems[s], v)
                    waited[s] = v
                r = op.fn(eng)
                if op.needed:
                    if op.chan is not None:
                        insts = r if isinstance(r, (list, tuple)) else [r]
                        assert len(insts) == op.ndma, (len(insts), op.ndma)
                        for ins in insts:
                            ins.then_inc(sems[op.sem], op.inc)
                    else:
                        r.then_inc(sems[op.sem], 1)

        block = stack.enter_context(nc.Block())

        @block.tensor
        def _(e):
            run("pe", e)

        @block.scalar
        def _(e):
            run("act", e)

        @block.vector
        def _(e):
            run("dve", e)

        @block.gpsimd
        def _(e):
            run("pool", e)

        @block.sync
        def _(e):
            run("sp", e)


def _t5_bucket(n):
    n = np.maximum(n, 0)
    max_exact = 16
    large = max_exact + (np.log(np.maximum(n, 1) / max_exact) / np.log(2048 / max_exact) * (32 - max_exact)).astype(np.int32)
    large = np.minimum(large, 31)
    return np.where(n < max_exact, n, large).astype(np.int32)


VARIANTS = ((1, 127), (1, 128), (4, 128), (16, 128))


def _consts():
    oh = np.zeros((33, 4, 384), np.float32)
    for v, (dil, maxd) in enumerate(VARIANTS):
        for m in range(384):
            dist = m - 127
            if 0 <= dist <= maxd:
                oh[int(_t5_bucket(np.array(dist * dil))), v, m] = 1.0
            else:
                oh[32, v, m] = 1.0
    cm = np.zeros((128, 4, 128), np.float32)
    cm[:, 0, :] = np.eye(128)
    cm[:, 1, :] = np.eye(128)[::-1]
    cm[:, 2, :] = np.triu(np.ones((128, 128)))
    on = np.zeros((128, 2, 128), np.float32)
    on[:, 0, 0:64] = 1.0
    on[:, 1, 64:128] = 1.0
    return oh, cm, on


PIPE = 3
TOK = 1024
NT = 8
D = 2048
KC = 16


def build_program(layers=(0, 1, 2, 3), stage=99, ncores=8):
    nc = bass.Bass("TRN2", target_bir_lowering=False)
    S = Sched()
    st = ExitStack()

    def din(name, shape, dt=F32):
        return nc.dram_tensor(name, list(shape), dt, kind="ExternalInput").ap()

    def dscr(name, shape, dt=BF16):
        return nc.dram_tensor(name, list(shape), dt, kind="Internal").ap()

    x_in = din("x", [TOK, D])
    ev_ln_g = din("ev_ln_g", [2, D])
    ev_w_in = din("ev_w_in", [2, D, 6400])
    ev_qk_g = din("ev_qk_g", [2, 4, 64])
    ev_sinks = din("ev_sinks", [2, 16])
    ev_w_out = din("ev_w_out", [2, D, D])
    od_ln_g = din("od_ln_g", [2, D])
    has_odd = any(l % 2 == 1 for l in layers)
    od_w_in = din("od_w_in", [2, D, 12288] if has_odd else [2, 128, 128])
    od_v_g = din("od_v_g", [2, 4096])
    od_w_s = din("od_w_s", [2, 16, 128, 128])
    od_b_s = din("od_b_s", [2, 16, 128])
    od_w_out = din("od_w_out", [2, 4096, D] if has_odd else [2, 128, 128])
    rel_bias = din("rel_bias", [32, 32])
    c_oh = din("c_oh", [33, 4, 384])
    c_cm = din("c_cm", [128, 4, 128])
    c_on = din("c_on", [128, 2, 128])
    c_pf = din("c_pf", [128, 1])
    y_out = nc.dram_tensor("y", [TOK, D], F32, kind="ExternalOutput").ap()

    qT_d = dscr("qT_d", [2048, TOK])
    kvK = dscr("kvK", [1024, TOK])
    kvV = dscr("kvV", [1024, TOK])
    kvA = dscr("kvA", [256, TOK])
    gK = dscr("gK", [2048, TOK])
    gV = dscr("gV", [2048, TOK])
    gA = dscr("gA", [512, TOK])
    gT_d = dscr("gT_d", [2048, TOK])
    evd = dscr("evd", [4, 16, 384])
    etd = dscr("etd", [4, 16, 128, 256])
    vsp = dscr("vsp", [TOK, 4096])
    ugd = dscr("ugd", [4096, TOK])

    def sb(name, shape, dt):
        return st.enter_context(nc.sbuf_tensor(name, list(shape), dt))

    with st:
        xres = sb("xres", [128, NT, D], F32)
        big = sb("big", [128, KC, TOK], BF16)
        wsl = [sb("wsl%d" % i, [128, KC, 512], BF16) for i in range(2)]
        AR = [sb("ar%d" % i, [128, 2048], F32) for i in range(7)]
        cmb = sb("cmb", [128, 4, 128], BF16)
        onb = sb("onb", [128, 2, 128], BF16)
        pfl = sb("pfl", [128, 1], F32)
        stat = sb("stat", [128, 256], F32)
        gtab = sb("gtab", [128, 4, 64], F32)
        gqk = sb("gqk", [128, 2, 64], F32)
        esk = sb("esk", [128, 8], F32)
        vgc = sb("vgc", [128, 32], F32)
        vAb = sb("vAb", [128, 2, 9, 256], BF16)
        PS = [st.enter_context(nc.psum_tensor("ps%d" % i, [128, 512], F32)) for i in range(8)]

        ident = cmb[:, 0, :]
        Jm = cmb[:, 1, :]
        trim = cmb[:, 2, :]

        def av(i, dt, off_bytes, shape):
            n = int(np.prod(shape))
            esz = 4 if dt == F32 else 2
            nbytes = n * esz
            assert off_bytes + nbytes <= 8192 and off_bytes % 4 == 0
            base = AR[i][:, off_bytes // 4:(off_bytes + nbytes) // 4]
            ap = base if dt == F32 else base.bitcast(BF16)
            if len(shape) == 2:
                ap = ap.rearrange("p (a b) -> p a b", a=shape[0])
            elif len(shape) == 3:
                ap = ap.rearrange("p (a b c) -> p a b c", a=shape[0], b=shape[1])
            keys = [("ar", i, s) for s in range(off_bytes // 512, (off_bytes + nbytes + 511) // 512)]
            return ap, keys

        def bc(ap, shape, axis):
            return ap.unsqueeze(axis).broadcast_to(list(shape))

        def psb(i):
            return PS[i][:, :].bitcast(BF16)

        def pk(i):
            return [("ps", i)]

        cnt = {"w": 0, "tb": 0}

        def dma(q, out, in_, reads, writes, chan, nonc=False):
            if nonc:
                S.add(q, lambda e: e.dma_start(out=out, in_=in_, allow_slow_non_contiguous=True), reads=reads, writes=writes, chan=chan)
            else:
                S.add(q, lambda e: e.dma_start(out=out, in_=in_), reads=reads, writes=writes, chan=chan)

        wplan = []
        wst = {"issued": 0, "used": 0}

        def wplan_for(l):
            j = l // 2
            if l % 2 == 0:
                wi, wo = ev_w_in[j], ev_w_out[j]
                sp_ = [(wi, 3328, 512, 0), (wi, 3840, 512, 0), (wi, 1024, 256, 0), (wi, 4352, 512, 0), (wi, 4864, 512, 0)]
                sp_ += [(wi, c, 512, 0) for c in (0, 512, 2304, 2816, 1280, 1792, 5376, 5888)] + [(wo, 512 * s_, 512, 0) for s_ in range(4)]
            else:
                wi, wo = od_w_in[j], od_w_out[j]
                sp_ = [(wi, 4096 + 512 * s_, 512, 0) for s_ in range(8)]
                for i in range(8):
                    sp_ += [(wi, 512 * i, 512, 0), (wi, 8192 + 512 * i, 512, 0)]
                for H in range(2):
                    sp_ += [(wo, 512 * s_, 512, 2048 * H) for s_ in range(4)]
            return sp_

        def load_w(wap, c0, W, rows0=0):
            k = wst["used"]
            assert wplan[k][1:] == (c0, W, rows0), (wplan[k][1:], c0, W, rows0)
            while wst["issued"] < min(k + 2, len(wplan)):
                i = wst["issued"]
                wp, pc0, pW, pr0 = wplan[i]
                src = wp[pr0:pr0 + D, pc0:pc0 + pW].rearrange("(k p) c -> p k c", p=128)
                dma("pool", wsl[i % 2][:, :, 0:pW], src, [], [("w", i % 2)], ("w", i % 2))
                wst["issued"] += 1
            wst["used"] += 1
            return k % 2

        def hkeys(k, tiles):
            return [("big", k, t) for t in tiles]

        def rsqrt_newton(v_ap, r_ap, t_ap, keys_v, keys_r, keys_t):
            S.add("act", lambda e: e.activation(out=r_ap, in_=v_ap, func=AF.Sqrt), reads=keys_v, writes=keys_r)
            S.add("dve", lambda e: e.reciprocal(out=r_ap, in_=r_ap), reads=keys_r, writes=keys_r)
            S.add("dve", lambda e: e.tensor_tensor(out=t_ap, in0=r_ap, in1=r_ap, op=ALU.mult), reads=keys_r, writes=keys_t)
            S.add("dve", lambda e: e.tensor_tensor(out=t_ap, in0=t_ap, in1=v_ap, op=ALU.mult), reads=keys_t + keys_v, writes=keys_t)
            S.add("dve", lambda e: e.tensor_scalar(out=t_ap, in0=t_ap, scalar1=-0.5, scalar2=1.5, op0=ALU.mult, op1=ALU.add), reads=keys_t, writes=keys_t)
            S.add("dve", lambda e: e.tensor_tensor(out=r_ap, in0=r_ap, in1=t_ap, op=ALU.mult), reads=keys_r + keys_t, writes=keys_r)

        dma("pool", cmb[:, :, :], c_cm, [], ["cmb"], "c0")
        dma("pool", onb[:, :, :], c_on, [], ["onb"], "c1")
        dma("sp", pfl[:, :], c_pf, [], ["pfl"], "c2")
        for t in range(NT):
            dma("sp", xres[:, t, :], x_in[t * 128:(t + 1) * 128, :], [], [("x", t)], ("xin", t))

        def phase0(lnrow):
            lng, lng_k = av(0, F32, 0, [2048])
            junk, junk_k = av(1, BF16, 0, [2048])
            hb = [av(1, BF16, 4096, [2048]), av(2, BF16, 0, [2048])]
            ssq, rr, tt = stat[:, 0:8], stat[:, 8:16], stat[:, 16:24]
            dma("sp", lng, bass.AP(lnrow.tensor, lnrow.offset, [[0, 128], [1, D]]), [], lng_k, "lng")
            S.add("dve", lambda e: e.memset(ssq, 0.0), writes=[("st", 0)])
            for t in range(NT):
                S.add("act", lambda e, t=t: e.activation(out=junk, in_=xres[:, t, :], func=AF.Square, accum_out=ssq[:, t:t + 1]),
                      reads=[("x", t)], writes=junk_k + [("st", 0)])
            S.add("dve", lambda e: e.tensor_scalar(out=ssq, in0=ssq, scalar1=1.0 / D, scalar2=EPS, op0=ALU.mult, op1=ALU.add),
                  reads=[("st", 0)], writes=[("st", 0)])
            rsqrt_newton(ssq, rr, tt, [("st", 0)], [("st", 1)], [("st", 2)])
            for t in range(NT):
                h_ap, h_k = hb[t % 2]
                S.add("dve", lambda e, t=t, h_ap=h_ap: e.scalar_tensor_tensor(out=h_ap, in0=xres[:, t, :], scalar=rr[:, t:t + 1], in1=lng,
                                                                              op0=ALU.mult, op1=ALU.mult),
                      reads=[("x", t), ("st", 1)] + lng_k, writes=h_k)
                for half in range(2):
                    bank = 6 + half
                    for i in range(8):
                        k = half * 8 + i
                        S.add("pe", lambda e, bank=bank, i=i, k=k, h_ap=h_ap: e.transpose(psb(bank)[:, i * 128:(i + 1) * 128], h_ap[:, k * 128:(k + 1) * 128], ident),
                              reads=h_k + ["cmb"], writes=pk(bank))
                    S.add("act", lambda e, bank=bank, half=half, t=t: e.activation(
                        out=big[:, half * 8:half * 8 + 8, t * 128:(t + 1) * 128],
                        in_=psb(bank).rearrange("p (a b) -> p a b", a=8), func=AF.Copy),
                        reads=pk(bank), writes=[("big", k, t) for k in range(half * 8, half * 8 + 8)])

        def mm_tok(bank, wb, W, t, wcol0=0):
            for k in range(KC):
                S.add("pe", lambda e, k=k: e.matmul(PS[bank][:, 0:W], lhsT=big[:, k, t * 128:(t + 1) * 128], rhs=wsl[wb][:, k, wcol0:wcol0 + W],
                                                    start=(k == 0), stop=(k == KC - 1)),
                      reads=[("big", k, t), ("w", wb)], writes=pk(bank))

        def mm_feat(bank, wb, wcol0, span):
            for k in range(KC):
                S.add("pe", lambda e, k=k: e.matmul(PS[bank][:, :], lhsT=wsl[wb][:, k, wcol0:wcol0 + 128], rhs=big[:, k, span * 512:(span + 1) * 512],
                                                    start=(k == 0), stop=(k == KC - 1)),
                      reads=hkeys(k, range(span * 4, span * 4 + 4)) + [("w", wb)], writes=pk(bank))

        def out_proj(wap, rows0):
            for sl in range(4):
                wb = load_w(wap, sl * 512, 512, rows0)
                for t in range(NT):
                    bank = cnt["tb"] % 4
                    cnt["tb"] += 1
                    mm_tok(bank, wb, 512, t)
                    S.add("dve", lambda e, bank=bank, t=t, sl=sl: e.tensor_tensor(out=xres[:, t, sl * 512:(sl + 1) * 512], in0=PS[bank][:, :],
                                                                                 in1=xres[:, t, sl * 512:(sl + 1) * 512], op=ALU.add),
                          reads=pk(bank) + [("x", t)], writes=[("x", t)])

        et_hooks = []

        def run_hook():
            if et_hooks:
                et_hooks.pop(0)()

        def build_etables():
            text, text_k = av(3, F32, 0, [32])
            oht, oht_k = av(4, F32, 0, [4, 384])
            evs, evs_k = av(3, BF16, 512, [4, 384])
            S.add("dve", lambda e: e.memset(text[32:33, :], NEG), writes=[("txr",)])
            dma("sp", text[0:32, :], rel_bias, [], text_k, "tb0")
            dma("sp", oht[0:33, :, :], c_oh, [], oht_k, "tb1")
            for v in range(4):
                hs = 0 if v == 0 else 16
                S.add("pe", lambda e, v=v, hs=hs: e.matmul(PS[0][0:16, 0:384], lhsT=text[0:33, hs:hs + 16], rhs=oht[0:33, v, :], start=True, stop=True),
                      reads=text_k + oht_k + [("txr",)], writes=pk(0))
                S.add("act", lambda e, v=v: e.activation(out=evs[0:16, v, :], in_=PS[0][0:16, 0:384], func=AF.Exp), reads=pk(0), writes=evs_k)
            dma("sp", evd.rearrange("v h m -> h v m"), evs[0:16, :, :], evs_k, ["evd"], "tb2")
            hk, hk_k = av(6, BF16, 0, [16, 256])
            ets, ets_k = av(4, BF16, 0, [16, 256])

            def stage_b(v):
                dma("sp", hk, bass.AP(evd.tensor, v * 16 * 384, [[1, 128], [384, 16], [1, 256]]), ["evd"], hk_k, "tb3")
                for hh in range(8):
                    bank = 4 + hh % 2
                    S.add("pe", lambda e, hh=hh, bank=bank: e.matmul(PS[bank][:, :], lhsT=Jm, rhs=hk[:, 2 * hh:2 * hh + 2, :], start=True, stop=True),
                          reads=hk_k + ["cmb"], writes=pk(bank))
                    S.add("act", lambda e, hh=hh, bank=bank: e.activation(out=ets[:, 2 * hh:2 * hh + 2, :], in_=PS[bank][:, :].rearrange("p (a b) -> p a b", a=2), func=AF.Copy),
                          reads=pk(bank), writes=ets_k)
                dma("sp", etd[v].rearrange("h k c -> k h c"), ets, ets_k, [("etd", v)], "tb4")

            for v in range(4):
                et_hooks.append(lambda v=v: stage_b(v))

        def even_layer(j):
            phase0(ev_ln_g[j])
            w_in = ev_w_in[j]
            dma("sp", gtab[:, :, :], bass.AP(ev_qk_g.tensor, j * 256, [[0, 128], [1, 256]]).rearrange("p (a b) -> p a b", a=4), [], ["gtab"], "gt")
            for m in range(2):
                S.add("dve", lambda e, m=m: e.scalar_tensor_tensor(out=gqk[:, m, :], in0=gtab[:, 2 * m, :], scalar=0.125, in1=gtab[:, 2 * m + 1, :],
                                                                   op0=ALU.mult, op1=ALU.mult), reads=["gtab"], writes=[("gqk", m)])
            for hf in range(2):
                dma("sp", esk[hf * 64:(hf + 1) * 64, :], bass.AP(ev_sinks.tensor, j * 16 + hf, [[0, 64], [2, 8]]), [], [("esk", hf)], ("esk", hf), nonc=True)
            S.add("act", lambda e: e.activation(out=esk[:, :], in_=esk[:, :], func=AF.Exp), reads=[("esk", 0), ("esk", 1)], writes=[("esk", 0), ("esk", 1)])

            f_sq, f_sq_k = av(0, F32, 0, [512])
            f_q, f_q_k = av(0, F32, 2048, [512])
            qnb = [av(0, BF16, 4096 + 1024 * i, [512]) for i in range(4)]
            qst, qst_k = av(2, BF16, 0, [4, 1024])
            vst, vst_k = av(3, BF16, 0, [8, 512])
            gst = [av(4, BF16, 2048 * i, [1024]) for i in range(4)]
            vast, vast_k = av(5, BF16, 0, [8, 128])
            it = {"n": 0}

            def qk_tile(bank, t, nh, col0, gm, dst, dst_k, dcol):
                W = nh * 64
                ss, rr, tt = stat[:, 32:32 + nh], stat[:, 48:48 + nh], stat[:, 64:64 + nh]
                src = PS[bank][:, col0:col0 + W]
                S.add("act", lambda e: e.activation(out=f_sq[:, 0:W], in_=src, func=AF.Square), reads=pk(bank), writes=f_sq_k)
                S.add("dve", lambda e: e.reduce_sum(out=ss, in_=f_sq[:, 0:W].rearrange("p (h d) -> p h d", d=64), axis=AX.X), reads=f_sq_k, writes=[("st", 4)])
                S.add("dve", lambda e: e.tensor_scalar(out=ss, in0=ss, scalar1=1.0 / 64, scalar2=EPS, op0=ALU.mult, op1=ALU.add), reads=[("st", 4)], writes=[("st", 4)])
                rsqrt_newton(ss, rr, tt, [("st", 4)], [("st", 5)], [("st", 6)])
                qb_ap, qb_k = qnb[it["n"] % 4]
                it["n"] += 1
                src3 = src.rearrange("p (h d) -> p h d", d=64)
                if gm is not None:
                    S.add("dve", lambda e: e.tensor_tensor(out=f_q[:, 0:W].rearrange("p (h d) -> p h d", d=64), in0=src3, in1=bc(rr, [128, nh, 64], 2), op=ALU.mult),
                          reads=pk(bank) + [("st", 5)], writes=f_q_k)
                    S.add("dve", lambda e: e.tensor_tensor(out=qb_ap[:, 0:W].rearrange("p (h d) -> p h d", d=64), in0=f_q[:, 0:W].rearrange("p (h d) -> p h d", d=64),
                                                            in1=bc(gqk[:, gm, :], [128, nh, 64], 1), op=ALU.mult),
                          reads=f_q_k + [("gqk", gm)], writes=qb_k)
                else:
                    S.add("dve", lambda e: e.tensor_tensor(out=qb_ap[:, 0:W].rearrange("p (h d) -> p h d", d=64), in0=src3, in1=bc(rr, [128, nh, 64], 2), op=ALU.mult),
                          reads=pk(bank) + [("st", 5)], writes=qb_k)
                npair = nh // 2
                tb = 6 + (it["n"] % 2)

                def post():
                    for i in range(npair):
                        S.add("pe", lambda e, i=i: e.transpose(psb(tb)[:, i * 128:(i + 1) * 128], qb_ap[:, i * 128:(i + 1) * 128], ident), reads=qb_k + ["cmb"], writes=pk(tb))
                    S.add("act", lambda e: e.activation(out=dst[:, dcol:dcol + npair, t * 128:(t + 1) * 128],
                                                        in_=psb(tb)[:, 0:npair * 128].rearrange("p (a b) -> p a b", a=npair), func=AF.Copy),
                          reads=pk(tb), writes=dst_k)
                return post

            def nb():
                b = cnt["tb"] % 4
                cnt["tb"] += 1
                return b

            def qk_slabs(specs):
                for (c0, gm, ddst, drow) in specs:
                    wb = load_w(w_in, c0, 512)
                    pend = []
                    for t in range(NT):
                        bank = nb()
                        mm_tok(bank, wb, 512, t)
                        if len(pend) >= 2:
                            pend.pop(0)()
                        pend.append(qk_tile(bank, t, 8, 0, gm, qst, qst_k, 0))
                    for p_ in pend:
                        p_()
                    dma("sp", ddst[drow:drow + 512, :].rearrange("(i p) n -> p i n", p=128), qst, qst_k, [("qk_d", id(ddst), drow)], "qst")
                    run_hook()

            qk_slabs(((3328, None, kvK, 0), (3840, None, kvK, 512)))
            wb = load_w(w_in, 1024, 256)
            pend = []
            for t in range(NT):
                bank = nb()
                mm_tok(bank, wb, 256, t)
                if len(pend) >= 2:
                    pend.pop(0)()
                pend.append(qk_tile(bank, t, 2, 0, None, qst, qst_k, 0))
                S.add("act", lambda e, bank=bank, t=t: e.activation(out=vast[:, t, :], in_=PS[bank][:, 128:256], func=AF.Copy), reads=pk(bank), writes=vast_k)
            for p_ in pend:
                p_()
            dma("sp", kvA[0:128, :], qst[:, 0, :], qst_k, [("kvl", "ka")], "qst")
            dma("sp", bass.AP(kvA.tensor, 128 * TOK, [[128, 128], [128 * 128, 8], [1, 128]]), vast, vast_k, [("kvl", "va")], "vast")
            run_hook()
            for s2 in range(2):
                wb = load_w(w_in, 4352 + 512 * s2, 512)
                for t in range(NT):
                    bank = nb()
                    mm_tok(bank, wb, 512, t)
                    S.add("act", lambda e, bank=bank, t=t: e.activation(out=vst[:, t, :], in_=PS[bank][:, :], func=AF.Copy), reads=pk(bank), writes=vst_k)
                dma("sp", kvV[:, 512 * s2:512 * s2 + 512].rearrange("(t p) c -> p t c", p=128), vst, vst_k, [("kvl", "vb", s2)], "vst")
                run_hook()
            rg = [[2 * i, 2 * i + 1] for i in range(ncores // 2)]
            for (src_, dst_, rk, wk, ch) in ((kvK, gK, [("qk_d", id(kvK), 0), ("qk_d", id(kvK), 512)], "gK", "cc0"),
                                            (kvV, gV, [("kvl", "vb", 0), ("kvl", "vb", 1)], "gV", "cc1"),
                                            (kvA, gA, [("kvl", "ka"), ("kvl", "va")], "gA", "cc2")):
                S.add("pool", lambda e, src_=src_, dst_=dst_: e.collective_compute("AllGather", ALU.bypass, replica_groups=rg, ins=[src_], outs=[dst_]),
                      reads=rk, writes=[wk], chan=ch, inc=1)

            qk_slabs(((0, 0, qT_d, 0), (512, 0, qT_d, 512), (2304, 1, qT_d, 1024), (2816, 1, qT_d, 1536)))
            while et_hooks:
                run_hook()
            gi = 0
            for (c0, drow) in ((1280, 0), (1792, 512), (5376, 1024), (5888, 1536)):
                wb = load_w(w_in, c0, 512)
                for ch in range(4):
                    g_ap, g_k = gst[gi % 4]
                    gi += 1
                    for span in range(2):
                        bank = nb()
                        mm_feat(bank, wb, ch * 128, span)
                        S.add("act", lambda e, bank=bank, span=span, g_ap=g_ap: e.activation(out=g_ap[:, span * 512:(span + 1) * 512], in_=PS[bank][:, :], func=AF.Silu),
                              reads=pk(bank), writes=g_k)
                    dma("sp", gT_d[drow + ch * 128:drow + (ch + 1) * 128, :], g_ap, g_k, [("gT_d", drow + ch * 128)], ("gst", gi % 4))

            if stage <= 1:
                return
            if stage <= 2:
                return
            vz, vz_k = av(5, BF16, 0, [4, 2, 128])
            S.add("dve", lambda e: e.memset(vz, 0.0), writes=vz_k + [("vz",)])
            NUM = (2, 3)
            DEN = (4, 5)
            SB = ((0, 1), (6, 7))
            LD = {}

            def pair_loads(pi, emit=True):
                    def dma_(*a_, **k_):
                        if emit:
                            dma(*a_, **k_)

                    def sadd(*a_, **k_):
                        if emit:
                            S.add(*a_, **k_)
                    m = pi // 8
                    jj = pi % 8
                    ab = pi % 2
                    arq = 1 + ab
                    qp, qp_k = av(arq, BF16, 0, [1024])
                    kp, kp_k = av(arq, BF16, 2048, [1024])
                    kpp, kpp_k = av(arq, BF16, 4096, [1024])
                    gp, gp_k = av(arq, BF16, 6144, [1024])
                    nv = 1 if m == 0 else 3
                    etp, etp_k = av(3, BF16, 4096 * 0, [3, 2, 256]) if ab == 0 else av(4, BF16, 0, [3, 2, 256])
                    frow = m * 1024 + jj * 128
                    dma_("sp", qp, qT_d[frow:frow + 128, :], [("qk_d", id(qT_d), (frow // 512) * 512)], qp_k, ("qp", ab))
                    if m == 1:
                        dma_("sp", kp, kvK[jj * 128:(jj + 1) * 128, :], [("qk_d", id(kvK), (jj // 4) * 512)], kp_k, ("kp", ab))
                        dma_("sp", kpp, gK[jj * 128:(jj + 1) * 128, :], ["gK"], kpp_k, ("kpp", ab))
                    else:
                        g = jj // 4
                        r0 = 64 * g
                        sadd("sp", lambda e, kp=kp, r0=r0: [e.dma_start(out=kp[0:64, :], in_=kvA[r0:r0 + 64, :]), e.dma_start(out=kp[64:128, :], in_=kvA[r0:r0 + 64, :])],
                              reads=[("kvl", "ka")], writes=kp_k, chan=("kp", ab), ndma=2)
                        sadd("sp", lambda e, kpp=kpp, r0=r0: [e.dma_start(out=kpp[0:64, :], in_=gA[r0:r0 + 64, :]), e.dma_start(out=kpp[64:128, :], in_=gA[r0:r0 + 64, :])],
                              reads=["gA"], writes=kpp_k, chan=("kpp", ab), ndma=2)
                    dma_("sp", gp, gT_d[frow:frow + 128, :], [("gT_d", frow)], gp_k, ("gp", ab))
                    if m == 0:
                        g = jj // 4
                        va4 = vAb[:, ab, :, :].rearrange("p t (s d) -> p t s d", s=4)
                        vsrc_o = [bass.AP(kvA.tensor, 128 * TOK + 64 * g, [[128, 128], [128 * 128, 8], [1, 64]]) for _ in range(2)]
                        vsrc_p = [bass.AP(gA.tensor, 128 * TOK + 896 * 128 + 64 * g, [[128, 128], [1, 64]]) for _ in range(2)]
                        sadd("sp", lambda e, va4=va4, vsrc_o=vsrc_o, vsrc_p=vsrc_p: [
                            e.dma_start(out=va4[:, 0:8, 0, :], in_=vsrc_o[0]), e.dma_start(out=va4[:, 0:8, 3, :], in_=vsrc_o[1]),
                            e.dma_start(out=va4[:, 8, 0, :], in_=vsrc_p[0]), e.dma_start(out=va4[:, 8, 3, :], in_=vsrc_p[1])],
                             reads=[("kvl", "va"), "gA"], writes=[("vA", ab)], chan=("vA", ab), ndma=4)
                    vlist = (0,) if m == 0 else (1, 2, 3)
                    for vi, v in enumerate(vlist):
                        dma_("sp", etp[:, vi, :, :], etd[v, 2 * jj:2 * jj + 2].rearrange("h k c -> k h c"), [("etd", v)], etp_k, ("etp", ab, vi))
                    LD[pi] = (qp, qp_k, kp, kp_k, kpp, kpp_k, gp, gp_k, etp, etp_k)

            PR = {}

            def zero_acc():
                for b_ in NUM + DEN:
                    S.add("act", lambda e, b_=b_: e.memzero(PS[b_][:, :]), writes=pk(b_))

            def make_pair(pi):
                    m = pi // 8
                    jj = pi % 8
                    pair_loads(pi, emit=False)
                    qp, qp_k, kp, kp_k, kpp, kpp_k, gp, gp_k, etp, etp_k = LD[pi]

                    jobs = []
                    if m == 0:
                        pats = ((0, 0),)
                    else:
                        pats = ((1, 0), (2, 1), (3, 2))
                    for (v, vi) in pats:
                        dil = VARIANTS[v][0]
                        if dil == 1:
                            for kb in range(8):
                                nq = 256 if kb < 7 else 128
                                pieces = []
                                for qi in range(nq // 128):
                                    qt = kb + qi
                                    pieces.append((qt // 4, (qt % 4) * 128, 1, 128, qi * 128))
                                jobs.append(dict(par=False, k=(kb * 128, 1, 128), q=(kb * 128, 1, nq), vi=vi, e0=0, pieces=pieces, vtok=(kb * 128, 1)))
                            jobs.append(dict(par=True, k=(896, 1, 128), q=(0, 1, 128), vi=vi, e0=128, pieces=[(0, 0, 1, 128, 0)], vtok=(896, 1)))
                        elif dil == 4:
                            for c in range(4):
                                for n in range(2):
                                    nq = 256 if n == 0 else 128
                                    pieces = [(n, c, 4, 128, 0)]
                                    if n == 0:
                                        pieces.append((1, c, 4, 128, 128))
                                    jobs.append(dict(par=False, k=(512 * n + c, 4, 128), q=(512 * n + c, 4, nq), vi=vi, e0=0, pieces=pieces, vtok=(512 * n + c, 4)))
                                jobs.append(dict(par=True, k=(512 + c, 4, 128), q=(c, 4, 128), vi=vi, e0=128, pieces=[(0, c, 4, 128, 0)], vtok=(512 + c, 4)))
                        else:
                            for c in range(16):
                                pieces = [(0, c, 16, 32, 0), (1, c, 16, 32, 32)]
                                jobs.append(dict(par=False, k=(c, 16, 64), q=(c, 16, 64), vi=vi, e0=0, pieces=pieces, vtok=(c, 16)))
                                jobs.append(dict(par=True, k=(c, 16, 64), q=(c, 16, 64), vi=vi, e0=64, pieces=pieces, vtok=(c, 16)))

                    if stage <= 3:
                        jobs = []

                    def job_bufs(ji):
                        r = ji % 4
                        sbk = ji % 2
                        vbuf, vbuf_k = av(5, BF16, 512 * r, [2, 128])
                        pe_ap, pe_k = av(6, BF16, 1024 * r, [2, 256])
                        pt_ap, pt_k = av(5, BF16, 4096 + 1024 * r, [2, 256])
                        return r, sbk, vbuf, vbuf_k, pe_ap, pe_k, pt_ap, pt_k

                    def front(ji, jb):
                        r, sbk, vbuf, vbuf_k, pe_ap, pe_k, pt_ap, pt_k = job_bufs(ji)
                        k0, kst, nk = jb["k"]
                        q0, qst_, nq = jb["q"]
                        ksrc, ksrc_k = (kpp, kpp_k) if jb["par"] else (kp, kp_k)
                        vt0, vts = jb["vtok"]
                        vv4 = av(5, BF16, 512 * r, [4, 64])[0]
                        if m == 1:
                            vsrc_t = gV if jb["par"] else kvV
                            vin1 = bass.AP(vsrc_t.tensor, vt0 * TOK + jj * 128, [[vts * TOK, nk], [64, 2], [1, 64]])
                            vrd = ["gV"] if jb["par"] else [("kvl", "vb", jj // 4)]
                            vout1 = vv4[0:nk, 0:4:3, :]
                            S.add("sp", lambda e: e.dma_start(out=vout1, in_=vin1), reads=vrd + [("vz",)], writes=vbuf_k, chan=("vb", r))
                        kap = ksrc[:, k0:k0 + kst * (nk - 1) + 1:kst]
                        qap = qp[:, q0:q0 + qst_ * (nq - 1) + 1:qst_]
                        for h in range(2):
                            bnk = SB[sbk][h]
                            S.add("pe", lambda e, h=h, bnk=bnk: e.matmul(
                                PS[bnk][0:nk, 0:nq], lhsT=kap[h * 64:(h + 1) * 64, :], rhs=qap[h * 64:(h + 1) * 64, :], start=True, stop=True),
                                reads=ksrc_k + qp_k, writes=pk(bnk))
                            S.add("act", lambda e, h=h, bnk=bnk: e.activation(out=pe_ap[0:nk, h, 0:nq], in_=PS[bnk][0:nk, 0:nq], func=AF.Exp),
                                  reads=pk(bnk), writes=pe_k)
                        e0 = jb["e0"]
                        vi = jb["vi"]
                        etp_l = etp
                        eng = "pool" if (ji % 3 == 2) else "dve"
                        if jb["par"]:
                            S.add("dve", lambda e: e.scalar_tensor_tensor(
                                out=pt_ap[0:nk, :, 0:nq], in0=pe_ap[0:nk, :, 0:nq], scalar=pfl[0:nk, 0:1], in1=etp_l[0:nk, vi, :, e0:e0 + nq], op0=ALU.mult, op1=ALU.mult),
                                reads=pe_k + etp_k + ["pfl"], writes=pt_k)
                        else:
                            S.add(eng, lambda e: e.tensor_tensor(
                                out=pt_ap[0:nk, :, 0:nq], in0=pe_ap[0:nk, :, 0:nq], in1=etp_l[0:nk, vi, :, e0:e0 + nq], op=ALU.mult),
                                reads=pe_k + etp_k, writes=pt_k)

                    def back(ji, jb):
                        r, sbk, vbuf, vbuf_k, pe_ap, pe_k, pt_ap, pt_k = job_bufs(ji)
                        nk = jb["k"][2]
                        if m == 0:
                            kbi = 8 if jb["par"] else jb["vtok"][0] // 128
                            vl = [vAb[0:nk, pi % 2, kbi, h_ * 128:(h_ + 1) * 128] for h_ in range(2)]
                            vl_k = [("vA", pi % 2)]
                        else:
                            vl = [vbuf[0:nk, h_, :] for h_ in range(2)]
                            vl_k = vbuf_k
                        for h in range(2):
                            for (span, c0, cs, cn, qo) in jb["pieces"]:
                                ocols = slice(c0, c0 + cs * (cn - 1) + 1, cs)
                                S.add("pe", lambda e, h=h, span=span, ocols=ocols, qo=qo, cn=cn: e.matmul(
                                    PS[NUM[span]][:, ocols], lhsT=vl[h], rhs=pt_ap[0:nk, h, qo:qo + cn], start=False, stop=False, skip_group_check=True),
                                    reads=vl_k + pt_k, writes=pk(NUM[span]))
                                S.add("pe", lambda e, h=h, span=span, ocols=ocols, qo=qo, cn=cn: e.matmul(
                                    PS[DEN[span]][:, ocols], lhsT=onb[0:nk, h, :], rhs=pt_ap[0:nk, h, qo:qo + cn], start=False, stop=False, skip_group_check=True),
                                    reads=["onb"] + pt_k, writes=pk(DEN[span]))

                    def fin():
                        for span in range(2):
                            rden, rden_k = av(0, F32, 6144 * span, [512])
                            tmpf, tmpf_k = av(0, F32, 2048 + 2048 * span, [512])
                            if m == 0:
                                S.add("act", lambda e, span=span, jj=jj, rden=rden: e.activation(out=rden, in_=PS[DEN[span]][:, :], func=AF.Identity, bias=esk[:, jj:jj + 1]),
                                      reads=pk(DEN[span]) + [("esk", 0), ("esk", 1)], writes=rden_k)
                                S.add("dve", lambda e, rden=rden: e.reciprocal(out=rden, in_=rden), reads=rden_k, writes=rden_k)
                            else:
                                S.add("dve", lambda e, span=span, rden=rden: e.reciprocal(out=rden, in_=PS[DEN[span]][:, :]), reads=pk(DEN[span]), writes=rden_k)
                            S.add("dve", lambda e, span=span, rden=rden, tmpf=tmpf: e.tensor_tensor(out=tmpf, in0=PS[NUM[span]][:, :], in1=rden, op=ALU.mult), reads=pk(NUM[span]) + rden_k, writes=tmpf_k)
                            S.add("pool", lambda e, span=span, pi=pi, gp=gp, tmpf=tmpf: e.tensor_tensor(out=big[:, pi, span * 512:(span + 1) * 512], in0=tmpf, in1=gp[:, span * 512:(span + 1) * 512], op=ALU.mult),
                                  reads=tmpf_k + gp_k, writes=[("big", pi, t) for t in range(span * 4, span * 4 + 4)])

                    PR[pi] = (jobs, front, back, fin)

            for pi in range(16):
                make_pair(pi)
            seq = [(pi, ji, jb) for pi in range(16) for ji, jb in enumerate(PR[pi][0])]
            pair_loads(0)
            pair_loads(1)
            zero_acc()
            for step in range(len(seq) + PIPE):
                if step < len(seq):
                    pi, ji, jb = seq[step]
                    PR[pi][1](step, jb)
                if step >= PIPE:
                    pi, ji, jb = seq[step - PIPE]
                    PR[pi][2](step - PIPE, jb)
                    if ji == len(PR[pi][0]) - 1:
                        PR[pi][3]()
                        if pi + 2 < 16:
                            pair_loads(pi + 2)
                        if pi + 1 < 16:
                            zero_acc()
            out_proj(ev_w_out[j], 0)

        def odd_layer(j):
            phase0(od_ln_g[j])
            w_in = od_w_in[j]
            dma("sp", vgc[:, :], od_v_g[j].rearrange("(c p) -> p c", p=128), [], ["vgc"], "vgc", nonc=True)

            def nb():
                b = cnt["tb"] % 4
                cnt["tb"] += 1
                return b
            vst, vst_k = av(1, BF16, 0, [8, 512])
            junk, junk_k = av(0, BF16, 0, [512])
            ssv = stat[:, 128:192]
            S.add("dve", lambda e: e.memset(ssv, 0.0), writes=[("st", 8)])
            for s in range(8):
                wb = load_w(w_in, 4096 + 512 * s, 512)
                for t in range(NT):
                    bank = nb()
                    mm_tok(bank, wb, 512, t)
                    S.add("act", lambda e, bank=bank, t=t: e.activation(out=vst[:, t, :], in_=PS[bank][:, :], func=AF.Gelu), reads=pk(bank), writes=vst_k)
                    S.add("act", lambda e, t=t, s=s: e.activation(out=junk, in_=vst[:, t, :], func=AF.Square, accum_out=ssv[:, t * 8 + s:t * 8 + s + 1]),
                          reads=vst_k, writes=junk_k + [("st", 8)])
                dma("sp", vsp[:, 512 * s:512 * s + 512].rearrange("(t p) c -> p t c", p=128), vst, vst_k, [("vsp", s)], "vst")
            sv, rv, tv = stat[:, 192:200], stat[:, 200:208], stat[:, 208:216]
            S.add("dve", lambda e: e.reduce_sum(out=sv, in_=ssv.rearrange("p (t s) -> p t s", s=8), axis=AX.X), reads=[("st", 8)], writes=[("st", 9)])
            S.add("dve", lambda e: e.tensor_scalar(out=sv, in0=sv, scalar1=1.0 / 4096, scalar2=EPS, op0=ALU.mult, op1=ALU.add), reads=[("st", 9)], writes=[("st", 9)])
            rsqrt_newton(sv, rv, tv, [("st", 9)], [("st", 10)], [("st", 11)])
            ust, ust_k = av(2, BF16, 0, [4, 1024])
            ugst, ugst_k = av(3, BF16, 0, [4, 1024])
            gtm = [av(0, F32, 2048 + 2048 * i, [512]) for i in range(2)]
            for i in range(8):
                wb = load_w(w_in, 512 * i, 512)
                for ch in range(4):
                    for span in range(2):
                        bank = nb()
                        mm_feat(bank, wb, ch * 128, span)
                        S.add("act", lambda e, bank=bank, ch=ch, span=span: e.activation(out=ust[:, ch, span * 512:(span + 1) * 512], in_=PS[bank][:, :], func=AF.Gelu),
                              reads=pk(bank), writes=ust_k)
                wb = load_w(w_in, 8192 + 512 * i, 512)
                for ch in range(4):
                    for span in range(2):
                        bank = nb()
                        mm_feat(bank, wb, ch * 128, span)
                        g_ap, g_k = gtm[(ch * 2 + span) % 2]
                        S.add("act", lambda e, bank=bank, g_ap=g_ap: e.activation(out=g_ap, in_=PS[bank][:, :], func=AF.Silu), reads=pk(bank), writes=g_k)
                        S.add("dve", lambda e, ch=ch, span=span, g_ap=g_ap: e.tensor_tensor(out=ugst[:, ch, span * 512:(span + 1) * 512], in0=ust[:, ch, span * 512:(span + 1) * 512],
                                                                                                 in1=g_ap, op=ALU.mult), reads=ust_k + g_k, writes=ugst_k)
                dma("sp", ugd[512 * i:512 * i + 512, :].rearrange("(c p) n -> p c n", p=128), ugst, ugst_k, [("ugd", i)], "ugst")
            wsn, wsn_k = av(4, BF16, 0, [16, 128])
            wsT, wsT_k = av(5, BF16, 0, [16, 128])
            dma("pool", wsn, od_w_s[j].rearrange("g t s -> t g s"), [], wsn_k, "wsn")
            for half in range(2):
                bank = 6 + half
                for i in range(8):
                    g = half * 8 + i
                    S.add("pe", lambda e, bank=bank, i=i, g=g: e.transpose(psb(bank)[:, i * 128:(i + 1) * 128], wsn[:, g, :], ident), reads=wsn_k + ["cmb"], writes=pk(bank))
                S.add("dve", lambda e, bank=bank, half=half: e.tensor_tensor(out=wsT[:, half * 8:half * 8 + 8, :], in0=psb(bank).rearrange("p (a b) -> p a b", a=8),
                                                                           in1=bc(trim, [128, 8, 128], 1), op=ALU.mult), reads=pk(bank) + ["cmb"], writes=wsT_k)
            b_bc, b_bc_k = av(4, F32, 0, [16, 128])
            dma("sp", b_bc, bass.AP(od_b_s.tensor, j * 2048, [[0, 128], [1, 2048]]).rearrange("p (a b) -> p a b", a=16), [], b_bc_k, "bbc")
            def load_vg(g):
                if g < 16:
                    vg_, vg_k_ = av(1, BF16, 4096 * (g % 2), [8, 256])
                    dma("sp", vg_, vsp[:, 256 * g:256 * g + 256].rearrange("(t p) c -> p t c", p=128), [("vsp", g // 2)], vg_k_, ("vg", g % 2))

            def load_ug(c):
                if c < 32:
                    ug_, ug_k_ = av(3, BF16, 2048 * (c % 2), [1024])
                    dma("sp", ug_, ugd[128 * c:128 * c + 128, :], [("ugd", c // 4)], ug_k_, ("ug", c % 2))

            load_vg(0)
            load_ug(0)
            for H in range(2):
                for g in range(8 * H, 8 * H + 8):
                    gb_ = g % 2
                    vg, vg_k = av(1, BF16, 4096 * gb_, [8, 256])
                    wsr, wsr_k = av(2, BF16, 2048 * gb_, [8, 128])
                    for n in range(NT):
                        S.add("act", lambda e, n=n, g=g, wsr=wsr: e.activation(out=wsr[:, n, :], in_=wsT[:, g, :], func=AF.Copy, scale=rv[:, n:n + 1]),
                              reads=wsT_k + [("st", 10)], writes=wsr_k)
                    load_vg(g + 1)
                    for cc in range(2):
                        c = 2 * g + cc
                        ugb = c % 2
                        ug, ug_k = av(3, BF16, 2048 * ugb, [1024])
                        load_ug(c + 1)
                        for span in range(2):
                            bank = nb()
                            for i in range(4):
                                n = span * 4 + i
                                S.add("pe", lambda e, bank=bank, i=i, n=n, cc=cc, vg=vg, wsr=wsr: e.matmul(
                                    PS[bank][:, i * 128:(i + 1) * 128], lhsT=vg[:, n, cc * 128:(cc + 1) * 128], rhs=wsr[:, n, :], start=True, stop=True),
                                    reads=vg_k + wsr_k, writes=pk(bank))
                            t1, t1_k = gtm[span]
                            S.add("dve", lambda e, bank=bank, c=c, g=g, t1=t1: e.scalar_tensor_tensor(
                                out=t1.rearrange("p (a b) -> p a b", a=4), in0=PS[bank][:, :].rearrange("p (a b) -> p a b", a=4), scalar=vgc[:, c:c + 1],
                                in1=bc(b_bc[:, g, :], [128, 4, 128], 1), op0=ALU.mult, op1=ALU.add),
                                reads=pk(bank) + ["vgc"] + b_bc_k, writes=t1_k)
                            S.add("dve", lambda e, c=c, span=span, H=H, t1=t1, ug=ug: e.tensor_tensor(
                                out=big[:, c - 16 * H, span * 512:(span + 1) * 512], in0=t1, in1=ug[:, span * 512:(span + 1) * 512], op=ALU.mult),
                                reads=t1_k + ug_k, writes=[("big", c - 16 * H, t) for t in range(span * 4, span * 4 + 4)])
                out_proj(od_w_out[j], 2048 * H)

        if any(l % 2 == 0 for l in layers):
            S.add("dve", lambda e: e.memset(vAb[:, :, :, :], 0.0), writes=[("vA", 0), ("vA", 1)])
            build_etables()
        for l in layers:
            wplan.extend(wplan_for(l))
        for l in layers:
            if stage <= 0:
                break
            if l % 2 == 0:
                even_layer(l // 2)
            else:
                odd_layer(l // 2)
        for t in range(NT):
            dma("sp", y_out[t * 128:(t + 1) * 128, :], xres[:, t, :], [("x", t)], [("y", t)], ("yout", t))
        S.add("sp", lambda e: e.nop(), reads=[("y", t) for t in range(NT)])
        S.emit(nc, st)
    return nc


_CACHE = {}


def kernel(x, ev_ln_g, ev_w_in, ev_qk_g, ev_sinks, ev_w_out, od_ln_g, od_w_in, od_v_g, od_w_s, od_b_s, od_w_out, rel_bias,
           _layers=(0, 1, 2, 3)):
    if _layers not in _CACHE:
        _CACHE[_layers] = build_program(_layers)
    nc = _CACHE[_layers]
    oh, cm, on = _consts()
    f = lambda a: np.ascontiguousarray(np.asarray(a, dtype=np.float32))
    x = f(x)
    shared = dict(ev_ln_g=f(ev_ln_g), ev_w_in=f(ev_w_in), ev_qk_g=f(ev_qk_g), ev_sinks=f(ev_sinks), ev_w_out=f(ev_w_out),
                  od_ln_g=f(od_ln_g), od_w_in=f(od_w_in), od_v_g=f(od_v_g), od_w_s=f(od_w_s), od_b_s=f(od_b_s), od_w_out=f(od_w_out),
                  rel_bias=f(rel_bias), c_oh=oh, c_cm=cm, c_on=on)
    in_maps = []
    for c in range(8):
        b, h = c // 2, c % 2
        m = dict(shared)
        m["x"] = np.ascontiguousarray(x[b, h * TOK:(h + 1) * TOK, :])
        m["c_pf"] = np.full((128, 1), float(h), np.float32)
        in_maps.append(m)
    res = run_bass_kernel_spmd(nc, in_maps, core_ids=list(range(8)))
    out = np.empty((4, 2048, D), np.float32)
    for c in range(8):
        b, h = c // 2, c % 2
        out[b, h * TOK:(h + 1) * TOK, :] = np.asarray(res.results[c]["y"], dtype=np.float32)
    return out
```
